# Optimizing a Trainium2 kernel written in Bass

```python
import jax, jax.numpy as jnp
from jax import lax
import numpy as np

D_MODEL = 1024
BATCH = 8
SEQ = 4096
DEPTH = 2

N_EVEN = (DEPTH + 1) // 2
N_ODD = DEPTH // 2
D_A = D_MODEL // 2
CONV_WIDTH = 31
CONV_PAD = CONV_WIDTH // 2
N_POOL = 4
POOL_WINDOWS = (2, 4, 8, 16)
D_B = D_MODEL // 2
G_POOL = D_B // N_POOL
N_FOURIER = 4
D_C = D_MODEL // 2
H_C = D_C // N_FOURIER
N_SGU = 4
D_D = D_MODEL // 2
H_D = D_D // N_SGU
CHUNK = 128
D_IN_EVEN = 2 * D_A + D_B
D_IN_ODD = D_C + 2 * D_D
D_MIX = D_A + D_B
D_FF = ((8 * D_MODEL // 3 + 255) // 256) * 256
EPS = 1e-6

kernel_name = "hybrid_conv_pool_fourier_sgu_encoder"


def rmsnorm(x, g):
    xf = x.astype(jnp.float32)
    y = xf * lax.rsqrt(jnp.mean(xf * xf, axis=-1, keepdims=True) + EPS)
    return (y * g.astype(jnp.float32)).astype(x.dtype)


def layernorm(x, g, b):
    xf = x.astype(jnp.float32)
    mu = jnp.mean(xf, axis=-1, keepdims=True)
    var = jnp.mean(jnp.square(xf - mu), axis=-1, keepdims=True)
    y = (xf - mu) * lax.rsqrt(var + EPS)
    return (y * g.astype(jnp.float32) + b.astype(jnp.float32)).astype(x.dtype)


def conformer_conv(a, gate, conv_w, conv_b, ln_g, ln_b):
    u = a * jax.nn.sigmoid(gate)
    u = lax.conv_general_dilated(
        u, conv_w[:, None, :].astype(u.dtype), window_strides=(1,),
        padding=[(CONV_PAD, CONV_PAD)], dimension_numbers=('NWC', 'WIO', 'NWC'),
        feature_group_count=D_A) + conv_b
    return jax.nn.silu(layernorm(u, ln_g, ln_b))


def multiscale_pool(p, pool_w, pool_scale):
    bn, s, _ = p.shape
    y = p.reshape(bn, s, N_POOL, G_POOL).astype(jnp.float32)
    cs = jnp.concatenate([jnp.zeros((bn, 1, N_POOL, G_POOL), jnp.float32),
                          jnp.cumsum(y, axis=1)], axis=1)
    t = jnp.arange(s)[:, None]
    half = jnp.array(POOL_WINDOWS, dtype=jnp.int32)[None, :] // 2
    lo = jnp.clip(t - half, 0, s - 1)
    hi = jnp.clip(t + half - 1, 0, s - 1)
    gi = jnp.arange(N_POOL)[None, :]
    win_sum = cs[:, hi + 1, gi] - cs[:, lo, gi]
    count = (hi - lo + 1).astype(jnp.float32)[..., None]
    pooled = (win_sum / count - y).astype(p.dtype)
    mixed = jnp.einsum('bsgc,gcd->bsgd', pooled, pool_w) * pool_scale
    return mixed.reshape(bn, s, D_B)


def even_mixer(h, w_in, conv_w, conv_b, ln_g, ln_b, pool_w, pool_scale, w_out):
    z = h @ w_in
    a, gate, p = z[..., :D_A], z[..., D_A:2 * D_A], z[..., 2 * D_A:]
    ya = conformer_conv(a, gate, conv_w, conv_b, ln_g, ln_b)
    yb = multiscale_pool(p, pool_w, pool_scale)
    return jnp.concatenate([ya, yb], axis=-1) @ w_out


def fourier_mix(c, fourier_w):
    bn, s, _ = c.shape
    cf = c.reshape(bn, s, N_FOURIER, H_C).astype(jnp.float32)
    yc = jnp.fft.fft2(cf, axes=(1, 3), norm='ortho').real.astype(c.dtype)
    return jnp.einsum('bshc,hcd->bshd', yc, fourier_w).reshape(bn, s, D_C)


def spatial_gating(u, v, v_ln_g, v_ln_b, spatial_w, spatial_b):
    bn, s, _ = v.shape
    vn = layernorm(v.reshape(bn, s, N_SGU, H_D), v_ln_g, v_ln_b)
    vn = vn.reshape(bn, s // CHUNK, CHUNK, N_SGU, H_D)
    sv = jnp.einsum('hqk,bnkhc->bnqhc', spatial_w, vn) + spatial_b.T[:, :, None]
    return u * sv.reshape(bn, s, D_D)


def odd_mixer(h, w_in, fourier_w, v_ln_g, v_ln_b, spatial_w, spatial_b, w_out):
    z = h @ w_in
    c = z[..., :D_C]
    uv = jax.nn.gelu(z[..., D_C:], approximate=False)
    u, v = uv[..., :D_D], uv[..., D_D:]
    yc = fourier_mix(c, fourier_w)
    yd = spatial_gating(u, v, v_ln_g, v_ln_b, spatial_w, spatial_b)
    return jnp.concatenate([yc, yd], axis=-1) @ w_out


def swiglu(h, w_gate, w_up, w_down):
    return (jax.nn.silu(h @ w_gate) * (h @ w_up)) @ w_down


def setup_inputs(seed: int = 0) -> dict:
    key = jax.random.key(seed)
    ks = jax.random.split(key, 24)
    f32 = jnp.float32
    nrm = lambda k, shape, scale: jax.random.normal(k, shape, f32) * scale
    return {
        'x': jax.random.normal(ks[0], (BATCH, SEQ, D_MODEL), f32),
        'mix_norm_g': 1.0 + nrm(ks[1], (DEPTH, D_MODEL), 0.02),
        'ffn_norm_g': 1.0 + nrm(ks[2], (DEPTH, D_MODEL), 0.02),
        'ev_w_in': nrm(ks[3], (N_EVEN, D_MODEL, D_IN_EVEN), D_MODEL ** -0.5),
        'ev_conv_w': nrm(ks[4], (N_EVEN, CONV_WIDTH, D_A), CONV_WIDTH ** -0.5),
        'ev_conv_b': nrm(ks[5], (N_EVEN, D_A), 0.02),
        'ev_ln_g': 1.0 + nrm(ks[6], (N_EVEN, D_A), 0.02),
        'ev_ln_b': nrm(ks[7], (N_EVEN, D_A), 0.02),
        'ev_pool_w': nrm(ks[8], (N_EVEN, N_POOL, G_POOL, G_POOL), G_POOL ** -0.5),
        'ev_pool_scale': 1.0 + nrm(ks[9], (N_EVEN, N_POOL, G_POOL), 0.02),
        'ev_w_out': nrm(ks[10], (N_EVEN, D_MIX, D_MODEL), D_MIX ** -0.5),
        'od_w_in': nrm(ks[11], (N_ODD, D_MODEL, D_IN_ODD), D_MODEL ** -0.5),
        'od_fourier_w': nrm(ks[12], (N_ODD, N_FOURIER, H_C, H_C), H_C ** -0.5),
        'od_v_ln_g': 1.0 + nrm(ks[13], (N_ODD, N_SGU, H_D), 0.02),
        'od_v_ln_b': nrm(ks[14], (N_ODD, N_SGU, H_D), 0.02),
        'od_spatial_w': nrm(ks[15], (N_ODD, N_SGU, CHUNK, CHUNK), CHUNK ** -0.5),
        'od_spatial_b': 1.0 + nrm(ks[16], (N_ODD, N_SGU, CHUNK), 0.02),
        'od_w_out': nrm(ks[17], (N_ODD, D_MIX, D_MODEL), D_MIX ** -0.5),
        'ffn_w_gate': nrm(ks[18], (DEPTH, D_MODEL, D_FF), D_MODEL ** -0.5),
        'ffn_w_up': nrm(ks[19], (DEPTH, D_MODEL, D_FF), D_MODEL ** -0.5),
        'ffn_w_down': nrm(ks[20], (DEPTH, D_FF, D_MODEL), D_FF ** -0.5),
        'final_norm_g': 1.0 + nrm(ks[21], (D_MODEL,), 0.02),
    }


def reference(x, mix_norm_g, ffn_norm_g, ev_w_in, ev_conv_w, ev_conv_b, ev_ln_g,
              ev_ln_b, ev_pool_w, ev_pool_scale, ev_w_out, od_w_in, od_fourier_w,
              od_v_ln_g, od_v_ln_b, od_spatial_w, od_spatial_b, od_w_out,
              ffn_w_gate, ffn_w_up, ffn_w_down, final_norm_g):
    for l in range(DEPTH):
        h = rmsnorm(x, mix_norm_g[l])
        if l % 2 == 0:
            i = l // 2
            x = x + even_mixer(h, ev_w_in[i], ev_conv_w[i], ev_conv_b[i], ev_ln_g[i],
                               ev_ln_b[i], ev_pool_w[i], ev_pool_scale[i], ev_w_out[i])
        else:
            i = l // 2
            x = x + odd_mixer(h, od_w_in[i], od_fourier_w[i], od_v_ln_g[i], od_v_ln_b[i],
                              od_spatial_w[i], od_spatial_b[i], od_w_out[i])
        h = rmsnorm(x, ffn_norm_g[l])
        x = x + swiglu(h, ffn_w_gate[l], ffn_w_up[l], ffn_w_down[l])
    return rmsnorm(x, final_norm_g)
```

```python
import numpy as np
from contextlib import ExitStack
import concourse.bass as bass
import concourse.mybir as mybir
from concourse.bass_utils import run_bass_kernel_spmd

F32 = mybir.dt.float32
BF16 = mybir.dt.bfloat16
AF = mybir.ActivationFunctionType
ALU = mybir.AluOpType

S = 4096
D = 1024
DFF = 2816
NT = S // 128
EPS = 1e-6
NJ = DFF // 128
ST = 1024
NST = S // ST


class Eng:
    def __init__(self, es, nc, eng, name):
        self.e = eng
        self.name = name
        self.sem = es.enter_context(nc.semaphore("es_" + name))
        self.cnt = 0
        self.seen = {}

    def wait(self, ev):
        if ev is None:
            return
        sem, val = ev
        k = id(sem)
        if self.seen.get(k, 0) >= val:
            return
        self.e.wait_ge(sem, val)
        self.seen[k] = val

    def sig(self, ins):
        self.cnt += 1
        ins.then_inc(self.sem, 1)
        return (self.sem, self.cnt)


class Dep:
    def __init__(self, sem=None):
        self.w = None
        self.r = {}
        self.sem = sem
        self.semval = 0


class K:
    def __init__(self, es, nc):
        self.nc = nc
        self.es = es
        self.pe = Eng(es, nc, nc.tensor, "pe")
        self.act = Eng(es, nc, nc.scalar, "act")
        self.dve = Eng(es, nc, nc.vector, "dve")
        self.pool = Eng(es, nc, nc.gpsimd, "pool")
        self.sp = Eng(es, nc, nc.sync, "sp")
        self.engs = [self.pe, self.act, self.dve, self.pool, self.sp]
        self.nsem = 0
        self.dma_deps = []
        self.ps = es.enter_context(nc.psum_tensor("ps", [128, 8, 512], F32))
        self.psd = [Dep() for _ in range(8)]
        self.bank_ptr = 0

    def dmadep(self):
        self.nsem += 1
        d = Dep(self.es.enter_context(self.nc.semaphore("ds%d" % self.nsem)))
        self.dma_deps.append(d)
        return d

    def _pre(self, eng, reads, writes):
        for d in reads:
            eng.wait(d.w)
        for d in writes:
            eng.wait(d.w)
            for ev in d.r.values():
                eng.wait(ev)

    def _post(self, ev, reads, writes):
        for d in reads:
            d.r[id(ev[0])] = ev
        for d in writes:
            d.w = ev
            d.r = {}

    def op(self, eng, fn, reads=(), writes=()):
        self._pre(eng, reads, writes)
        ins = fn(eng.e)
        ev = eng.sig(ins)
        self._post(ev, reads, writes)
        return ev

    def mm(self, mms, reads=(), writes=()):
        self._pre(self.pe, reads, writes)
        ins = None
        for (o, l, r, st, sp) in mms:
            ins = self.nc.tensor.matmul(o, lhsT=l, rhs=r, start=st, stop=sp)
        ev = self.pe.sig(ins)
        self._post(ev, reads, writes)
        return ev

    def dma(self, q, pairs, reads=(), writes=(), semdep=None):
        self._pre(q, reads, writes)
        for (o, i) in pairs:
            ins = q.e.dma_start(out=o, in_=i)
            semdep.semval += 16
            ins.then_inc(semdep.sem, 16)
        ev = (semdep.sem, semdep.semval)
        self._post(ev, reads, writes)
        return ev

    def banks(self, n=1):
        if n == 2 and self.bank_ptr % 2:
            self.bank_ptr += 1
        b = self.bank_ptr % 8
        self.bank_ptr += n
        return b

    def barrier(self):
        evs = []
        for e in self.engs:
            if e.cnt:
                evs.append((e.sem, e.cnt))
        for d in self.dma_deps:
            if d.semval:
                evs.append((d.sem, d.semval))
        for e in self.engs:
            for ev in evs:
                e.wait(ev)


_SB_UID = [0]


def sb(es, nc, name, shape, dt):
    _SB_UID[0] += 1
    return es.enter_context(nc.sbuf_tensor("sb%d_%s" % (_SB_UID[0], name), shape, dt))


def make_consts():
    c = {}
    c["ident"] = np.eye(128, dtype=np.float32)
    c["ones512"] = np.full((128, 128), 1.0 / 512.0, dtype=np.float32)
    wins = (2, 4, 8, 16)
    band = np.zeros((4, 5, 128, 128), dtype=np.float64)
    for g, w in enumerate(wins):
        half = w // 2
        for v, (tile, dt) in enumerate([(5, 0), (0, 0), (NT - 1, 0), (5, -1), (5, 1)]):
            for tl in range(128):
                t = tile * 128 + tl
                lo = min(max(t - half, 0), S - 1)
                hi = min(max(t + half - 1, 0), S - 1)
                cnt = hi - lo + 1
                for j in range(lo, hi + 1):
                    jt, jl = divmod(j, 128)
                    if jt == tile + dt:
                        band[g, v, jl, tl] += 1.0 / cnt
                if dt == 0:
                    band[g, v, tl, tl] -= 1.0
    c["band"] = np.ascontiguousarray(band.transpose(2, 0, 1, 3)).astype(np.float32)
    b = np.arange(32)
    ang = 2 * np.pi * np.outer(b, b) / 32.0
    Cr = np.cos(ang)
    Ci = -np.sin(ang)
    I4 = np.eye(4)
    FA = np.concatenate([np.kron(Cr, I4), np.kron(Ci, I4)], axis=1)
    FB = np.concatenate([np.kron(-Ci, I4), np.kron(Cr, I4)], axis=1)
    c["fab"] = np.stack([FA, FB], axis=1).astype(np.float32)
    a = np.arange(128)
    sp = (32 * a[None, :] + b[:, None])
    th = 2 * np.pi * (a[:, None, None] * sp[None, :, :]) / 4096.0
    c["gcs"] = np.stack([np.cos(th), np.sin(th)], axis=1).astype(np.float32)
    ch = np.arange(128)
    a2 = 2 * np.pi * np.outer(ch, ch) / 128.0
    c["cs128"] = np.stack([np.cos(a2), np.sin(a2)], axis=1).astype(np.float32)
    return c


CONST_SHAPES = {
    "ident": [128, 128], "ones512": [128, 128], "band": [128, 4, 5, 128], "fab": [128, 2, 256],
    "gcs": [128, 2, 32, 128], "cs128": [128, 2, 128],
}

IN_SHAPES = {
    "x": [S, D],
    "norm_g": [5, 128, D],
    "ev_w_in": [D, 1536], "ev_w_out": [D, D], "od_w_in": [D, 1536], "od_w_out": [D, D],
    "ffn_w_gate": [2, D, DFF], "ffn_w_up": [2, D, DFF], "ffn_w_down": [2, DFF, D],
    "ev_conv_w": [128, 4, 31],
    "ev_vec": [128, 4, 4],
    "ev_pool_w": [128, 4, 128],
    "od_fourier_w": [128, 4, 128],
    "od_vln": [128, 2, 512],
    "od_spatial_wT": [128, 4, 128],
    "od_spatial_b": [128, 512],
}
IN_SHAPES.update(CONST_SHAPES)


def build(stop_after=None, dumps=()):
    nc = bass.Bass("TRN2", target_bir_lowering=False)
    T = {}
    for name, shp in IN_SHAPES.items():
        T[name] = nc.dram_tensor(name, shp, F32, kind="ExternalInput").ap()
    out = nc.dram_tensor("out", [S, D], F32, kind="ExternalOutput").ap()
    xres = nc.dram_tensor("xres", [S, D], F32, kind="Internal").ap()
    dump_t = {}

    with ExitStack() as es:
        k = K(es, nc)
        pe, act, dve, pool, sp = k.pe, k.act, k.dve, k.pool, k.sp
        ps = k.ps
        psd = k.psd

        ident = sb(es, nc, "ident", [128, 128], BF16)
        ssb = sb(es, nc, "ssb", [128, 8], F32)
        tmp = sb(es, nc, "tmp", [128, 3, 512], F32)
        d_ident = k.dmadep()
        d_ss = [Dep() for _ in range(8)]
        d_tmp = [Dep() for _ in range(3)]
        d_xres = [Dep() for _ in range(NT)]
        cnt = {"xin": 0, "xout": 0, "hb": 0, "ss": 0, "tmp": 0}
        uid = [0]

        def rr(name, n):
            v = cnt[name] % n
            cnt[name] += 1
            return v

        k.dma(pool, [(ident[:], T["ident"])], writes=[d_ident], semdep=d_ident)

        io_sems = {"gbc": [k.dmadep(), k.dmadep()], "xin": [k.dmadep() for _ in range(3)],
                   "xout": [k.dmadep() for _ in range(2)]}

        class IO:
            def __init__(self, scope):
                uid[0] += 1
                u = str(uid[0])
                self.gbc = sb(scope, nc, "gbc" + u, [128, 2, D], F32)
                self.xin = sb(scope, nc, "xin" + u, [128, 3, D], F32)
                self.xout = sb(scope, nc, "xout" + u, [128, 2, D], F32)
                self.hb = sb(scope, nc, "hb" + u, [128, 2, D], BF16)
                self.junk = sb(scope, nc, "junk" + u, [128, D], BF16)
                self.d_gbc = io_sems["gbc"]
                self.d_xin = io_sems["xin"]
                self.d_xout = io_sems["xout"]
                self.d_hb = [Dep(), Dep()]
                self.d_junk = Dep()

        def add_dump(name, ap_sb, shape, dt, deps):
            if name not in dumps:
                return
            t = nc.dram_tensor("dbg_" + name, shape, dt, kind="ExternalOutput").ap()
            dd = k.dmadep()
            k.dma(sp, [(t, ap_sb)], reads=deps, semdep=dd)
            dump_t[name] = dd

        def norm_phase(io, src_fn, src_deps, tiles, gidx, hT, hT_deps, col0=0):
            gbc, xin, xout, hb, junk = io.gbc, io.xin, io.xout, io.hb, io.junk
            d_gbc, d_xin, d_xout, d_hb, d_junk = io.d_gbc, io.d_xin, io.d_xout, io.d_hb, io.d_junk
            k.dma(sp, [(gbc[:, 0, :], T["norm_g"][gidx])], writes=[d_gbc[0]], semdep=d_gbc[0])
            for i, t in enumerate(tiles):
                xs = rr("xin", 3)
                k.dma(sp, [(xin[:, xs, :], src_fn(t))], reads=[src_deps[t]] if src_deps else [],
                      writes=[d_xin[xs]], semdep=d_xin[xs])
                s_ = rr("ss", 8)
                k.op(act, lambda e: e.activation(out=junk[:], in_=xin[:, xs, :], func=AF.Square,
                                                 accum_out=ssb[:, s_:s_ + 1]),
                     reads=[d_xin[xs]], writes=[d_junk, d_ss[s_]])
                k.op(act, lambda e: e.activation(out=ssb[:, s_:s_ + 1], in_=ssb[:, s_:s_ + 1], func=AF.Sqrt, bias=float(D * EPS)),
                     reads=[d_ss[s_]], writes=[d_ss[s_]])
                k.op(dve, lambda e: e.reciprocal(out=ssb[:, s_:s_ + 1], in_=ssb[:, s_:s_ + 1]),
                     reads=[d_ss[s_]], writes=[d_ss[s_]])
                h_ = rr("hb", 2)
                k.op(dve, lambda e: e.scalar_tensor_tensor(out=hb[:, h_, :], in0=xin[:, xs, :],
                                                           scalar=ssb[:, s_:s_ + 1], in1=gbc[:, 0, :],
                                                           op0=ALU.mult, op1=ALU.mult),
                     reads=[d_xin[xs], d_ss[s_], d_gbc[0]], writes=[d_hb[h_]])
                b = k.banks(2)
                psv = ps[:, b:b + 2, :].rearrange("p b (c n) -> p (b c) n", n=128)
                k.mm([(psv[:, kk, :], hb[:, h_, kk * 128:(kk + 1) * 128], ident[:], True, True)
                      for kk in range(8)],
                     reads=[d_hb[h_], d_ident], writes=[psd[b], psd[b + 1]])
                c0 = col0 + i * 128
                k.op(act, lambda e: e.activation(out=hT[:, :, c0:c0 + 128], in_=psv, func=AF.Copy,
                                                 scale=float(np.sqrt(D))),
                     reads=[psd[b], psd[b + 1]], writes=[hT_deps[i]])

        def proj_fm(w_sb, w_dep, col_chunks, hT, hT_deps_all, ntg, epilogue):
            for tg in range(ntg):
                for ci, cols in enumerate(col_chunks):
                    bl = []
                    for c0 in cols:
                        b = k.banks(1)
                        k.mm([(ps[:, b, :], w_sb[:, kk, c0:c0 + 128], hT[:, kk, tg * 512:(tg + 1) * 512],
                               kk == 0, kk == 7) for kk in range(8)],
                             reads=[w_dep] + hT_deps_all(tg), writes=[psd[b]])
                        bl.append(b)
                    epilogue(tg, ci, bl)

        def out_phase(io, yT, yT_dep, wo, wo_dep, src_fn, src_deps):
            gbc, xin, xout, hb, junk = io.gbc, io.xin, io.xout, io.hb, io.junk
            d_gbc, d_xin, d_xout, d_hb, d_junk = io.d_gbc, io.d_xin, io.d_xout, io.d_hb, io.d_junk
            for t in range(NT):
                xs = rr("xin", 3)
                k.dma(sp, [(xin[:, xs, :], src_fn(t))], reads=[src_deps[t]] if src_deps else [],
                      writes=[d_xin[xs]], semdep=d_xin[xs])
                b = k.banks(2)
                mms = []
                for nh in range(2):
                    for kk in range(8):
                        mms.append((ps[:, b + nh, :], yT[:, kk, t * 128:(t + 1) * 128],
                                    wo[:, kk, nh * 512:(nh + 1) * 512], kk == 0, kk == 7))
                k.mm(mms, reads=[yT_dep, wo_dep], writes=[psd[b], psd[b + 1]])
                xo = rr("xout", 2)
                k.op(dve, lambda e: e.tensor_tensor(out=xout[:, xo, :], in0=xin[:, xs, :],
                                                    in1=ps[:, b:b + 2, :].rearrange("p b n -> p (b n)"),
                                                    op=ALU.add),
                     reads=[d_xin[xs], psd[b], psd[b + 1]], writes=[d_xout[xo]])
                k.dma(sp, [(xres[t * 128:(t + 1) * 128, :], xout[:, xo, :])], reads=[d_xout[xo]],
                      writes=[d_xres[t]], semdep=d_xout[xo])

        def ffn_phase(l, last):
            with ExitStack() as fs:
                io = IO(fs)
                gbc, xin, xout, hb, junk = io.gbc, io.xin, io.xout, io.hb, io.junk
                d_gbc, d_xin, d_xout, d_hb, d_junk = io.d_gbc, io.d_xin, io.d_xout, io.d_hb, io.d_junk
                if last:
                    k.dma(sp, [(gbc[:, 1, :], T["norm_g"][4])], writes=[d_gbc[1]], semdep=d_gbc[1])
                    k.op(dve, lambda e: e.tensor_scalar(out=gbc[:, 1, :], in0=gbc[:, 1, :], scalar1=float(np.sqrt(D)),
                                                        scalar2=None, op0=ALU.mult),
                         reads=[d_gbc[1]], writes=[d_gbc[1]])
                h2T = sb(fs, nc, "h2T%d" % l, [128, 8, ST], BF16)
                gT = sb(fs, nc, "gT%d" % l, [128, NJ, ST], BF16)
                wd = sb(fs, nc, "wd%d" % l, [128, NJ, D], BF16)
                wgu = sb(fs, nc, "wgu%d" % l, [128, 3, 2, 8, 256], BF16)
                d_h2T = [Dep() for _ in range(8)]
                d_gT = [Dep() for _ in range(NJ)]
                d_wd = k.dmadep()
                d_wgu = [k.dmadep() for _ in range(3)]
                wdv = T["ffn_w_down"][l].rearrange("(j p) n -> p j n", p=128)
                k.dma(pool, [(wd[:, 0:11, :], wdv[:, 0:11, :]), (wd[:, 11:22, :], wdv[:, 11:22, :])],
                      writes=[d_wd], semdep=d_wd)
                wgv = T["ffn_w_gate"][l].rearrange("(k p) n -> p k n", p=128)
                wuv = T["ffn_w_up"][l].rearrange("(k p) n -> p k n", p=128)
                nslot = 0
                for st in range(NST):
                    tiles = list(range(st * 8, st * 8 + 8))
                    norm_phase(io, lambda t: xres[t * 128:(t + 1) * 128, :], d_xres, tiles, 1 + 2 * l, h2T, d_h2T)
                    for c in range(NJ // 2):
                        sl = nslot % 3
                        nslot += 1
                        k.dma(pool, [(wgu[:, sl, 0, :, :], wgv[:, :, c * 256:(c + 1) * 256]),
                                     (wgu[:, sl, 1, :, :], wuv[:, :, c * 256:(c + 1) * 256])],
                              writes=[d_wgu[sl]], semdep=d_wgu[sl])
                        for jj in range(2):
                            j = c * 2 + jj
                            for tg in range(ST // 512):
                                bg = k.banks(1)
                                k.mm([(ps[:, bg, :], wgu[:, sl, 0, kk, jj * 128:(jj + 1) * 128],
                                       h2T[:, kk, tg * 512:(tg + 1) * 512], kk == 0, kk == 7) for kk in range(8)],
                                     reads=[d_wgu[sl]] + d_h2T[tg * 4:(tg + 1) * 4], writes=[psd[bg]])
                                bu = k.banks(1)
                                k.mm([(ps[:, bu, :], wgu[:, sl, 1, kk, jj * 128:(jj + 1) * 128],
                                       h2T[:, kk, tg * 512:(tg + 1) * 512], kk == 0, kk == 7) for kk in range(8)],
                                     reads=[d_wgu[sl]] + d_h2T[tg * 4:(tg + 1) * 4], writes=[psd[bu]])
                                ts_ = rr("tmp", 3)
                                k.op(act, lambda e: e.activation(out=tmp[:, ts_, :], in_=ps[:, bg, :], func=AF.Silu),
                                     reads=[psd[bg]], writes=[d_tmp[ts_]])
                                k.op(dve, lambda e: e.tensor_tensor(out=gT[:, j, tg * 512:(tg + 1) * 512],
                                                                    in0=tmp[:, ts_, :], in1=ps[:, bu, :], op=ALU.mult),
                                     reads=[d_tmp[ts_], psd[bu]], writes=[d_gT[j]])
                    for ti, t in enumerate(tiles):
                        xs = rr("xin", 3)
                        k.dma(sp, [(xin[:, xs, :], xres[t * 128:(t + 1) * 128, :])], reads=[d_xres[t]],
                              writes=[d_xin[xs]], semdep=d_xin[xs])
                        b = k.banks(2)
                        mms = []
                        for nh in range(2):
                            for j in range(NJ):
                                mms.append((ps[:, b + nh, :], gT[:, j, ti * 128:(ti + 1) * 128],
                                            wd[:, j, nh * 512:(nh + 1) * 512], j == 0, j == NJ - 1))
                        k.mm(mms, reads=[d_wd] + d_gT, writes=[psd[b], psd[b + 1]])
                        xo = rr("xout", 2)
                        k.op(dve, lambda e: e.tensor_tensor(out=xout[:, xo, :], in0=xin[:, xs, :],
                                                            in1=ps[:, b:b + 2, :].rearrange("p b n -> p (b n)"),
                                                            op=ALU.add),
                             reads=[d_xin[xs], psd[b], psd[b + 1]], writes=[d_xout[xo]])
                        if not last:
                            k.dma(sp, [(xres[t * 128:(t + 1) * 128, :], xout[:, xo, :])], reads=[d_xout[xo]],
                                  writes=[d_xres[t]], semdep=d_xout[xo])
                        else:
                            s_ = rr("ss", 8)
                            k.op(act, lambda e: e.activation(out=junk[:], in_=xout[:, xo, :], func=AF.Square,
                                                             accum_out=ssb[:, s_:s_ + 1]),
                                 reads=[d_xout[xo]], writes=[d_junk, d_ss[s_]])
                            k.op(act, lambda e: e.activation(out=ssb[:, s_:s_ + 1], in_=ssb[:, s_:s_ + 1], func=AF.Sqrt, bias=float(D * EPS)),
                                 reads=[d_ss[s_]], writes=[d_ss[s_]])
                            k.op(dve, lambda e: e.reciprocal(out=ssb[:, s_:s_ + 1], in_=ssb[:, s_:s_ + 1]),
                                 reads=[d_ss[s_]], writes=[d_ss[s_]])
                            xs2 = rr("xin", 3)
                            k.op(dve, lambda e: e.scalar_tensor_tensor(out=xin[:, xs2, :], in0=xout[:, xo, :],
                                                                       scalar=ssb[:, s_:s_ + 1], in1=gbc[:, 1, :],
                                                                       op0=ALU.mult, op1=ALU.mult),
                                 reads=[d_xout[xo], d_ss[s_], d_gbc[1]], writes=[d_xin[xs2]])
                            k.dma(sp, [(out[t * 128:(t + 1) * 128, :], xin[:, xs2, :])], reads=[d_xin[xs2]],
                                  semdep=d_xin[xs2])
                k.barrier()

        def mixer0():
            with ExitStack() as ms:
                bigA = sb(ms, nc, "bigA", [128, 8, S], BF16)
                uT = sb(ms, nc, "uT", [128, 4, S + 32], BF16)
                d_hT = [Dep() for _ in range(NT)]
                d_uT = [Dep() for _ in range(4)]
                d_yT = Dep()
                evec = sb(ms, nc, "evec", [128, 4, 4], F32)
                d_evec = k.dmadep()
                k.dma(sp, [(evec[:], T["ev_vec"])], writes=[d_evec], semdep=d_evec)
                for c in range(4):
                    k.op(pool, lambda e: e.memset(uT[:, c, 0:16], 0.0), writes=[d_uT[c]])
                    k.op(pool, lambda e: e.memset(uT[:, c, S + 16:S + 32], 0.0), writes=[d_uT[c]])
                src = lambda t: T["x"][t * 128:(t + 1) * 128, :]
                with ExitStack() as sn:
                    norm_phase(IO(sn), src, None, list(range(NT)), 0, bigA, d_hT)
                    k.barrier()
                if stop_after == "norm0":
                    add_dump("hT", bigA[:], [128, 8, S], BF16, d_hT)
                    return
                with ExitStack() as s1:
                    p_sb = sb(s1, nc, "p_sb", [128, NT, 512], BF16)
                    d_p = [Dep() for _ in range(NT)]
                    with ExitStack() as s2:
                        w_in = sb(s2, nc, "w_in", [128, 8, 1536], BF16)
                        d_w = k.dmadep()
                        wv = T["ev_w_in"].rearrange("(k p) n -> p k n", p=128)
                        k.dma(pool, [(w_in[:, :, 0:768], wv[:, :, 0:768]), (w_in[:, :, 768:1536], wv[:, :, 768:1536])],
                              writes=[d_w], semdep=d_w)

                        def epi_a(tg, c, bl):
                            ts_ = rr("tmp", 3)
                            k.op(act, lambda e: e.activation(out=tmp[:, ts_, :], in_=ps[:, bl[1], :], func=AF.Sigmoid),
                                 reads=[psd[bl[1]]], writes=[d_tmp[ts_]])
                            k.op(dve, lambda e: e.tensor_tensor(out=uT[:, c, 16 + tg * 512:16 + (tg + 1) * 512],
                                                                in0=tmp[:, ts_, :], in1=ps[:, bl[0], :], op=ALU.mult),
                                 reads=[d_tmp[ts_], psd[bl[0]]], writes=[d_uT[c]])

                        proj_fm(w_in, d_w, [[c * 128, 512 + c * 128] for c in range(4)], bigA,
                                lambda tg: d_hT[tg * 4:(tg + 1) * 4], 8, epi_a)
                        for t in range(NT):
                            b = k.banks(1)
                            k.mm([(ps[:, b, :], bigA[:, kk, t * 128:(t + 1) * 128], w_in[:, kk, 1024:1536],
                                   kk == 0, kk == 7) for kk in range(8)],
                                 reads=[d_w, d_hT[t]], writes=[psd[b]])
                            eng = act if t % 2 == 0 else dve
                            if eng is act:
                                k.op(act, lambda e: e.activation(out=p_sb[:, t, :], in_=ps[:, b, :], func=AF.Copy),
                                     reads=[psd[b]], writes=[d_p[t]])
                            else:
                                k.op(dve, lambda e: e.tensor_copy(out=p_sb[:, t, :], in_=ps[:, b, :]),
                                     reads=[psd[b]], writes=[d_p[t]])
                        k.barrier()
                    if stop_after == "proj0":
                        add_dump("uT", uT[:], [128, 4, S + 32], BF16, d_uT)
                        add_dump("p_sb", p_sb[:], [128, NT, 512], BF16, d_p)
                        return
                    with ExitStack() as s2:
                        band = sb(s2, nc, "band", [128, 4, 5, 128], BF16)
                        pw = sb(s2, nc, "pw", [128, 4, 128], BF16)
                        pooled = sb(s2, nc, "pooled", [128, 2, 512], BF16)
                        d_band = k.dmadep()
                        d_pw = k.dmadep()
                        d_pooled = [Dep(), Dep()]
                        k.dma(pool, [(band[:], T["band"])], writes=[d_band], semdep=d_band)
                        k.dma(pool, [(pw[:], T["ev_pool_w"])], writes=[d_pw], semdep=d_pw)
                        npl = 0
                        for tg in range(8):
                            for g in range(4):
                                b = k.banks(1)
                                mms = []
                                rd = {}
                                for ti in range(4):
                                    t = tg * 4 + ti
                                    srcs = []
                                    if t > 0:
                                        srcs.append((t - 1, 3))
                                    srcs.append((t, 1 if t == 0 else (2 if t == NT - 1 else 0)))
                                    if t < NT - 1:
                                        srcs.append((t + 1, 4))
                                    for si, (tt, v) in enumerate(srcs):
                                        mms.append((ps[:, b, ti * 128:(ti + 1) * 128],
                                                    p_sb[:, tt, g * 128:(g + 1) * 128], band[:, g, v, :],
                                                    si == 0, si == len(srcs) - 1))
                                        rd[tt] = d_p[tt]
                                k.mm(mms, reads=[d_band] + list(rd.values()), writes=[psd[b]])
                                pl = npl % 2
                                npl += 1
                                k.op(act, lambda e: e.activation(out=pooled[:, pl, :], in_=ps[:, b, :], func=AF.Copy),
                                     reads=[psd[b]], writes=[d_pooled[pl]])
                                b2 = k.banks(1)
                                k.mm([(ps[:, b2, :], pw[:, g, :], pooled[:, pl, :], True, True)],
                                     reads=[d_pw, d_pooled[pl]], writes=[psd[b2]])
                                k.op(dve, lambda e: e.tensor_scalar(out=bigA[:, 4 + g, tg * 512:(tg + 1) * 512],
                                                                    in0=ps[:, b2, :], scalar1=evec[:, g, 3:4],
                                                                    scalar2=None, op0=ALU.mult),
                                     reads=[psd[b2], d_evec] + d_hT, writes=[d_yT])
                        k.barrier()
                if stop_after == "pool0":
                    add_dump("yT", bigA[:], [128, 8, S], BF16, [d_yT])
                    return
                with ExitStack() as s1:
                    cw = sb(s1, nc, "cw", [128, 4, 31], F32)
                    identf = sb(s1, nc, "identf", [128, 128], F32)
                    ones = sb(s1, nc, "ones", [128, 128], BF16)
                    diag = sb(s1, nc, "diag", [128, 4, 31, 128], BF16)
                    vf = sb(s1, nc, "vf", [128, 2, 4, 512], F32)
                    vb = sb(s1, nc, "vb", [128, 2, 4, 512], BF16)
                    vq = sb(s1, nc, "vq", [128, 2, 4, 512], BF16)
                    st_sb = sb(s1, nc, "st_sb", [128, 2, 3, 512], F32)
                    t1 = sb(s1, nc, "t1", [128, 3, 512], F32)
                    d_cw = k.dmadep()
                    d_idf = k.dmadep()
                    d_ones = k.dmadep()
                    d_diag = Dep()
                    d_vf = [[Dep() for _ in range(4)] for _ in range(2)]
                    d_vb = [[Dep() for _ in range(4)] for _ in range(2)]
                    d_vq = [[Dep() for _ in range(4)] for _ in range(2)]
                    d_st = [[Dep() for _ in range(3)] for _ in range(2)]
                    d_t1 = [Dep() for _ in range(3)]
                    k.dma(sp, [(cw[:], T["ev_conv_w"])], writes=[d_cw], semdep=d_cw)
                    k.dma(sp, [(identf[:], T["ident"])], writes=[d_idf], semdep=d_idf)
                    k.dma(pool, [(ones[:], T["ones512"])], writes=[d_ones], semdep=d_ones)
                    for c in range(4):
                        for tap in range(31):
                            k.op(dve, lambda e: e.tensor_scalar(out=diag[:, c, tap, :], in0=identf[:],
                                                                scalar1=cw[:, c, tap:tap + 1], scalar2=None,
                                                                op0=ALU.mult),
                                 reads=[d_cw, d_idf], writes=[d_diag])
                    nt1 = 0
                    for tg in range(8):
                        r_ = tg % 2
                        for c in range(4):
                            b = k.banks(1)
                            k.mm([(ps[:, b, :], diag[:, c, tap, :],
                                   uT[:, c, 1 + tg * 512 + tap:1 + tg * 512 + tap + 512], tap == 0, tap == 30)
                                  for tap in range(31)],
                                 reads=[d_diag, d_uT[c]], writes=[psd[b]])
                            k.op(act, lambda e: e.activation(out=vf[:, r_, c, :], in_=ps[:, b, :], func=AF.Identity,
                                                             bias=evec[:, c, 0:1]),
                                 reads=[psd[b], d_evec], writes=[d_vf[r_][c]])
                            k.op(act, lambda e: e.activation(out=vq[:, r_, c, :], in_=ps[:, b, :], func=AF.Square,
                                                             bias=evec[:, c, 0:1]),
                                 reads=[psd[b], d_evec], writes=[d_vq[r_][c]])
                            k.op(pool, lambda e: e.tensor_copy(out=vb[:, r_, c, :], in_=vf[:, r_, c, :]),
                                 reads=[d_vf[r_][c]], writes=[d_vb[r_][c]])
                        bm = k.banks(1)
                        k.mm([(ps[:, bm, :], ones[:], vb[:, r_, c, :], c == 0, c == 3) for c in range(4)],
                             reads=[d_ones] + d_vb[r_], writes=[psd[bm]])
                        bq = k.banks(1)
                        k.mm([(ps[:, bq, :], ones[:], vq[:, r_, c, :], c == 0, c == 3) for c in range(4)],
                             reads=[d_ones] + d_vq[r_], writes=[psd[bq]])
                        k.op(act, lambda e: e.activation(out=st_sb[:, r_, 0, :], in_=ps[:, bm, :], func=AF.Copy),
                             reads=[psd[bm]], writes=[d_st[r_][0]])
                        k.op(act, lambda e: e.activation(out=st_sb[:, r_, 1, :], in_=ps[:, bm, :], func=AF.Square),
                             reads=[psd[bm]], writes=[d_st[r_][1]])
                        k.op(dve, lambda e: e.tensor_tensor(out=st_sb[:, r_, 1, :], in0=ps[:, bq, :],
                                                            in1=st_sb[:, r_, 1, :], op=ALU.subtract),
                             reads=[psd[bq], d_st[r_][1]], writes=[d_st[r_][1]])
                        k.op(act, lambda e: e.activation(out=st_sb[:, r_, 1, :], in_=st_sb[:, r_, 1, :], func=AF.Sqrt,
                                                         bias=float(EPS)),
                             reads=[d_st[r_][1]], writes=[d_st[r_][1]])
                        k.op(dve, lambda e: e.reciprocal(out=st_sb[:, r_, 1, :], in_=st_sb[:, r_, 1, :]),
                             reads=[d_st[r_][1]], writes=[d_st[r_][1]])
                        for c in range(4):
                            ti_ = nt1 % 3
                            nt1 += 1
                            k.op(pool, lambda e: e.tensor_tensor(out=t1[:, ti_, :], in0=vf[:, r_, c, :],
                                                                 in1=st_sb[:, r_, 0, :], op=ALU.subtract),
                                 reads=[d_vf[r_][c], d_st[r_][0]], writes=[d_t1[ti_]])
                            k.op(dve, lambda e: e.tensor_tensor(out=t1[:, ti_, :], in0=t1[:, ti_, :],
                                                                in1=st_sb[:, r_, 1, :], op=ALU.mult),
                                 reads=[d_t1[ti_], d_st[r_][1]], writes=[d_t1[ti_]])
                            k.op(act, lambda e: e.activation(out=bigA[:, c, tg * 512:(tg + 1) * 512], in_=t1[:, ti_, :],
                                                             func=AF.Silu, scale=evec[:, c, 1:2], bias=evec[:, c, 2:3]),
                                 reads=[d_t1[ti_], d_evec, d_yT], writes=[d_yT])
                    k.barrier()
                if stop_after == "conv0":
                    add_dump("yT", bigA[:], [128, 8, S], BF16, [d_yT])
                    return
                with ExitStack() as s1:
                    wo = sb(s1, nc, "wo", [128, 8, D], BF16)
                    d_wo = k.dmadep()
                    k.dma(pool, [(wo[:], T["ev_w_out"].rearrange("(k p) n -> p k n", p=128))],
                          writes=[d_wo], semdep=d_wo)
                    out_phase(IO(s1), bigA, d_yT, wo, d_wo, src, None)
                    k.barrier()
                k.barrier()

        def mixer1():
            with ExitStack() as ms:
                bigA = sb(ms, nc, "bigA1", [128, 8, S], BF16)
                cT = sb(ms, nc, "cT", [128, 4, S], BF16)
                d_hT = [Dep() for _ in range(NT)]
                d_cT = [Dep() for _ in range(4)]
                d_yT = Dep()
                src = lambda t: xres[t * 128:(t + 1) * 128, :]
                with ExitStack() as sn:
                    norm_phase(IO(sn), src, d_xres, list(range(NT)), 2, bigA, d_hT)
                    k.barrier()
                with ExitStack() as s1:
                    uT = sb(s1, nc, "uT1", [128, 4, S], BF16)
                    d_uT = [Dep() for _ in range(4)]
                    w_in = sb(s1, nc, "w_in1", [128, 8, 1536], BF16)
                    d_w = k.dmadep()
                    wv = T["od_w_in"].rearrange("(k p) n -> p k n", p=128)
                    k.dma(pool, [(w_in[:, :, 0:768], wv[:, :, 0:768]), (w_in[:, :, 768:1536], wv[:, :, 768:1536])],
                          writes=[d_w], semdep=d_w)
                    vln = sb(s1, nc, "vln", [128, 2, 512], F32)
                    sbb = sb(s1, nc, "sbb", [128, 512], F32)
                    swT = sb(s1, nc, "swT", [128, 4, 128], BF16)
                    d_vln = k.dmadep()
                    d_sbb = k.dmadep()
                    d_swT = k.dmadep()
                    k.dma(sp, [(vln[:], T["od_vln"])], writes=[d_vln], semdep=d_vln)
                    k.dma(sp, [(sbb[:], T["od_spatial_b"])], writes=[d_sbb], semdep=d_sbb)
                    k.dma(pool, [(swT[:], T["od_spatial_wT"])], writes=[d_swT], semdep=d_swT)

                    nev = [0]

                    def epi_c(tg, c, bl):
                        nev[0] += 1
                        if nev[0] % 2:
                            k.op(act, lambda e: e.activation(out=cT[:, c, tg * 512:(tg + 1) * 512], in_=ps[:, bl[0], :],
                                                             func=AF.Copy),
                                 reads=[psd[bl[0]]], writes=[d_cT[c]])
                        else:
                            k.op(dve, lambda e: e.tensor_copy(out=cT[:, c, tg * 512:(tg + 1) * 512], in_=ps[:, bl[0], :]),
                                 reads=[psd[bl[0]]], writes=[d_cT[c]])

                    proj_fm(w_in, d_w, [[c * 128] for c in range(4)], bigA, lambda tg: d_hT[tg * 4:(tg + 1) * 4], 8, epi_c)

                    def epi_u(tg, c, bl):
                        k.op(act, lambda e: e.activation(out=uT[:, c, tg * 512:(tg + 1) * 512], in_=ps[:, bl[0], :],
                                                         func=AF.Gelu),
                             reads=[psd[bl[0]]], writes=[d_uT[c]])

                    proj_fm(w_in, d_w, [[512 + c * 128] for c in range(4)], bigA, lambda tg: d_hT[tg * 4:(tg + 1) * 4],
                            8, epi_u)
                    if stop_after == "proj1":
                        add_dump("cT", cT[:], [128, 4, S], BF16, d_cT)
                        add_dump("uT", uT[:], [128, 4, S], BF16, d_uT)
                        return
                    vt = sb(s1, nc, "vt", [128, 2, 512], F32)
                    vn = sb(s1, nc, "vn", [128, 2, 512], BF16)
                    bst = sb(s1, nc, "bst", [128, 2, 4, 6], F32)
                    mv = sb(s1, nc, "mv", [128, 2, 4, 2], F32)
                    d_vt = [Dep(), Dep()]
                    d_vn = [Dep(), Dep()]
                    d_bst = [Dep(), Dep()]
                    d_mv = [Dep(), Dep()]
                    for t in range(NT):
                        r_ = t % 2
                        b = k.banks(1)
                        k.mm([(ps[:, b, :], bigA[:, kk, t * 128:(t + 1) * 128], w_in[:, kk, 1024:1536],
                               kk == 0, kk == 7) for kk in range(8)],
                             reads=[d_w, d_hT[t]], writes=[psd[b]])
                        k.op(act, lambda e: e.activation(out=vt[:, r_, :], in_=ps[:, b, :], func=AF.Gelu),
                             reads=[psd[b]], writes=[d_vt[r_]])
                        for h in range(4):
                            k.op(dve, lambda e: e.bn_stats(out=bst[:, r_, h, :], in_=vt[:, r_, h * 128:(h + 1) * 128]),
                                 reads=[d_vt[r_]], writes=[d_bst[r_]])
                        for h in range(4):
                            k.op(dve, lambda e: e.bn_aggr(out=mv[:, r_, h, :], in_=bst[:, r_, h, :]),
                                 reads=[d_bst[r_]], writes=[d_mv[r_]])
                        k.op(act, lambda e: e.activation(out=mv[:, r_, :, 1], in_=mv[:, r_, :, 1], func=AF.Sqrt, bias=float(EPS)),
                             reads=[d_mv[r_]], writes=[d_mv[r_]])
                        k.op(dve, lambda e: e.reciprocal(out=mv[:, r_, :, 1], in_=mv[:, r_, :, 1]),
                             reads=[d_mv[r_]], writes=[d_mv[r_]])
                        for h in range(4):
                            k.op(dve, lambda e: e.tensor_scalar(out=vt[:, r_, h * 128:(h + 1) * 128],
                                                                in0=vt[:, r_, h * 128:(h + 1) * 128],
                                                                scalar1=mv[:, r_, h, 0:1], scalar2=mv[:, r_, h, 1:2],
                                                                op0=ALU.subtract, op1=ALU.mult),
                                 reads=[d_vt[r_], d_mv[r_]], writes=[d_vt[r_]])
                        k.op(pool, lambda e: e.tensor_tensor(out=vt[:, r_, :], in0=vt[:, r_, :], in1=vln[:, 0, :],
                                                             op=ALU.mult),
                             reads=[d_vt[r_], d_vln], writes=[d_vt[r_]])
                        k.op(pool, lambda e: e.tensor_tensor(out=vn[:, r_, :], in0=vt[:, r_, :], in1=vln[:, 1, :],
                                                             op=ALU.add),
                             reads=[d_vt[r_], d_vln], writes=[d_vn[r_]])
                        b2 = k.banks(1)
                        k.mm([(ps[:, b2, h * 128:(h + 1) * 128], vn[:, r_, h * 128:(h + 1) * 128], swT[:, h, :],
                               True, True) for h in range(4)],
                             reads=[d_vn[r_], d_swT], writes=[psd[b2]])
                        ts_ = rr("tmp", 3)
                        k.op(dve, lambda e: e.tensor_tensor(out=tmp[:, ts_, :], in0=ps[:, b2, :], in1=sbb[:],
                                                            op=ALU.add),
                             reads=[psd[b2], d_sbb], writes=[d_tmp[ts_]])
                        k.op(dve, lambda e: e.tensor_tensor(
                            out=bigA[:, 4:8, t * 128:(t + 1) * 128],
                            in0=tmp[:, ts_, :].rearrange("p (h q) -> p h q", h=4),
                            in1=uT[:, :, t * 128:(t + 1) * 128], op=ALU.mult),
                             reads=[d_tmp[ts_]] + d_uT + d_hT, writes=[d_yT])
                    k.barrier()
                if stop_after == "sgu1":
                    add_dump("yT", bigA[:], [128, 8, S], BF16, [d_yT])
                    return
                with ExitStack() as s1:
                    fab = sb(s1, nc, "fab", [128, 2, 256], BF16)
                    gcs = sb(s1, nc, "gcs", [128, 2, 32, 128], BF16)
                    cs128 = sb(s1, nc, "cs128", [128, 2, 128], BF16)
                    fw = sb(s1, nc, "fw", [128, 4, 128], BF16)
                    m12 = sb(s1, nc, "m12", [128, 4, 2, 128], BF16)
                    Q = sb(s1, nc, "Q", [128, 2, 32, 32, 4], BF16)
                    Q2 = sb(s1, nc, "Q2", [128, 2, 32, 128], BF16)
                    Y = sb(s1, nc, "Y", [128, 2, 32, 128], BF16)
                    d_fab = k.dmadep()
                    d_gcs = k.dmadep()
                    d_cs = k.dmadep()
                    d_fw = k.dmadep()
                    d_m12 = Dep()
                    d_Q = Dep()
                    d_Q2 = Dep()
                    d_Y = Dep()
                    k.dma(pool, [(fab[:], T["fab"])], writes=[d_fab], semdep=d_fab)
                    k.dma(pool, [(gcs[:, 0], T["gcs"][:, 0]), (gcs[:, 1], T["gcs"][:, 1])], writes=[d_gcs], semdep=d_gcs)
                    k.dma(pool, [(cs128[:], T["cs128"])], writes=[d_cs], semdep=d_cs)
                    k.dma(pool, [(fw[:], T["od_fourier_w"])], writes=[d_fw], semdep=d_fw)
                    sc = 1.0 / np.sqrt(4096.0 * 128.0)
                    for h in range(4):
                        b = k.banks(1)
                        k.mm([(ps[:, b, 0:128], cs128[:, 0, :], fw[:, h, :], True, True),
                              (ps[:, b, 128:256], cs128[:, 1, :], fw[:, h, :], True, True)],
                             reads=[d_cs, d_fw], writes=[psd[b]])
                        k.op(act, lambda e: e.activation(out=m12[:, h, 0, :], in_=ps[:, b, 0:128], func=AF.Copy,
                                                         scale=float(sc)),
                             reads=[psd[b]], writes=[d_m12])
                        k.op(act, lambda e: e.activation(out=m12[:, h, 1, :], in_=ps[:, b, 128:256], func=AF.Copy,
                                                         scale=float(-sc)),
                             reads=[psd[b]], writes=[d_m12])
                    nq = [0]

                    def evac(out_ap, in_ap, reads, writes):
                        nq[0] += 1
                        if nq[0] % 2:
                            k.op(act, lambda e: e.activation(out=out_ap, in_=in_ap, func=AF.Copy), reads=reads,
                                 writes=writes)
                        else:
                            k.op(dve, lambda e: e.tensor_copy(out=out_ap, in_=in_ap), reads=reads, writes=writes)

                    for h in range(4):
                        for bb in range(0, 32, 2):
                            b = k.banks(1)
                            k.mm([(ps[:, b, i * 256:(i + 1) * 256], cT[:, h, (bb + i) * 128:(bb + i + 1) * 128],
                                   m12[:, h, :, :].rearrange("p r c -> p (r c)"), True, True) for i in range(2)],
                                 reads=[d_cT[h], d_m12], writes=[psd[b]])
                            for i in range(2):
                                evac(Q[:, :, :, bb + i, :],
                                     ps[:, b, i * 256:(i + 1) * 256].rearrange("p (r c j) -> p r c j", r=2, j=4),
                                     [psd[b]], [d_Q])
                        for ri in range(2):
                            for cg0 in range(0, 32, 4):
                                b = k.banks(1)
                                k.mm([(ps[:, b, i * 128:(i + 1) * 128],
                                       Q[:, ri, cg0 + i, :, :].rearrange("p b j -> p (b j)"), ident[:], True, True)
                                      for i in range(4)],
                                     reads=[d_Q, d_ident], writes=[psd[b]])
                                evac(Q2[:, ri, cg0:cg0 + 4, :].rearrange("p c a -> p (c a)"), ps[:, b, :],
                                     [psd[b]], [d_Q2])
                        for cg0 in range(0, 32, 2):
                            b = k.banks(1)
                            mms = []
                            for i in range(2):
                                mms.append((ps[:, b, i * 256:(i + 1) * 256], Q2[:, 0, cg0 + i, :], fab[:, 0, :], True, False))
                                mms.append((ps[:, b, i * 256:(i + 1) * 256], Q2[:, 1, cg0 + i, :], fab[:, 1, :], False, True))
                            k.mm(mms, reads=[d_Q2, d_fab], writes=[psd[b]])
                            for i in range(2):
                                cg = cg0 + i
                                evac(Y[:, :, :, cg * 4:(cg + 1) * 4],
                                     ps[:, b, i * 256:(i + 1) * 256].rearrange("p (r b j) -> p r b j", r=2, j=4),
                                     [psd[b]], [d_Y])
                        yv = bigA[:, h, :].rearrange("p (a b) -> p b a", b=32)
                        for b0 in range(0, 32, 4):
                            b = k.banks(1)
                            mms = []
                            for i in range(4):
                                mms.append((ps[:, b, i * 128:(i + 1) * 128], Y[:, 0, b0 + i, :], gcs[:, 0, b0 + i, :], True, False))
                                mms.append((ps[:, b, i * 128:(i + 1) * 128], Y[:, 1, b0 + i, :], gcs[:, 1, b0 + i, :], False, True))
                            k.mm(mms, reads=[d_Y, d_gcs], writes=[psd[b]])
                            evac(yv[:, b0:b0 + 4, :], ps[:, b, :].rearrange("p (b a) -> p b a", b=4),
                                 [psd[b]] + d_hT, [d_yT])
                    k.barrier()
                if stop_after == "four1":
                    add_dump("yT", bigA[:], [128, 8, S], BF16, [d_yT])
                    return
                with ExitStack() as s1:
                    wo = sb(s1, nc, "wo1", [128, 8, D], BF16)
                    d_wo = k.dmadep()
                    k.dma(pool, [(wo[:], T["od_w_out"].rearrange("(k p) n -> p k n", p=128))],
                          writes=[d_wo], semdep=d_wo)
                    out_phase(IO(s1), bigA, d_yT, wo, d_wo, src, d_xres)
                    k.barrier()
                k.barrier()

        stages = ["mixer0", "ffn0", "mixer1", "ffn1"]
        m0_stops = ("norm0", "proj0", "pool0", "conv0")
        m1_stops = ("proj1", "sgu1", "four1")
        done = False
        mixer0()
        if stop_after in m0_stops:
            done = True
        if not done and stop_after == "mixer0":
            done = True
        if not done:
            ffn_phase(0, last=False)
            if stop_after == "ffn0":
                done = True
        if not done:
            mixer1()
            if stop_after in m1_stops or stop_after == "mixer1":
                done = True
        if not done:
            ffn_phase(1, last=True)
        if done and stop_after in ("mixer0", "ffn0", "mixer1"):
            k.barrier()
            dd = k.dmadep()
            k.dma(sp, [(out[:, :], xres[:, :])], semdep=dd)
        k.barrier()
    return nc


def _rep(v, n=128):
    return np.ascontiguousarray(np.broadcast_to(np.asarray(v, np.float32).reshape(1, -1), (n, v.size)))


def prep_inputs(inputs):
    f = lambda a: np.ascontiguousarray(np.asarray(a, dtype=np.float32))
    g = {}
    mg, fg, fin = f(inputs["mix_norm_g"]), f(inputs["ffn_norm_g"]), f(inputs["final_norm_g"])
    g["norm_g"] = np.stack([_rep(mg[0]), _rep(fg[0]), _rep(mg[1]), _rep(fg[1]), _rep(fin)], axis=0)
    g["ev_w_in"] = f(inputs["ev_w_in"])[0]
    g["ev_w_out"] = f(inputs["ev_w_out"])[0]
    g["od_w_in"] = f(inputs["od_w_in"])[0]
    g["od_w_out"] = f(inputs["od_w_out"])[0]
    g["ffn_w_gate"] = f(inputs["ffn_w_gate"])
    g["ffn_w_up"] = f(inputs["ffn_w_up"])
    g["ffn_w_down"] = f(inputs["ffn_w_down"])
    cw = f(inputs["ev_conv_w"])[0]
    g["ev_conv_w"] = np.ascontiguousarray(cw.reshape(31, 4, 128).transpose(2, 1, 0))
    vecs = np.stack([f(inputs["ev_conv_b"])[0], f(inputs["ev_ln_g"])[0], f(inputs["ev_ln_b"])[0],
                     f(inputs["ev_pool_scale"])[0].reshape(512)], axis=0)
    g["ev_vec"] = np.ascontiguousarray(vecs.reshape(4, 4, 128).transpose(2, 1, 0))
    g["ev_pool_w"] = np.ascontiguousarray(f(inputs["ev_pool_w"])[0].transpose(1, 0, 2))
    g["od_fourier_w"] = np.ascontiguousarray(f(inputs["od_fourier_w"])[0].transpose(1, 0, 2))
    g["od_vln"] = np.stack([_rep(f(inputs["od_v_ln_g"])[0].reshape(512)),
                            _rep(f(inputs["od_v_ln_b"])[0].reshape(512))], axis=1)
    g["od_spatial_wT"] = np.ascontiguousarray(f(inputs["od_spatial_w"])[0].transpose(2, 0, 1))
    g["od_spatial_b"] = _rep(f(inputs["od_spatial_b"])[0].reshape(512))
    g.update(make_consts())
    return g


_NC_CACHE = {}


def kernel(**inputs):
    x = np.asarray(inputs["x"], dtype=np.float32)
    shared = prep_inputs(inputs)
    if "nc" not in _NC_CACHE:
        _NC_CACHE["nc"] = build()
    nc = _NC_CACHE["nc"]
    in_maps = []
    for b in range(8):
        m = dict(shared)
        m["x"] = np.ascontiguousarray(x[b])
        in_maps.append(m)
    res = run_bass_kernel_spmd(nc, in_maps, core_ids=list(range(8)))
    return np.stack([np.asarray(r["out"], dtype=np.float32) for r in res.results], axis=0)
```

```python
import numpy as np
from contextlib import ExitStack
import concourse.bass as bass
import concourse.mybir as mybir
from concourse.bass_utils import run_bass_kernel_spmd

F32 = mybir.dt.float32
BF16 = mybir.dt.bfloat16
AF = mybir.ActivationFunctionType
ALU = mybir.AluOpType

S = 4096
D = 1024
DFF = 2816
NT = S // 128
EPS = 1e-6
NJ = DFF // 128
ST = 1024
NST = S // ST


class Eng:
    def __init__(self, es, nc, eng, name):
        self.e = eng
        self.name = name
        self.sem = es.enter_context(nc.semaphore("es_" + name))
        self.cnt = 0
        self.seen = {}

    def wait(self, ev):
        if ev is None:
            return
        sem, val = ev
        k = id(sem)
        if self.seen.get(k, 0) >= val:
            return
        self.e.wait_ge(sem, val)
        self.seen[k] = val

    def sig(self, ins):
        self.cnt += 1
        ins.then_inc(self.sem, 1)
        return (self.sem, self.cnt)


class Dep:
    def __init__(self, sem=None):
        self.w = None
        self.r = {}
        self.sem = sem
        self.semval = 0


class K:
    def __init__(self, es, nc):
        self.nc = nc
        self.es = es
        self.pe = Eng(es, nc, nc.tensor, "pe")
        self.act = Eng(es, nc, nc.scalar, "act")
        self.dve = Eng(es, nc, nc.vector, "dve")
        self.pool = Eng(es, nc, nc.gpsimd, "pool")
        self.sp = Eng(es, nc, nc.sync, "sp")
        self.engs = [self.pe, self.act, self.dve, self.pool, self.sp]
        self.nsem = 0
        self.dma_deps = []
        self.ps = es.enter_context(nc.psum_tensor("ps", [128, 8, 512], F32))
        self.psd = [Dep() for _ in range(8)]
        self.bank_ptr = 0

    def dmadep(self):
        self.nsem += 1
        d = Dep(self.es.enter_context(self.nc.semaphore("ds%d" % self.nsem)))
        self.dma_deps.append(d)
        return d

    def _pre(self, eng, reads, writes):
        for d in reads:
            eng.wait(d.w)
        for d in writes:
            eng.wait(d.w)
            for ev in d.r.values():
                eng.wait(ev)

    def _post(self, ev, reads, writes):
        for d in reads:
            d.r[id(ev[0])] = ev
        for d in writes:
            d.w = ev
            d.r = {}

    def op(self, eng, fn, reads=(), writes=()):
        self._pre(eng, reads, writes)
        ins = fn(eng.e)
        ev = eng.sig(ins)
        self._post(ev, reads, writes)
        return ev

    def mm(self, mms, reads=(), writes=()):
        self._pre(self.pe, reads, writes)
        ins = None
        for (o, l, r, st, sp) in mms:
            ins = self.nc.tensor.matmul(o, lhsT=l, rhs=r, start=st, stop=sp)
        ev = self.pe.sig(ins)
        self._post(ev, reads, writes)
        return ev

    def dma(self, q, pairs, reads=(), writes=(), semdep=None):
        self._pre(q, reads, writes)
        for (o, i) in pairs:
            ins = q.e.dma_start(out=o, in_=i)
            semdep.semval += 16
            ins.then_inc(semdep.sem, 16)
        ev = (semdep.sem, semdep.semval)
        self._post(ev, reads, writes)
        return ev

    def banks(self, n=1):
        if n == 2 and self.bank_ptr % 2:
            self.bank_ptr += 1
        b = self.bank_ptr % 8
        self.bank_ptr += n
        return b

    def barrier(self):
        evs = []
        for e in self.engs:
            if e.cnt:
                evs.append((e.sem, e.cnt))
        for d in self.dma_deps:
            if d.semval:
                evs.append((d.sem, d.semval))
        for e in self.engs:
            for ev in evs:
                e.wait(ev)


_SB_UID = [0]


def sb(es, nc, name, shape, dt):
    _SB_UID[0] += 1
    return es.enter_context(nc.sbuf_tensor("sb%d_%s" % (_SB_UID[0], name), shape, dt))


def make_consts():
    c = {}
    c["ident"] = np.eye(128, dtype=np.float32)
    c["ones512"] = np.full((128, 128), 1.0 / 512.0, dtype=np.float32)
    wins = (2, 4, 8, 16)
    band = np.zeros((4, 5, 128, 128), dtype=np.float64)
    for g, w in enumerate(wins):
        half = w // 2
        for v, (tile, dt) in enumerate([(5, 0), (0, 0), (NT - 1, 0), (5, -1), (5, 1)]):
            for tl in range(128):
                t = tile * 128 + tl
                lo = min(max(t - half, 0), S - 1)
                hi = min(max(t + half - 1, 0), S - 1)
                cnt = hi - lo + 1
                for j in range(lo, hi + 1):
                    jt, jl = divmod(j, 128)
                    if jt == tile + dt:
                        band[g, v, jl, tl] += 1.0 / cnt
                if dt == 0:
                    band[g, v, tl, tl] -= 1.0
    c["band"] = np.ascontiguousarray(band.transpose(2, 0, 1, 3)).astype(np.float32)
    b = np.arange(32)
    ang = 2 * np.pi * np.outer(b, b) / 32.0
    Cr = np.cos(ang)
    Ci = -np.sin(ang)
    I4 = np.eye(4)
    FA = np.concatenate([np.kron(Cr, I4), np.kron(Ci, I4)], axis=1)
    FB = np.concatenate([np.kron(-Ci, I4), np.kron(Cr, I4)], axis=1)
    c["fab"] = np.stack([FA, FB], axis=1).astype(np.float32)
    a = np.arange(128)
    sp = (32 * a[None, :] + b[:, None])
    th = 2 * np.pi * (a[:, None, None] * sp[None, :, :]) / 4096.0
    c["gcs"] = np.stack([np.cos(th), np.sin(th)], axis=1).astype(np.float32)
    ch = np.arange(128)
    a2 = 2 * np.pi * np.outer(ch, ch) / 128.0
    c["cs128"] = np.stack([np.cos(a2), np.sin(a2)], axis=1).astype(np.float32)
    return c


CONST_SHAPES = {
    "ident": [128, 128], "ones512": [128, 128], "band": [128, 4, 5, 128], "fab": [128, 2, 256],
    "gcs": [128, 2, 32, 128], "cs128": [128, 2, 128],
}

IN_SHAPES = {
    "x": [S, D],
    "norm_g": [5, 128, D],
    "ev_w_in": [D, 1536], "ev_w_out": [D, D], "od_w_in": [D, 1536], "od_w_out": [D, D],
    "ffn_w_gate": [2, D, DFF], "ffn_w_up": [2, D, DFF], "ffn_w_down": [2, DFF, D],
    "ev_conv_w": [128, 4, 31],
    "ev_vec": [128, 4, 4],
    "ev_pool_w": [128, 4, 128],
    "od_fourier_w": [128, 4, 128],
    "od_vln": [128, 2, 512],
    "od_spatial_wT": [128, 4, 128],
    "od_spatial_b": [128, 512],
}
IN_SHAPES.update(CONST_SHAPES)


def build(stop_after=None, dumps=()):
    nc = bass.Bass("TRN2", target_bir_lowering=False)
    T = {}
    for name, shp in IN_SHAPES.items():
        T[name] = nc.dram_tensor(name, shp, F32, kind="ExternalInput").ap()
    out = nc.dram_tensor("out", [S, D], F32, kind="ExternalOutput").ap()
    xres = nc.dram_tensor("xres", [S, D], F32, kind="Internal").ap()
    dump_t = {}

    with ExitStack() as es:
        k = K(es, nc)
        pe, act, dve, pool, sp = k.pe, k.act, k.dve, k.pool, k.sp
        ps = k.ps
        psd = k.psd

        ident = sb(es, nc, "ident", [128, 128], BF16)
        ssb = sb(es, nc, "ssb", [128, 8], F32)
        tmp = sb(es, nc, "tmp", [128, 3, 512], F32)
        d_ident = k.dmadep()
        d_ss = [Dep() for _ in range(8)]
        d_tmp = [Dep() for _ in range(3)]
        d_xres = [Dep() for _ in range(NT)]
        cnt = {"xin": 0, "xout": 0, "hb": 0, "ss": 0, "tmp": 0}
        uid = [0]

        def rr(name, n):
            v = cnt[name] % n
            cnt[name] += 1
            return v

        k.dma(pool, [(ident[:], T["ident"])], writes=[d_ident], semdep=d_ident)

        io_sems = {"gbc": [k.dmadep(), k.dmadep()], "xin": [k.dmadep() for _ in range(6)],
                   "xout": [k.dmadep() for _ in range(2)]}

        class IO:
            def __init__(self, scope, nxin=3):
                self.nxin = nxin
                uid[0] += 1
                u = str(uid[0])
                self.gbc = sb(scope, nc, "gbc" + u, [128, 2, D], F32)
                self.xin = sb(scope, nc, "xin" + u, [128, nxin, D], F32)
                self.xout = sb(scope, nc, "xout" + u, [128, 2, D], F32)
                self.hb = sb(scope, nc, "hb" + u, [128, 3, D], BF16)
                self.junk = sb(scope, nc, "junk" + u, [128, D], BF16)
                self.d_gbc = io_sems["gbc"]
                self.d_xin = io_sems["xin"]
                self.d_xout = io_sems["xout"]
                self.d_hb = [Dep(), Dep(), Dep()]
                self.d_junk = Dep()

        def add_dump(name, ap_sb, shape, dt, deps):
            if name not in dumps:
                return
            t = nc.dram_tensor("dbg_" + name, shape, dt, kind="ExternalOutput").ap()
            dd = k.dmadep()
            k.dma(sp, [(t, ap_sb)], reads=deps, semdep=dd)
            dump_t[name] = dd

        def run_pipeline(n, stages):
            mx = max(sk for sk, _ in stages)
            for i in range(n + mx):
                for sk, fn in stages:
                    t = i - sk
                    if 0 <= t < n:
                        fn(t)

        def norm_stages(io, src_fn, src_deps, tiles, gidx, hT, hT_deps, col0=0):
            gbc, xin, hb, junk = io.gbc, io.xin, io.hb, io.junk
            d_gbc, d_xin, d_hb, d_junk = io.d_gbc, io.d_xin, io.d_hb, io.d_junk
            k.dma(sp, [(gbc[:, 0, :], T["norm_g"][gidx])], writes=[d_gbc[0]], semdep=d_gbc[0])
            stt = {}

            def s_load(i):
                t = tiles[i]
                xs = rr("xin", io.nxin)
                stt[i] = {"xs": xs}
                k.dma(sp, [(xin[:, xs, :], src_fn(t))], reads=[src_deps[t]] if src_deps else [],
                      writes=[d_xin[xs]], semdep=d_xin[xs])

            def s_stat(i):
                xs = stt[i]["xs"]
                s_ = rr("ss", 8)
                k.op(act, lambda e: e.activation(out=junk[:], in_=xin[:, xs, :], func=AF.Square,
                                                 accum_out=ssb[:, s_:s_ + 1]),
                     reads=[d_xin[xs]], writes=[d_junk, d_ss[s_]])
                k.op(act, lambda e: e.activation(out=ssb[:, s_:s_ + 1], in_=ssb[:, s_:s_ + 1], func=AF.Sqrt,
                                                 bias=float(D * EPS)),
                     reads=[d_ss[s_]], writes=[d_ss[s_]])
                k.op(dve, lambda e: e.reciprocal(out=ssb[:, s_:s_ + 1], in_=ssb[:, s_:s_ + 1]),
                     reads=[d_ss[s_]], writes=[d_ss[s_]])
                h_ = rr("hb", 3)
                stt[i]["h"] = h_
                k.op(dve, lambda e: e.scalar_tensor_tensor(out=hb[:, h_, :], in0=xin[:, xs, :],
                                                           scalar=ssb[:, s_:s_ + 1], in1=gbc[:, 0, :],
                                                           op0=ALU.mult, op1=ALU.mult),
                     reads=[d_xin[xs], d_ss[s_], d_gbc[0]], writes=[d_hb[h_]])

            def s_tr(i):
                h_ = stt[i]["h"]
                b = k.banks(2)
                psv = ps[:, b:b + 2, :].rearrange("p b (c n) -> p (b c) n", n=128)
                k.mm([(psv[:, kk, :], hb[:, h_, kk * 128:(kk + 1) * 128], ident[:], True, True)
                      for kk in range(8)],
                     reads=[d_hb[h_], d_ident], writes=[psd[b], psd[b + 1]])
                c0 = col0 + i * 128
                k.op(act, lambda e: e.activation(out=hT[:, :, c0:c0 + 128], in_=psv, func=AF.Copy,
                                                 scale=float(np.sqrt(D))),
                     reads=[psd[b], psd[b + 1]], writes=[hT_deps[i]])

            return [(0, s_load), (2, s_tr), (1, s_stat)]

        def norm_phase(io, src_fn, src_deps, tiles, gidx, hT, hT_deps, col0=0):
            run_pipeline(len(tiles), norm_stages(io, src_fn, src_deps, tiles, gidx, hT, hT_deps, col0))

        def proj_fm(w_sb, w_dep, col_chunks, hT, hT_deps_all, ntg, epilogue):
            for tg in range(ntg):
                for ci, cols in enumerate(col_chunks):
                    bl = []
                    for c0 in cols:
                        b = k.banks(1)
                        k.mm([(ps[:, b, :], w_sb[:, kk, c0:c0 + 128], hT[:, kk, tg * 512:(tg + 1) * 512],
                               kk == 0, kk == 7) for kk in range(8)],
                             reads=[w_dep] + hT_deps_all(tg), writes=[psd[b]])
                        bl.append(b)
                    epilogue(tg, ci, bl)

        def out_phase(io, yT, yT_dep, wo, wo_dep, src_fn, src_deps):
            xin, xout = io.xin, io.xout
            d_xin, d_xout = io.d_xin, io.d_xout
            stt = {}

            def s_load(t):
                xs = rr("xin", io.nxin)
                stt[t] = {"xs": xs}
                k.dma(sp, [(xin[:, xs, :], src_fn(t))], reads=[src_deps[t]] if src_deps else [],
                      writes=[d_xin[xs]], semdep=d_xin[xs])

            def s_mm(t):
                b = k.banks(2)
                stt[t]["b"] = b
                mms = []
                for nh in range(2):
                    for kk in range(8):
                        mms.append((ps[:, b + nh, :], yT[:, kk, t * 128:(t + 1) * 128],
                                    wo[:, kk, nh * 512:(nh + 1) * 512], kk == 0, kk == 7))
                k.mm(mms, reads=[yT_dep, wo_dep], writes=[psd[b], psd[b + 1]])

            def s_add(t):
                xs, b = stt[t]["xs"], stt[t]["b"]
                xo = rr("xout", 2)
                k.op(dve, lambda e: e.tensor_tensor(out=xout[:, xo, :], in0=xin[:, xs, :],
                                                    in1=ps[:, b:b + 2, :].rearrange("p b n -> p (b n)"),
                                                    op=ALU.add),
                     reads=[d_xin[xs], psd[b], psd[b + 1]], writes=[d_xout[xo]])
                k.dma(sp, [(xres[t * 128:(t + 1) * 128, :], xout[:, xo, :])], reads=[d_xout[xo]],
                      writes=[d_xres[t]], semdep=d_xout[xo])

            run_pipeline(NT, [(0, s_load), (1, s_mm), (2, s_add)])

        def ffn_phase(l, last):
            with ExitStack() as fs:
                io = IO(fs, nxin=6)
                gbc, xin, xout, junk = io.gbc, io.xin, io.xout, io.junk
                d_gbc, d_xin, d_xout, d_junk = io.d_gbc, io.d_xin, io.d_xout, io.d_junk
                if last:
                    k.dma(sp, [(gbc[:, 1, :], T["norm_g"][4])], writes=[d_gbc[1]], semdep=d_gbc[1])
                    k.op(dve, lambda e: e.tensor_scalar(out=gbc[:, 1, :], in0=gbc[:, 1, :], scalar1=float(np.sqrt(D)),
                                                        scalar2=None, op0=ALU.mult),
                         reads=[d_gbc[1]], writes=[d_gbc[1]])
                h2T = sb(fs, nc, "h2T%d" % l, [128, 2, 8, ST], BF16)
                gT = sb(fs, nc, "gT%d" % l, [128, NJ, ST], BF16)
                wd = sb(fs, nc, "wd%d" % l, [128, NJ, D], BF16)
                wgu = sb(fs, nc, "wgu%d" % l, [128, 3, 2, 8, 256], BF16)
                d_h2T = [[Dep() for _ in range(8)] for _ in range(2)]
                d_gT = [Dep() for _ in range(NJ)]
                d_wd = k.dmadep()
                d_wgu = [k.dmadep() for _ in range(3)]
                wdv = T["ffn_w_down"][l].rearrange("(j p) n -> p j n", p=128)
                wgv = T["ffn_w_gate"][l].rearrange("(k p) n -> p k n", p=128)
                wuv = T["ffn_w_up"][l].rearrange("(k p) n -> p k n", p=128)
                nslot = [0]
                xsrc = lambda t: xres[t * 128:(t + 1) * 128, :]

                def nstages(st):
                    tiles = list(range(st * 8, st * 8 + 8))
                    return norm_stages(io, xsrc, d_xres, tiles, 1 + 2 * l, h2T[:, st % 2], d_h2T[st % 2])

                def gu(st):
                    hT_ = h2T[:, st % 2]
                    dh = d_h2T[st % 2]
                    for c in range(NJ // 2):
                        sl = nslot[0] % 3
                        nslot[0] += 1
                        k.dma(pool, [(wgu[:, sl, 0, :, :], wgv[:, :, c * 256:(c + 1) * 256]),
                                     (wgu[:, sl, 1, :, :], wuv[:, :, c * 256:(c + 1) * 256])],
                              writes=[d_wgu[sl]], semdep=d_wgu[sl])
                        if st == 0 and c == 2:
                            k.dma(pool, [(wd[:, 0:11, :], wdv[:, 0:11, :]), (wd[:, 11:22, :], wdv[:, 11:22, :])],
                                  writes=[d_wd], semdep=d_wd)
                        for jj in range(2):
                            j = c * 2 + jj
                            for tg in range(ST // 512):
                                bg = k.banks(1)
                                k.mm([(ps[:, bg, :], wgu[:, sl, 0, kk, jj * 128:(jj + 1) * 128],
                                       hT_[:, kk, tg * 512:(tg + 1) * 512], kk == 0, kk == 7) for kk in range(8)],
                                     reads=[d_wgu[sl]] + dh[tg * 4:(tg + 1) * 4], writes=[psd[bg]])
                                bu = k.banks(1)
                                k.mm([(ps[:, bu, :], wgu[:, sl, 1, kk, jj * 128:(jj + 1) * 128],
                                       hT_[:, kk, tg * 512:(tg + 1) * 512], kk == 0, kk == 7) for kk in range(8)],
                                     reads=[d_wgu[sl]] + dh[tg * 4:(tg + 1) * 4], writes=[psd[bu]])
                                ts_ = rr("tmp", 3)
                                k.op(act, lambda e: e.activation(out=tmp[:, ts_, :], in_=ps[:, bg, :], func=AF.Silu),
                                     reads=[psd[bg]], writes=[d_tmp[ts_]])
                                k.op(dve, lambda e: e.tensor_tensor(out=gT[:, j, tg * 512:(tg + 1) * 512],
                                                                    in0=tmp[:, ts_, :], in1=ps[:, bu, :], op=ALU.mult),
                                     reads=[d_tmp[ts_], psd[bu]], writes=[d_gT[j]])

                def down_stages(st):
                    stt = {}

                    def s_load(ti):
                        t = st * 8 + ti
                        xs = rr("xin", io.nxin)
                        stt[ti] = {"xs": xs}
                        k.dma(sp, [(xin[:, xs, :], xres[t * 128:(t + 1) * 128, :])], reads=[d_xres[t]],
                              writes=[d_xin[xs]], semdep=d_xin[xs])

                    def s_mm(ti):
                        b = k.banks(2)
                        stt[ti]["b"] = b
                        mms = []
                        for nh in range(2):
                            for j in range(NJ):
                                mms.append((ps[:, b + nh, :], gT[:, j, ti * 128:(ti + 1) * 128],
                                            wd[:, j, nh * 512:(nh + 1) * 512], j == 0, j == NJ - 1))
                        k.mm(mms, reads=[d_wd] + d_gT, writes=[psd[b], psd[b + 1]])

                    def s_add(ti):
                        t = st * 8 + ti
                        xs, b = stt[ti]["xs"], stt[ti]["b"]
                        xo = rr("xout", 2)
                        k.op(dve, lambda e: e.tensor_tensor(out=xout[:, xo, :], in0=xin[:, xs, :],
                                                            in1=ps[:, b:b + 2, :].rearrange("p b n -> p (b n)"),
                                                            op=ALU.add),
                             reads=[d_xin[xs], psd[b], psd[b + 1]], writes=[d_xout[xo]])
                        if not last:
                            k.dma(sp, [(xres[t * 128:(t + 1) * 128, :], xout[:, xo, :])], reads=[d_xout[xo]],
                                  writes=[d_xres[t]], semdep=d_xout[xo])
                        else:
                            s_ = rr("ss", 8)
                            k.op(act, lambda e: e.activation(out=junk[:], in_=xout[:, xo, :], func=AF.Square,
                                                             accum_out=ssb[:, s_:s_ + 1]),
                                 reads=[d_xout[xo]], writes=[d_junk, d_ss[s_]])
                            k.op(act, lambda e: e.activation(out=ssb[:, s_:s_ + 1], in_=ssb[:, s_:s_ + 1],
                                                             func=AF.Sqrt, bias=float(D * EPS)),
                                 reads=[d_ss[s_]], writes=[d_ss[s_]])
                            k.op(dve, lambda e: e.reciprocal(out=ssb[:, s_:s_ + 1], in_=ssb[:, s_:s_ + 1]),
                                 reads=[d_ss[s_]], writes=[d_ss[s_]])
                            k.op(dve, lambda e: e.scalar_tensor_tensor(out=xout[:, xo, :], in0=xout[:, xo, :],
                                                                       scalar=ssb[:, s_:s_ + 1], in1=gbc[:, 1, :],
                                                                       op0=ALU.mult, op1=ALU.mult),
                                 reads=[d_xout[xo], d_ss[s_], d_gbc[1]], writes=[d_xout[xo]])
                            k.dma(sp, [(out[t * 128:(t + 1) * 128, :], xout[:, xo, :])], reads=[d_xout[xo]],
                                  semdep=d_xout[xo])

                    return [(0, s_load), (1, s_mm), (2, s_add)]

                run_pipeline(8, nstages(0))
                for st in range(NST):
                    gu(st)
                    stages = down_stages(st)
                    if st + 1 < NST:
                        ns = nstages(st + 1)
                        stages = [stages[0], ns[0], stages[1], ns[1], ns[2], stages[2]]
                    run_pipeline(8, stages)
                k.barrier()

        def mixer0():
            with ExitStack() as ms:
                bigA = sb(ms, nc, "bigA", [128, 8, S], BF16)
                uT = sb(ms, nc, "uT", [128, 4, S + 32], BF16)
                d_hT = [Dep() for _ in range(NT)]
                d_uT = [Dep() for _ in range(4)]
                d_yT = Dep()
                evec = sb(ms, nc, "evec", [128, 4, 4], F32)
                d_evec = k.dmadep()
                k.dma(sp, [(evec[:], T["ev_vec"])], writes=[d_evec], semdep=d_evec)
                for c in range(4):
                    k.op(pool, lambda e: e.memset(uT[:, c, 0:16], 0.0), writes=[d_uT[c]])
                    k.op(pool, lambda e: e.memset(uT[:, c, S + 16:S + 32], 0.0), writes=[d_uT[c]])
                src = lambda t: T["x"][t * 128:(t + 1) * 128, :]
                with ExitStack() as sn:
                    norm_phase(IO(sn), src, None, list(range(NT)), 0, bigA, d_hT)
                    k.barrier()
                if stop_after == "norm0":
                    add_dump("hT", bigA[:], [128, 8, S], BF16, d_hT)
                    return
                with ExitStack() as s1:
                    p_sb = sb(s1, nc, "p_sb", [128, NT, 512], BF16)
                    d_p = [Dep() for _ in range(NT)]
                    with ExitStack() as s2:
                        w_in = sb(s2, nc, "w_in", [128, 8, 1536], BF16)
                        d_w = k.dmadep()
                        wv = T["ev_w_in"].rearrange("(k p) n -> p k n", p=128)
                        k.dma(pool, [(w_in[:, :, 0:768], wv[:, :, 0:768]), (w_in[:, :, 768:1536], wv[:, :, 768:1536])],
                              writes=[d_w], semdep=d_w)

                        def epi_a(tg, c, bl):
                            ts_ = rr("tmp", 3)
                            k.op(act, lambda e: e.activation(out=tmp[:, ts_, :], in_=ps[:, bl[1], :], func=AF.Sigmoid),
                                 reads=[psd[bl[1]]], writes=[d_tmp[ts_]])
                            k.op(dve, lambda e: e.tensor_tensor(out=uT[:, c, 16 + tg * 512:16 + (tg + 1) * 512],
                                                                in0=tmp[:, ts_, :], in1=ps[:, bl[0], :], op=ALU.mult),
                                 reads=[d_tmp[ts_], psd[bl[0]]], writes=[d_uT[c]])

                        proj_fm(w_in, d_w, [[c * 128, 512 + c * 128] for c in range(4)], bigA,
                                lambda tg: d_hT[tg * 4:(tg + 1) * 4], 8, epi_a)
                        for t in range(NT):
                            b = k.banks(1)
                            k.mm([(ps[:, b, :], bigA[:, kk, t * 128:(t + 1) * 128], w_in[:, kk, 1024:1536],
                                   kk == 0, kk == 7) for kk in range(8)],
                                 reads=[d_w, d_hT[t]], writes=[psd[b]])
                            eng = act if t % 2 == 0 else dve
                            if eng is act:
                                k.op(act, lambda e: e.activation(out=p_sb[:, t, :], in_=ps[:, b, :], func=AF.Copy),
                                     reads=[psd[b]], writes=[d_p[t]])
                            else:
                                k.op(dve, lambda e: e.tensor_copy(out=p_sb[:, t, :], in_=ps[:, b, :]),
                                     reads=[psd[b]], writes=[d_p[t]])
                        k.barrier()
                    if stop_after == "proj0":
                        add_dump("uT", uT[:], [128, 4, S + 32], BF16, d_uT)
                        add_dump("p_sb", p_sb[:], [128, NT, 512], BF16, d_p)
                        return
                    with ExitStack() as s2:
                        band = sb(s2, nc, "band", [128, 4, 5, 128], BF16)
                        pw = sb(s2, nc, "pw", [128, 4, 128], BF16)
                        pooled = sb(s2, nc, "pooled", [128, 2, 512], BF16)
                        d_band = k.dmadep()
                        d_pw = k.dmadep()
                        d_pooled = [Dep(), Dep()]
                        k.dma(pool, [(band[:], T["band"])], writes=[d_band], semdep=d_band)
                        k.dma(pool, [(pw[:], T["ev_pool_w"])], writes=[d_pw], semdep=d_pw)
                        npl = 0
                        for tg in range(8):
                            for g in range(4):
                                b = k.banks(1)
                                mms = []
                                rd = {}
                                for ti in range(4):
                                    t = tg * 4 + ti
                                    srcs = []
                                    if t > 0:
                                        srcs.append((t - 1, 3))
                                    srcs.append((t, 1 if t == 0 else (2 if t == NT - 1 else 0)))
                                    if t < NT - 1:
                                        srcs.append((t + 1, 4))
                                    for si, (tt, v) in enumerate(srcs):
                                        mms.append((ps[:, b, ti * 128:(ti + 1) * 128],
                                                    p_sb[:, tt, g * 128:(g + 1) * 128], band[:, g, v, :],
                                                    si == 0, si == len(srcs) - 1))
                                        rd[tt] = d_p[tt]
                                k.mm(mms, reads=[d_band] + list(rd.values()), writes=[psd[b]])
                                pl = npl % 2
                                npl += 1
                                k.op(act, lambda e: e.activation(out=pooled[:, pl, :], in_=ps[:, b, :], func=AF.Copy),
                                     reads=[psd[b]], writes=[d_pooled[pl]])
                                b2 = k.banks(1)
                                k.mm([(ps[:, b2, :], pw[:, g, :], pooled[:, pl, :], True, True)],
                                     reads=[d_pw, d_pooled[pl]], writes=[psd[b2]])
                                k.op(dve, lambda e: e.tensor_scalar(out=bigA[:, 4 + g, tg * 512:(tg + 1) * 512],
                                                                    in0=ps[:, b2, :], scalar1=evec[:, g, 3:4],
                                                                    scalar2=None, op0=ALU.mult),
                                     reads=[psd[b2], d_evec] + d_hT, writes=[d_yT])
                        k.barrier()
                if stop_after == "pool0":
                    add_dump("yT", bigA[:], [128, 8, S], BF16, [d_yT])
                    return
                with ExitStack() as s1:
                    cw = sb(s1, nc, "cw", [128, 4, 31], F32)
                    identf = sb(s1, nc, "identf", [128, 128], F32)
                    ones = sb(s1, nc, "ones", [128, 128], BF16)
                    diag = sb(s1, nc, "diag", [128, 4, 31, 128], BF16)
                    vf = sb(s1, nc, "vf", [128, 2, 4, 512], F32)
                    vb = sb(s1, nc, "vb", [128, 2, 4, 512], BF16)
                    vq = sb(s1, nc, "vq", [128, 2, 4, 512], BF16)
                    st_sb = sb(s1, nc, "st_sb", [128, 2, 3, 512], F32)
                    t1 = sb(s1, nc, "t1", [128, 3, 512], F32)
                    d_cw = k.dmadep()
                    d_idf = k.dmadep()
                    d_ones = k.dmadep()
                    d_diag = Dep()
                    d_vf = [[Dep() for _ in range(4)] for _ in range(2)]
                    d_vb = [[Dep() for _ in range(4)] for _ in range(2)]
                    d_vq = [[Dep() for _ in range(4)] for _ in range(2)]
                    d_st = [[Dep() for _ in range(3)] for _ in range(2)]
                    d_t1 = [Dep() for _ in range(3)]
                    k.dma(sp, [(cw[:], T["ev_conv_w"])], writes=[d_cw], semdep=d_cw)
                    k.dma(sp, [(identf[:], T["ident"])], writes=[d_idf], semdep=d_idf)
                    k.dma(pool, [(ones[:], T["ones512"])], writes=[d_ones], semdep=d_ones)
                    for c in range(4):
                        for tap in range(31):
                            k.op(dve, lambda e: e.tensor_scalar(out=diag[:, c, tap, :], in0=identf[:],
                                                                scalar1=cw[:, c, tap:tap + 1], scalar2=None,
                                                                op0=ALU.mult),
                                 reads=[d_cw, d_idf], writes=[d_diag])
                    nt1 = 0
                    for tg in range(8):
                        r_ = tg % 2
                        for c in range(4):
                            b = k.banks(1)
                            k.mm([(ps[:, b, :], diag[:, c, tap, :],
                                   uT[:, c, 1 + tg * 512 + tap:1 + tg * 512 + tap + 512], tap == 0, tap == 30)
                                  for tap in range(31)],
                                 reads=[d_diag, d_uT[c]], writes=[psd[b]])
                            k.op(act, lambda e: e.activation(out=vf[:, r_, c, :], in_=ps[:, b, :], func=AF.Identity,
                                                             bias=evec[:, c, 0:1]),
                                 reads=[psd[b], d_evec], writes=[d_vf[r_][c]])
                            k.op(act, lambda e: e.activation(out=vq[:, r_, c, :], in_=ps[:, b, :], func=AF.Square,
                                                             bias=evec[:, c, 0:1]),
                                 reads=[psd[b], d_evec], writes=[d_vq[r_][c]])
                            k.op(pool, lambda e: e.tensor_copy(out=vb[:, r_, c, :], in_=vf[:, r_, c, :]),
                                 reads=[d_vf[r_][c]], writes=[d_vb[r_][c]])
                        bm = k.banks(1)
                        k.mm([(ps[:, bm, :], ones[:], vb[:, r_, c, :], c == 0, c == 3) for c in range(4)],
                             reads=[d_ones] + d_vb[r_], writes=[psd[bm]])
                        bq = k.banks(1)
                        k.mm([(ps[:, bq, :], ones[:], vq[:, r_, c, :], c == 0, c == 3) for c in range(4)],
                             reads=[d_ones] + d_vq[r_], writes=[psd[bq]])
                        k.op(act, lambda e: e.activation(out=st_sb[:, r_, 0, :], in_=ps[:, bm, :], func=AF.Copy),
                             reads=[psd[bm]], writes=[d_st[r_][0]])
                        k.op(act, lambda e: e.activation(out=st_sb[:, r_, 1, :], in_=ps[:, bm, :], func=AF.Square),
                             reads=[psd[bm]], writes=[d_st[r_][1]])
                        k.op(dve, lambda e: e.tensor_tensor(out=st_sb[:, r_, 1, :], in0=ps[:, bq, :],
                                                            in1=st_sb[:, r_, 1, :], op=ALU.subtract),
                             reads=[psd[bq], d_st[r_][1]], writes=[d_st[r_][1]])
                        k.op(act, lambda e: e.activation(out=st_sb[:, r_, 1, :], in_=st_sb[:, r_, 1, :], func=AF.Sqrt,
                                                         bias=float(EPS)),
                             reads=[d_st[r_][1]], writes=[d_st[r_][1]])
                        k.op(dve, lambda e: e.reciprocal(out=st_sb[:, r_, 1, :], in_=st_sb[:, r_, 1, :]),
                             reads=[d_st[r_][1]], writes=[d_st[r_][1]])
                        for c in range(4):
                            ti_ = nt1 % 3
                            nt1 += 1
                            k.op(pool, lambda e: e.tensor_tensor(out=t1[:, ti_, :], in0=vf[:, r_, c, :],
                                                                 in1=st_sb[:, r_, 0, :], op=ALU.subtract),
                                 reads=[d_vf[r_][c], d_st[r_][0]], writes=[d_t1[ti_]])
                            k.op(dve, lambda e: e.tensor_tensor(out=t1[:, ti_, :], in0=t1[:, ti_, :],
                                                                in1=st_sb[:, r_, 1, :], op=ALU.mult),
                                 reads=[d_t1[ti_], d_st[r_][1]], writes=[d_t1[ti_]])
                            k.op(act, lambda e: e.activation(out=bigA[:, c, tg * 512:(tg + 1) * 512], in_=t1[:, ti_, :],
                                                             func=AF.Silu, scale=evec[:, c, 1:2], bias=evec[:, c, 2:3]),
                                 reads=[d_t1[ti_], d_evec, d_yT], writes=[d_yT])
                    k.barrier()
                if stop_after == "conv0":
                    add_dump("yT", bigA[:], [128, 8, S], BF16, [d_yT])
                    return
                with ExitStack() as s1:
                    wo = sb(s1, nc, "wo", [128, 8, D], BF16)
                    d_wo = k.dmadep()
                    k.dma(pool, [(wo[:], T["ev_w_out"].rearrange("(k p) n -> p k n", p=128))],
                          writes=[d_wo], semdep=d_wo)
                    out_phase(IO(s1), bigA, d_yT, wo, d_wo, src, None)
                    k.barrier()
                k.barrier()

        def mixer1():
            with ExitStack() as ms:
                bigA = sb(ms, nc, "bigA1", [128, 8, S], BF16)
                cT = sb(ms, nc, "cT", [128, 4, S], BF16)
                d_hT = [Dep() for _ in range(NT)]
                d_cT = [Dep() for _ in range(4)]
                d_yT = Dep()
                src = lambda t: xres[t * 128:(t + 1) * 128, :]
                with ExitStack() as sn:
                    norm_phase(IO(sn), src, d_xres, list(range(NT)), 2, bigA, d_hT)
                    k.barrier()
                with ExitStack() as s1:
                    uT = sb(s1, nc, "uT1", [128, 4, S], BF16)
                    d_uT = [Dep() for _ in range(4)]
                    w_in = sb(s1, nc, "w_in1", [128, 8, 1536], BF16)
                    d_w = k.dmadep()
                    wv = T["od_w_in"].rearrange("(k p) n -> p k n", p=128)
                    k.dma(pool, [(w_in[:, :, 0:768], wv[:, :, 0:768]), (w_in[:, :, 768:1536], wv[:, :, 768:1536])],
                          writes=[d_w], semdep=d_w)
                    vln = sb(s1, nc, "vln", [128, 2, 512], F32)
                    sbb = sb(s1, nc, "sbb", [128, 512], F32)
                    swT = sb(s1, nc, "swT", [128, 4, 128], BF16)
                    d_vln = k.dmadep()
                    d_sbb = k.dmadep()
                    d_swT = k.dmadep()
                    k.dma(sp, [(vln[:], T["od_vln"])], writes=[d_vln], semdep=d_vln)
                    k.dma(sp, [(sbb[:], T["od_spatial_b"])], writes=[d_sbb], semdep=d_sbb)
                    k.dma(pool, [(swT[:], T["od_spatial_wT"])], writes=[d_swT], semdep=d_swT)

                    nev = [0]

                    def epi_c(tg, c, bl):
                        nev[0] += 1
                        if nev[0] % 2:
                            k.op(act, lambda e: e.activation(out=cT[:, c, tg * 512:(tg + 1) * 512], in_=ps[:, bl[0], :],
                                                             func=AF.Copy),
                                 reads=[psd[bl[0]]], writes=[d_cT[c]])
                        else:
                            k.op(dve, lambda e: e.tensor_copy(out=cT[:, c, tg * 512:(tg + 1) * 512], in_=ps[:, bl[0], :]),
                                 reads=[psd[bl[0]]], writes=[d_cT[c]])

                    proj_fm(w_in, d_w, [[c * 128] for c in range(4)], bigA, lambda tg: d_hT[tg * 4:(tg + 1) * 4], 8, epi_c)

                    def epi_u(tg, c, bl):
                        k.op(act, lambda e: e.activation(out=uT[:, c, tg * 512:(tg + 1) * 512], in_=ps[:, bl[0], :],
                                                         func=AF.Gelu),
                             reads=[psd[bl[0]]], writes=[d_uT[c]])

                    proj_fm(w_in, d_w, [[512 + c * 128] for c in range(4)], bigA, lambda tg: d_hT[tg * 4:(tg + 1) * 4],
                            8, epi_u)
                    if stop_after == "proj1":
                        add_dump("cT", cT[:], [128, 4, S], BF16, d_cT)
                        add_dump("uT", uT[:], [128, 4, S], BF16, d_uT)
                        return
                    vt = sb(s1, nc, "vt", [128, 2, 512], F32)
                    vn = sb(s1, nc, "vn", [128, 2, 512], BF16)
                    bst = sb(s1, nc, "bst", [128, 2, 4, 6], F32)
                    mv = sb(s1, nc, "mv", [128, 2, 4, 2], F32)
                    d_vt = [Dep(), Dep()]
                    d_vn = [Dep(), Dep()]
                    d_bst = [Dep(), Dep()]
                    d_mv = [Dep(), Dep()]
                    for t in range(NT):
                        r_ = t % 2
                        b = k.banks(1)
                        k.mm([(ps[:, b, :], bigA[:, kk, t * 128:(t + 1) * 128], w_in[:, kk, 1024:1536],
                               kk == 0, kk == 7) for kk in range(8)],
                             reads=[d_w, d_hT[t]], writes=[psd[b]])
                        k.op(act, lambda e: e.activation(out=vt[:, r_, :], in_=ps[:, b, :], func=AF.Gelu),
                             reads=[psd[b]], writes=[d_vt[r_]])
                        for h in range(4):
                            k.op(dve, lambda e: e.bn_stats(out=bst[:, r_, h, :], in_=vt[:, r_, h * 128:(h + 1) * 128]),
                                 reads=[d_vt[r_]], writes=[d_bst[r_]])
                        for h in range(4):
                            k.op(dve, lambda e: e.bn_aggr(out=mv[:, r_, h, :], in_=bst[:, r_, h, :]),
                                 reads=[d_bst[r_]], writes=[d_mv[r_]])
                        k.op(act, lambda e: e.activation(out=mv[:, r_, :, 1], in_=mv[:, r_, :, 1], func=AF.Sqrt, bias=float(EPS)),
                             reads=[d_mv[r_]], writes=[d_mv[r_]])
                        k.op(dve, lambda e: e.reciprocal(out=mv[:, r_, :, 1], in_=mv[:, r_, :, 1]),
                             reads=[d_mv[r_]], writes=[d_mv[r_]])
                        for h in range(4):
                            k.op(dve, lambda e: e.tensor_scalar(out=vt[:, r_, h * 128:(h + 1) * 128],
                                                                in0=vt[:, r_, h * 128:(h + 1) * 128],
                                                                scalar1=mv[:, r_, h, 0:1], scalar2=mv[:, r_, h, 1:2],
                                                                op0=ALU.subtract, op1=ALU.mult),
                                 reads=[d_vt[r_], d_mv[r_]], writes=[d_vt[r_]])
                        k.op(pool, lambda e: e.tensor_tensor(out=vt[:, r_, :], in0=vt[:, r_, :], in1=vln[:, 0, :],
                                                             op=ALU.mult),
                             reads=[d_vt[r_], d_vln], writes=[d_vt[r_]])
                        k.op(pool, lambda e: e.tensor_tensor(out=vn[:, r_, :], in0=vt[:, r_, :], in1=vln[:, 1, :],
                                                             op=ALU.add),
                             reads=[d_vt[r_], d_vln], writes=[d_vn[r_]])
                        b2 = k.banks(1)
                        k.mm([(ps[:, b2, h * 128:(h + 1) * 128], vn[:, r_, h * 128:(h + 1) * 128], swT[:, h, :],
                               True, True) for h in range(4)],
                             reads=[d_vn[r_], d_swT], writes=[psd[b2]])
                        ts_ = rr("tmp", 3)
                        k.op(dve, lambda e: e.tensor_tensor(out=tmp[:, ts_, :], in0=ps[:, b2, :], in1=sbb[:],
                                                            op=ALU.add),
                             reads=[psd[b2], d_sbb], writes=[d_tmp[ts_]])
                        k.op(dve, lambda e: e.tensor_tensor(
                            out=bigA[:, 4:8, t * 128:(t + 1) * 128],
                            in0=tmp[:, ts_, :].rearrange("p (h q) -> p h q", h=4),
                            in1=uT[:, :, t * 128:(t + 1) * 128], op=ALU.mult),
                             reads=[d_tmp[ts_]] + d_uT + d_hT, writes=[d_yT])
                    k.barrier()
                if stop_after == "sgu1":
                    add_dump("yT", bigA[:], [128, 8, S], BF16, [d_yT])
                    return
                with ExitStack() as s1:
                    fab = sb(s1, nc, "fab", [128, 2, 256], BF16)
                    gcs = sb(s1, nc, "gcs", [128, 2, 32, 128], BF16)
                    cs128 = sb(s1, nc, "cs128", [128, 2, 128], BF16)
                    fw = sb(s1, nc, "fw", [128, 4, 128], BF16)
                    m12 = sb(s1, nc, "m12", [128, 4, 2, 128], BF16)
                    Q = sb(s1, nc, "Q", [128, 2, 32, 32, 4], BF16)
                    Q2 = sb(s1, nc, "Q2", [128, 2, 32, 128], BF16)
                    Y = sb(s1, nc, "Y", [128, 2, 32, 128], BF16)
                    d_fab = k.dmadep()
                    d_gcs = k.dmadep()
                    d_cs = k.dmadep()
                    d_fw = k.dmadep()
                    d_m12 = Dep()
                    d_Q = Dep()
                    d_Q2 = Dep()
                    d_Y = Dep()
                    k.dma(pool, [(fab[:], T["fab"])], writes=[d_fab], semdep=d_fab)
                    k.dma(pool, [(gcs[:, 0], T["gcs"][:, 0]), (gcs[:, 1], T["gcs"][:, 1])], writes=[d_gcs], semdep=d_gcs)
                    k.dma(pool, [(cs128[:], T["cs128"])], writes=[d_cs], semdep=d_cs)
                    k.dma(pool, [(fw[:], T["od_fourier_w"])], writes=[d_fw], semdep=d_fw)
                    sc = 1.0 / np.sqrt(4096.0 * 128.0)
                    for h in range(4):
                        b = k.banks(1)
                        k.mm([(ps[:, b, 0:128], cs128[:, 0, :], fw[:, h, :], True, True),
                              (ps[:, b, 128:256], cs128[:, 1, :], fw[:, h, :], True, True)],
                             reads=[d_cs, d_fw], writes=[psd[b]])
                        k.op(act, lambda e: e.activation(out=m12[:, h, 0, :], in_=ps[:, b, 0:128], func=AF.Copy,
                                                         scale=float(sc)),
                             reads=[psd[b]], writes=[d_m12])
                        k.op(act, lambda e: e.activation(out=m12[:, h, 1, :], in_=ps[:, b, 128:256], func=AF.Copy,
                                                         scale=float(-sc)),
                             reads=[psd[b]], writes=[d_m12])
                    nq = [0]

                    def evac(out_ap, in_ap, reads, writes):
                        nq[0] += 1
                        if nq[0] % 2:
                            k.op(act, lambda e: e.activation(out=out_ap, in_=in_ap, func=AF.Copy), reads=reads,
                                 writes=writes)
                        else:
                            k.op(dve, lambda e: e.tensor_copy(out=out_ap, in_=in_ap), reads=reads, writes=writes)

                    for h in range(4):
                        for bb in range(0, 32, 2):
                            b = k.banks(1)
                            k.mm([(ps[:, b, i * 256:(i + 1) * 256], cT[:, h, (bb + i) * 128:(bb + i + 1) * 128],
                                   m12[:, h, :, :].rearrange("p r c -> p (r c)"), True, True) for i in range(2)],
                                 reads=[d_cT[h], d_m12], writes=[psd[b]])
                            for i in range(2):
                                evac(Q[:, :, :, bb + i, :],
                                     ps[:, b, i * 256:(i + 1) * 256].rearrange("p (r c j) -> p r c j", r=2, j=4),
                                     [psd[b]], [d_Q])
                        for ri in range(2):
                            for cg0 in range(0, 32, 4):
                                b = k.banks(1)
                                k.mm([(ps[:, b, i * 128:(i + 1) * 128],
                                       Q[:, ri, cg0 + i, :, :].rearrange("p b j -> p (b j)"), ident[:], True, True)
                                      for i in range(4)],
                                     reads=[d_Q, d_ident], writes=[psd[b]])
                                evac(Q2[:, ri, cg0:cg0 + 4, :].rearrange("p c a -> p (c a)"), ps[:, b, :],
                                     [psd[b]], [d_Q2])
                        for cg0 in range(0, 32, 2):
                            b = k.banks(1)
                            mms = []
                            for i in range(2):
                                mms.append((ps[:, b, i * 256:(i + 1) * 256], Q2[:, 0, cg0 + i, :], fab[:, 0, :], True, False))
                                mms.append((ps[:, b, i * 256:(i + 1) * 256], Q2[:, 1, cg0 + i, :], fab[:, 1, :], False, True))
                            k.mm(mms, reads=[d_Q2, d_fab], writes=[psd[b]])
                            for i in range(2):
                                cg = cg0 + i
                                evac(Y[:, :, :, cg * 4:(cg + 1) * 4],
                                     ps[:, b, i * 256:(i + 1) * 256].rearrange("p (r b j) -> p r b j", r=2, j=4),
                                     [psd[b]], [d_Y])
                        yv = bigA[:, h, :].rearrange("p (a b) -> p b a", b=32)
                        for b0 in range(0, 32, 4):
                            b = k.banks(1)
                            mms = []
                            for i in range(4):
                                mms.append((ps[:, b, i * 128:(i + 1) * 128], Y[:, 0, b0 + i, :], gcs[:, 0, b0 + i, :], True, False))
                                mms.append((ps[:, b, i * 128:(i + 1) * 128], Y[:, 1, b0 + i, :], gcs[:, 1, b0 + i, :], False, True))
                            k.mm(mms, reads=[d_Y, d_gcs], writes=[psd[b]])
                            evac(yv[:, b0:b0 + 4, :], ps[:, b, :].rearrange("p (b a) -> p b a", b=4),
                                 [psd[b]] + d_hT, [d_yT])
                    k.barrier()
                if stop_after == "four1":
                    add_dump("yT", bigA[:], [128, 8, S], BF16, [d_yT])
                    return
                with ExitStack() as s1:
                    wo = sb(s1, nc, "wo1", [128, 8, D], BF16)
                    d_wo = k.dmadep()
                    k.dma(pool, [(wo[:], T["od_w_out"].rearrange("(k p) n -> p k n", p=128))],
                          writes=[d_wo], semdep=d_wo)
                    out_phase(IO(s1), bigA, d_yT, wo, d_wo, src, d_xres)
                    k.barrier()
                k.barrier()

        stages = ["mixer0", "ffn0", "mixer1", "ffn1"]
        m0_stops = ("norm0", "proj0", "pool0", "conv0")
        m1_stops = ("proj1", "sgu1", "four1")
        done = False
        mixer0()
        if stop_after in m0_stops:
            done = True
        if not done and stop_after == "mixer0":
            done = True
        if not done:
            ffn_phase(0, last=False)
            if stop_after == "ffn0":
                done = True
        if not done:
            mixer1()
            if stop_after in m1_stops or stop_after == "mixer1":
                done = True
        if not done:
            ffn_phase(1, last=True)
        if done and stop_after in ("mixer0", "ffn0", "mixer1"):
            k.barrier()
            dd = k.dmadep()
            k.dma(sp, [(out[:, :], xres[:, :])], semdep=dd)
        k.barrier()
    return nc


def _rep(v, n=128):
    return np.ascontiguousarray(np.broadcast_to(np.asarray(v, np.float32).reshape(1, -1), (n, v.size)))


def prep_inputs(inputs):
    f = lambda a: np.ascontiguousarray(np.asarray(a, dtype=np.float32))
    g = {}
    mg, fg, fin = f(inputs["mix_norm_g"]), f(inputs["ffn_norm_g"]), f(inputs["final_norm_g"])
    g["norm_g"] = np.stack([_rep(mg[0]), _rep(fg[0]), _rep(mg[1]), _rep(fg[1]), _rep(fin)], axis=0)
    g["ev_w_in"] = f(inputs["ev_w_in"])[0]
    g["ev_w_out"] = f(inputs["ev_w_out"])[0]
    g["od_w_in"] = f(inputs["od_w_in"])[0]
    g["od_w_out"] = f(inputs["od_w_out"])[0]
    g["ffn_w_gate"] = f(inputs["ffn_w_gate"])
    g["ffn_w_up"] = f(inputs["ffn_w_up"])
    g["ffn_w_down"] = f(inputs["ffn_w_down"])
    cw = f(inputs["ev_conv_w"])[0]
    g["ev_conv_w"] = np.ascontiguousarray(cw.reshape(31, 4, 128).transpose(2, 1, 0))
    vecs = np.stack([f(inputs["ev_conv_b"])[0], f(inputs["ev_ln_g"])[0], f(inputs["ev_ln_b"])[0],
                     f(inputs["ev_pool_scale"])[0].reshape(512)], axis=0)
    g["ev_vec"] = np.ascontiguousarray(vecs.reshape(4, 4, 128).transpose(2, 1, 0))
    g["ev_pool_w"] = np.ascontiguousarray(f(inputs["ev_pool_w"])[0].transpose(1, 0, 2))
    g["od_fourier_w"] = np.ascontiguousarray(f(inputs["od_fourier_w"])[0].transpose(1, 0, 2))
    g["od_vln"] = np.stack([_rep(f(inputs["od_v_ln_g"])[0].reshape(512)),
                            _rep(f(inputs["od_v_ln_b"])[0].reshape(512))], axis=1)
    g["od_spatial_wT"] = np.ascontiguousarray(f(inputs["od_spatial_w"])[0].transpose(2, 0, 1))
    g["od_spatial_b"] = _rep(f(inputs["od_spatial_b"])[0].reshape(512))
    g.update(make_consts())
    return g


_NC_CACHE = {}


def kernel(**inputs):
    x = np.asarray(inputs["x"], dtype=np.float32)
    shared = prep_inputs(inputs)
    if "nc" not in _NC_CACHE:
        _NC_CACHE["nc"] = build()
    nc = _NC_CACHE["nc"]
    in_maps = []
    for b in range(8):
        m = dict(shared)
        m["x"] = np.ascontiguousarray(x[b])
        in_maps.append(m)
    res = run_bass_kernel_spmd(nc, in_maps, core_ids=list(range(8)))
    return np.stack([np.asarray(r["out"], dtype=np.float32) for r in res.results], axis=0)
```

```python
import numpy as np
from contextlib import ExitStack
import concourse.bass as bass
import concourse.mybir as mybir
from concourse.bass_utils import run_bass_kernel_spmd

F32 = mybir.dt.float32
BF16 = mybir.dt.bfloat16
AF = mybir.ActivationFunctionType
ALU = mybir.AluOpType

S = 4096
D = 1024
DFF = 2816
NT = S // 128
EPS = 1e-6
NJ = DFF // 128
ST = 1024
NST = S // ST


class Eng:
    def __init__(self, es, nc, eng, name):
        self.e = eng
        self.name = name
        self.sem = es.enter_context(nc.semaphore("es_" + name))
        self.cnt = 0
        self.seen = {}

    def wait(self, ev):
        if ev is None:
            return
        sem, val = ev
        k = id(sem)
        if self.seen.get(k, 0) >= val:
            return
        self.e.wait_ge(sem, val)
        self.seen[k] = val

    def sig(self, ins):
        self.cnt += 1
        ins.then_inc(self.sem, 1)
        return (self.sem, self.cnt)


class Dep:
    def __init__(self, sem=None):
        self.w = None
        self.r = {}
        self.sem = sem
        self.semval = 0


class K:
    def __init__(self, es, nc):
        self.nc = nc
        self.es = es
        self.pe = Eng(es, nc, nc.tensor, "pe")
        self.act = Eng(es, nc, nc.scalar, "act")
        self.dve = Eng(es, nc, nc.vector, "dve")
        self.pool = Eng(es, nc, nc.gpsimd, "pool")
        self.sp = Eng(es, nc, nc.sync, "sp")
        self.engs = [self.pe, self.act, self.dve, self.pool, self.sp]
        self.nsem = 0
        self.dma_deps = []
        self.ps = es.enter_context(nc.psum_tensor("ps", [128, 8, 512], F32))
        self.psd = [Dep() for _ in range(8)]
        self.bank_ptr = 0

    def dmadep(self):
        self.nsem += 1
        d = Dep(self.es.enter_context(self.nc.semaphore("ds%d" % self.nsem)))
        self.dma_deps.append(d)
        return d

    def _pre(self, eng, reads, writes):
        for d in reads:
            eng.wait(d.w)
        for d in writes:
            eng.wait(d.w)
            for ev in d.r.values():
                eng.wait(ev)

    def _post(self, ev, reads, writes):
        for d in reads:
            d.r[id(ev[0])] = ev
        for d in writes:
            d.w = ev
            d.r = {}

    def op(self, eng, fn, reads=(), writes=()):
        self._pre(eng, reads, writes)
        ins = fn(eng.e)
        ev = eng.sig(ins)
        self._post(ev, reads, writes)
        return ev

    def mm(self, mms, reads=(), writes=()):
        self._pre(self.pe, reads, writes)
        ins = None
        for (o, l, r, st, sp) in mms:
            ins = self.nc.tensor.matmul(o, lhsT=l, rhs=r, start=st, stop=sp)
        ev = self.pe.sig(ins)
        self._post(ev, reads, writes)
        return ev

    def dma(self, q, pairs, reads=(), writes=(), semdep=None):
        self._pre(q, reads, writes)
        for (o, i) in pairs:
            ins = q.e.dma_start(out=o, in_=i)
            semdep.semval += 16
            ins.then_inc(semdep.sem, 16)
        ev = (semdep.sem, semdep.semval)
        self._post(ev, reads, writes)
        return ev

    def banks(self, n=1):
        if n == 2 and self.bank_ptr % 2:
            self.bank_ptr += 1
        b = self.bank_ptr % 8
        self.bank_ptr += n
        return b

    def barrier(self):
        evs = []
        for e in self.engs:
            if e.cnt:
                evs.append((e.sem, e.cnt))
        for d in self.dma_deps:
            if d.semval:
                evs.append((d.sem, d.semval))
        for e in self.engs:
            for ev in evs:
                e.wait(ev)


_SB_UID = [0]


def sb(es, nc, name, shape, dt):
    _SB_UID[0] += 1
    return es.enter_context(nc.sbuf_tensor("sb%d_%s" % (_SB_UID[0], name), shape, dt))


def make_consts():
    c = {}
    c["ident"] = np.eye(128, dtype=np.float32)
    c["ones512"] = np.full((128, 128), 1.0 / 512.0, dtype=np.float32)
    wins = (2, 4, 8, 16)
    band = np.zeros((4, 5, 128, 128), dtype=np.float64)
    for g, w in enumerate(wins):
        half = w // 2
        for v, (tile, dt) in enumerate([(5, 0), (0, 0), (NT - 1, 0), (5, -1), (5, 1)]):
            for tl in range(128):
                t = tile * 128 + tl
                lo = min(max(t - half, 0), S - 1)
                hi = min(max(t + half - 1, 0), S - 1)
                cnt = hi - lo + 1
                for j in range(lo, hi + 1):
                    jt, jl = divmod(j, 128)
                    if jt == tile + dt:
                        band[g, v, jl, tl] += 1.0 / cnt
                if dt == 0:
                    band[g, v, tl, tl] -= 1.0
    c["band"] = np.ascontiguousarray(band.transpose(2, 0, 1, 3)).astype(np.float32)
    b = np.arange(32)
    ang = 2 * np.pi * np.outer(b, b) / 32.0
    Cr = np.cos(ang)
    Ci = -np.sin(ang)
    I4 = np.eye(4)
    FA = np.concatenate([np.kron(Cr, I4), np.kron(Ci, I4)], axis=1)
    FB = np.concatenate([np.kron(-Ci, I4), np.kron(Cr, I4)], axis=1)
    c["fab"] = np.stack([FA, FB], axis=1).astype(np.float32)
    a = np.arange(128)
    sp = (32 * a[None, :] + b[:, None])
    th = 2 * np.pi * (a[:, None, None] * sp[None, :, :]) / 4096.0
    c["gcs"] = np.stack([np.cos(th), np.sin(th)], axis=1).astype(np.float32)
    ch = np.arange(128)
    a2 = 2 * np.pi * np.outer(ch, ch) / 128.0
    c["cs128"] = np.stack([np.cos(a2), np.sin(a2)], axis=1).astype(np.float32)
    return c


CONST_SHAPES = {
    "ident": [128, 128], "ones512": [128, 128], "band": [128, 4, 5, 128], "fab": [128, 2, 256],
    "gcs": [128, 2, 32, 128], "cs128": [128, 2, 128],
}

IN_SHAPES = {
    "x": [S, D],
    "norm_g": [5, 128, D],
    "ev_w_in": [D, 1536], "ev_w_out": [D, D], "od_w_in": [D, 1536], "od_w_out": [D, D],
    "ffn_w_gate": [2, D, DFF], "ffn_w_up": [2, D, DFF], "ffn_w_down": [2, DFF, D],
    "ev_conv_w": [128, 4, 31],
    "ev_vec": [128, 4, 4],
    "ev_pool_w": [128, 4, 128],
    "od_fourier_w": [128, 4, 128],
    "od_vln": [128, 2, 512],
    "od_spatial_wT": [128, 4, 128],
    "od_spatial_b": [128, 512],
}
IN_SHAPES.update(CONST_SHAPES)


def build(stop_after=None, dumps=()):
    nc = bass.Bass("TRN2", target_bir_lowering=False)
    T = {}
    for name, shp in IN_SHAPES.items():
        T[name] = nc.dram_tensor(name, shp, F32, kind="ExternalInput").ap()
    out = nc.dram_tensor("out", [S, D], F32, kind="ExternalOutput").ap()
    xres = nc.dram_tensor("xres", [S, D], F32, kind="Internal").ap()
    dump_t = {}

    with ExitStack() as es:
        k = K(es, nc)
        pe, act, dve, pool, sp = k.pe, k.act, k.dve, k.pool, k.sp
        ps = k.ps
        psd = k.psd

        ident = sb(es, nc, "ident", [128, 128], BF16)
        ssb = sb(es, nc, "ssb", [128, 8], F32)
        tmp = sb(es, nc, "tmp", [128, 3, 512], F32)
        d_ident = k.dmadep()
        d_ss = [Dep() for _ in range(8)]
        d_tmp = [Dep() for _ in range(3)]
        d_xres = [Dep() for _ in range(NT)]
        cnt = {"xin": 0, "xout": 0, "hb": 0, "ss": 0, "tmp": 0}
        uid = [0]

        def rr(name, n):
            v = cnt[name] % n
            cnt[name] += 1
            return v

        k.dma(pool, [(ident[:], T["ident"])], writes=[d_ident], semdep=d_ident)

        io_sems = {"gbc": [k.dmadep(), k.dmadep()], "xin": [k.dmadep() for _ in range(6)],
                   "xout": [k.dmadep() for _ in range(2)]}

        class IO:
            def __init__(self, scope, nxin=6):
                self.nxin = nxin
                uid[0] += 1
                u = str(uid[0])
                self.gbc = sb(scope, nc, "gbc" + u, [128, 2, D], F32)
                self.xin = sb(scope, nc, "xin" + u, [128, nxin, D], F32)
                self.xout = sb(scope, nc, "xout" + u, [128, 2, D], F32)
                self.hb = sb(scope, nc, "hb" + u, [128, 3, D], BF16)
                self.junk = sb(scope, nc, "junk" + u, [128, D], BF16)
                self.d_gbc = io_sems["gbc"]
                self.d_xin = io_sems["xin"]
                self.d_xout = io_sems["xout"]
                self.d_hb = [Dep(), Dep(), Dep()]
                self.d_junk = Dep()

        def add_dump(name, ap_sb, shape, dt, deps):
            if name not in dumps:
                return
            t = nc.dram_tensor("dbg_" + name, shape, dt, kind="ExternalOutput").ap()
            dd = k.dmadep()
            k.dma(sp, [(t, ap_sb)], reads=deps, semdep=dd)
            dump_t[name] = dd

        def run_pipeline(n, stages):
            mx = max(sk for sk, _ in stages)
            for i in range(n + mx):
                for sk, fn in stages:
                    t = i - sk
                    if 0 <= t < n:
                        fn(t)

        def norm_stages(io, src_fn, src_deps, tiles, gidx, hT, hT_deps, col0=0):
            gbc, xin, hb, junk = io.gbc, io.xin, io.hb, io.junk
            d_gbc, d_xin, d_hb, d_junk = io.d_gbc, io.d_xin, io.d_hb, io.d_junk
            k.dma(sp, [(gbc[:, 0, :], T["norm_g"][gidx])], writes=[d_gbc[0]], semdep=d_gbc[0])
            stt = {}

            def s_load(i):
                t = tiles[i]
                xs = rr("xin", io.nxin)
                stt[i] = {"xs": xs}
                k.dma(sp, [(xin[:, xs, :], src_fn(t))], reads=[src_deps[t]] if src_deps else [],
                      writes=[d_xin[xs]], semdep=d_xin[xs])

            def s_stat(i):
                xs = stt[i]["xs"]
                s_ = rr("ss", 8)
                k.op(act, lambda e: e.activation(out=junk[:], in_=xin[:, xs, :], func=AF.Square,
                                                 accum_out=ssb[:, s_:s_ + 1]),
                     reads=[d_xin[xs]], writes=[d_junk, d_ss[s_]])
                k.op(act, lambda e: e.activation(out=ssb[:, s_:s_ + 1], in_=ssb[:, s_:s_ + 1], func=AF.Sqrt,
                                                 bias=float(D * EPS)),
                     reads=[d_ss[s_]], writes=[d_ss[s_]])
                k.op(dve, lambda e: e.reciprocal(out=ssb[:, s_:s_ + 1], in_=ssb[:, s_:s_ + 1]),
                     reads=[d_ss[s_]], writes=[d_ss[s_]])
                h_ = rr("hb", 3)
                stt[i]["h"] = h_
                k.op(dve, lambda e: e.scalar_tensor_tensor(out=hb[:, h_, :], in0=xin[:, xs, :],
                                                           scalar=ssb[:, s_:s_ + 1], in1=gbc[:, 0, :],
                                                           op0=ALU.mult, op1=ALU.mult),
                     reads=[d_xin[xs], d_ss[s_], d_gbc[0]], writes=[d_hb[h_]])

            def s_tr(i):
                h_ = stt[i]["h"]
                b = k.banks(2)
                psv = ps[:, b:b + 2, :].rearrange("p b (c n) -> p (b c) n", n=128)
                k.mm([(psv[:, kk, :], hb[:, h_, kk * 128:(kk + 1) * 128], ident[:], True, True)
                      for kk in range(8)],
                     reads=[d_hb[h_], d_ident], writes=[psd[b], psd[b + 1]])
                c0 = col0 + i * 128
                k.op(act, lambda e: e.activation(out=hT[:, :, c0:c0 + 128], in_=psv, func=AF.Copy,
                                                 scale=float(np.sqrt(D))),
                     reads=[psd[b], psd[b + 1]], writes=[hT_deps[i]])

            return [(0, s_load), (4, s_tr), (2, s_stat)]

        def norm_phase(io, src_fn, src_deps, tiles, gidx, hT, hT_deps, col0=0):
            run_pipeline(len(tiles), norm_stages(io, src_fn, src_deps, tiles, gidx, hT, hT_deps, col0))

        def proj_fm(w_sb, w_dep, col_chunks, hT, hT_deps_all, ntg, epilogue):
            for tg in range(ntg):
                for ci, cols in enumerate(col_chunks):
                    bl = []
                    for c0 in cols:
                        b = k.banks(1)
                        k.mm([(ps[:, b, :], w_sb[:, kk, c0:c0 + 128], hT[:, kk, tg * 512:(tg + 1) * 512],
                               kk == 0, kk == 7) for kk in range(8)],
                             reads=[w_dep] + hT_deps_all(tg), writes=[psd[b]])
                        bl.append(b)
                    epilogue(tg, ci, bl)

        def out_phase(io, yT, yT_dep, wo, wo_dep, src_fn, src_deps):
            xin, xout = io.xin, io.xout
            d_xin, d_xout = io.d_xin, io.d_xout
            stt = {}

            def s_load(t):
                xs = rr("xin", io.nxin)
                stt[t] = {"xs": xs}
                k.dma(sp, [(xin[:, xs, :], src_fn(t))], reads=[src_deps[t]] if src_deps else [],
                      writes=[d_xin[xs]], semdep=d_xin[xs])

            def s_mm(t):
                b = k.banks(2)
                stt[t]["b"] = b
                mms = []
                for nh in range(2):
                    for kk in range(8):
                        mms.append((ps[:, b + nh, :], yT[:, kk, t * 128:(t + 1) * 128],
                                    wo[:, kk, nh * 512:(nh + 1) * 512], kk == 0, kk == 7))
                k.mm(mms, reads=list(yT_dep) + [wo_dep], writes=[psd[b], psd[b + 1]])

            def s_add(t):
                xs, b = stt[t]["xs"], stt[t]["b"]
                xo = rr("xout", 2)
                k.op(dve, lambda e: e.tensor_tensor(out=xout[:, xo, :], in0=xin[:, xs, :],
                                                    in1=ps[:, b:b + 2, :].rearrange("p b n -> p (b n)"),
                                                    op=ALU.add),
                     reads=[d_xin[xs], psd[b], psd[b + 1]], writes=[d_xout[xo]])
                k.dma(sp, [(xres[t * 128:(t + 1) * 128, :], xout[:, xo, :])], reads=[d_xout[xo]],
                      writes=[d_xres[t]], semdep=d_xout[xo])

            run_pipeline(NT, [(0, s_load), (1, s_mm), (2, s_add)])

        def ffn_phase(l, last):
            with ExitStack() as fs:
                io = IO(fs, nxin=6)
                gbc, xin, xout, junk = io.gbc, io.xin, io.xout, io.junk
                d_gbc, d_xin, d_xout, d_junk = io.d_gbc, io.d_xin, io.d_xout, io.d_junk
                if last:
                    k.dma(sp, [(gbc[:, 1, :], T["norm_g"][4])], writes=[d_gbc[1]], semdep=d_gbc[1])
                    k.op(dve, lambda e: e.tensor_scalar(out=gbc[:, 1, :], in0=gbc[:, 1, :], scalar1=float(np.sqrt(D)),
                                                        scalar2=None, op0=ALU.mult),
                         reads=[d_gbc[1]], writes=[d_gbc[1]])
                h2T = sb(fs, nc, "h2T%d" % l, [128, 2, 8, ST], BF16)
                gT = sb(fs, nc, "gT%d" % l, [128, NJ, ST], BF16)
                wd = sb(fs, nc, "wd%d" % l, [128, NJ, D], BF16)
                wgu = sb(fs, nc, "wgu%d" % l, [128, 3, 2, 8, 256], BF16)
                d_h2T = [[Dep() for _ in range(8)] for _ in range(2)]
                d_gT = [Dep() for _ in range(NJ)]
                d_wd = k.dmadep()
                d_wgu = [k.dmadep() for _ in range(3)]
                wdv = T["ffn_w_down"][l].rearrange("(j p) n -> p j n", p=128)
                wgv = T["ffn_w_gate"][l].rearrange("(k p) n -> p k n", p=128)
                wuv = T["ffn_w_up"][l].rearrange("(k p) n -> p k n", p=128)
                nslot = [0]
                xsrc = lambda t: xres[t * 128:(t + 1) * 128, :]

                def nstages(st):
                    tiles = list(range(st * 8, st * 8 + 8))
                    return norm_stages(io, xsrc, d_xres, tiles, 1 + 2 * l, h2T[:, st % 2], d_h2T[st % 2])

                def gu(st):
                    hT_ = h2T[:, st % 2]
                    dh = d_h2T[st % 2]
                    for c in range(NJ // 2):
                        sl = nslot[0] % 3
                        nslot[0] += 1
                        k.dma(pool, [(wgu[:, sl, 0, :, :], wgv[:, :, c * 256:(c + 1) * 256]),
                                     (wgu[:, sl, 1, :, :], wuv[:, :, c * 256:(c + 1) * 256])],
                              writes=[d_wgu[sl]], semdep=d_wgu[sl])
                        if st == 0 and c == 2:
                            k.dma(pool, [(wd[:, 0:11, :], wdv[:, 0:11, :]), (wd[:, 11:22, :], wdv[:, 11:22, :])],
                                  writes=[d_wd], semdep=d_wd)
                        for jj in range(2):
                            j = c * 2 + jj
                            for tg in range(ST // 512):
                                bg = k.banks(1)
                                k.mm([(ps[:, bg, :], wgu[:, sl, 0, kk, jj * 128:(jj + 1) * 128],
                                       hT_[:, kk, tg * 512:(tg + 1) * 512], kk == 0, kk == 7) for kk in range(8)],
                                     reads=[d_wgu[sl]] + dh[tg * 4:(tg + 1) * 4], writes=[psd[bg]])
                                bu = k.banks(1)
                                k.mm([(ps[:, bu, :], wgu[:, sl, 1, kk, jj * 128:(jj + 1) * 128],
                                       hT_[:, kk, tg * 512:(tg + 1) * 512], kk == 0, kk == 7) for kk in range(8)],
                                     reads=[d_wgu[sl]] + dh[tg * 4:(tg + 1) * 4], writes=[psd[bu]])
                                ts_ = rr("tmp", 3)
                                k.op(act, lambda e: e.activation(out=tmp[:, ts_, :], in_=ps[:, bg, :], func=AF.Silu),
                                     reads=[psd[bg]], writes=[d_tmp[ts_]])
                                k.op(dve, lambda e: e.tensor_tensor(out=gT[:, j, tg * 512:(tg + 1) * 512],
                                                                    in0=tmp[:, ts_, :], in1=ps[:, bu, :], op=ALU.mult),
                                     reads=[d_tmp[ts_], psd[bu]], writes=[d_gT[j]])

                def down_stages(st):
                    stt = {}

                    def s_load(ti):
                        t = st * 8 + ti
                        xs = rr("xin", io.nxin)
                        stt[ti] = {"xs": xs}
                        k.dma(sp, [(xin[:, xs, :], xres[t * 128:(t + 1) * 128, :])], reads=[d_xres[t]],
                              writes=[d_xin[xs]], semdep=d_xin[xs])

                    def s_mm(ti):
                        b = k.banks(2)
                        stt[ti]["b"] = b
                        mms = []
                        for nh in range(2):
                            for j in range(NJ):
                                mms.append((ps[:, b + nh, :], gT[:, j, ti * 128:(ti + 1) * 128],
                                            wd[:, j, nh * 512:(nh + 1) * 512], j == 0, j == NJ - 1))
                        k.mm(mms, reads=[d_wd] + d_gT, writes=[psd[b], psd[b + 1]])

                    def s_add(ti):
                        t = st * 8 + ti
                        xs, b = stt[ti]["xs"], stt[ti]["b"]
                        xo = rr("xout", 2)
                        k.op(dve, lambda e: e.tensor_tensor(out=xout[:, xo, :], in0=xin[:, xs, :],
                                                            in1=ps[:, b:b + 2, :].rearrange("p b n -> p (b n)"),
                                                            op=ALU.add),
                             reads=[d_xin[xs], psd[b], psd[b + 1]], writes=[d_xout[xo]])
                        if not last:
                            k.dma(sp, [(xres[t * 128:(t + 1) * 128, :], xout[:, xo, :])], reads=[d_xout[xo]],
                                  writes=[d_xres[t]], semdep=d_xout[xo])
                        else:
                            s_ = rr("ss", 8)
                            k.op(act, lambda e: e.activation(out=junk[:], in_=xout[:, xo, :], func=AF.Square,
                                                             accum_out=ssb[:, s_:s_ + 1]),
                                 reads=[d_xout[xo]], writes=[d_junk, d_ss[s_]])
                            k.op(act, lambda e: e.activation(out=ssb[:, s_:s_ + 1], in_=ssb[:, s_:s_ + 1],
                                                             func=AF.Sqrt, bias=float(D * EPS)),
                                 reads=[d_ss[s_]], writes=[d_ss[s_]])
                            k.op(dve, lambda e: e.reciprocal(out=ssb[:, s_:s_ + 1], in_=ssb[:, s_:s_ + 1]),
                                 reads=[d_ss[s_]], writes=[d_ss[s_]])
                            k.op(dve, lambda e: e.scalar_tensor_tensor(out=xout[:, xo, :], in0=xout[:, xo, :],
                                                                       scalar=ssb[:, s_:s_ + 1], in1=gbc[:, 1, :],
                                                                       op0=ALU.mult, op1=ALU.mult),
                                 reads=[d_xout[xo], d_ss[s_], d_gbc[1]], writes=[d_xout[xo]])
                            k.dma(sp, [(out[t * 128:(t + 1) * 128, :], xout[:, xo, :])], reads=[d_xout[xo]],
                                  semdep=d_xout[xo])

                    return [(0, s_load), (1, s_mm), (2, s_add)]

                run_pipeline(8, nstages(0))
                for st in range(NST):
                    gu(st)
                    stages = down_stages(st)
                    if st + 1 < NST:
                        ns = nstages(st + 1)
                        stages = [stages[0], ns[0], stages[1], ns[1], ns[2], stages[2]]
                    run_pipeline(8, stages)
                k.barrier()

        def mixer0():
            with ExitStack() as ms:
                bigA = sb(ms, nc, "bigA", [128, 8, S], BF16)
                uT = sb(ms, nc, "uT", [128, 4, S + 32], BF16)
                d_hT = [Dep() for _ in range(NT)]
                d_uT = [Dep() for _ in range(4)]
                d_yT = Dep()
                evec = sb(ms, nc, "evec", [128, 4, 4], F32)
                d_evec = k.dmadep()
                k.dma(sp, [(evec[:], T["ev_vec"])], writes=[d_evec], semdep=d_evec)
                for c in range(4):
                    k.op(pool, lambda e: e.memset(uT[:, c, 0:16], 0.0), writes=[d_uT[c]])
                    k.op(pool, lambda e: e.memset(uT[:, c, S + 16:S + 32], 0.0), writes=[d_uT[c]])
                src = lambda t: T["x"][t * 128:(t + 1) * 128, :]
                with ExitStack() as sn:
                    norm_phase(IO(sn), src, None, list(range(NT)), 0, bigA, d_hT)
                    k.barrier()
                if stop_after == "norm0":
                    add_dump("hT", bigA[:], [128, 8, S], BF16, d_hT)
                    return
                with ExitStack() as s1:
                    p_sb = sb(s1, nc, "p_sb", [128, NT, 512], BF16)
                    d_p = [Dep() for _ in range(NT)]
                    with ExitStack() as s2:
                        w_in = sb(s2, nc, "w_in", [128, 8, 1536], BF16)
                        d_w = k.dmadep()
                        wv = T["ev_w_in"].rearrange("(k p) n -> p k n", p=128)
                        k.dma(pool, [(w_in[:, :, 0:768], wv[:, :, 0:768]), (w_in[:, :, 768:1536], wv[:, :, 768:1536])],
                              writes=[d_w], semdep=d_w)

                        def epi_a(tg, c, bl):
                            ts_ = rr("tmp", 3)
                            k.op(act, lambda e: e.activation(out=tmp[:, ts_, :], in_=ps[:, bl[1], :], func=AF.Sigmoid),
                                 reads=[psd[bl[1]]], writes=[d_tmp[ts_]])
                            k.op(dve, lambda e: e.tensor_tensor(out=uT[:, c, 16 + tg * 512:16 + (tg + 1) * 512],
                                                                in0=tmp[:, ts_, :], in1=ps[:, bl[0], :], op=ALU.mult),
                                 reads=[d_tmp[ts_], psd[bl[0]]], writes=[d_uT[c]])

                        proj_fm(w_in, d_w, [[c * 128, 512 + c * 128] for c in range(4)], bigA,
                                lambda tg: d_hT[tg * 4:(tg + 1) * 4], 8, epi_a)
                        for t in range(NT):
                            b = k.banks(1)
                            k.mm([(ps[:, b, :], bigA[:, kk, t * 128:(t + 1) * 128], w_in[:, kk, 1024:1536],
                                   kk == 0, kk == 7) for kk in range(8)],
                                 reads=[d_w, d_hT[t]], writes=[psd[b]])
                            eng = act if t % 2 == 0 else dve
                            if eng is act:
                                k.op(act, lambda e: e.activation(out=p_sb[:, t, :], in_=ps[:, b, :], func=AF.Copy),
                                     reads=[psd[b]], writes=[d_p[t]])
                            else:
                                k.op(dve, lambda e: e.tensor_copy(out=p_sb[:, t, :], in_=ps[:, b, :]),
                                     reads=[psd[b]], writes=[d_p[t]])
                        k.barrier()
                    if stop_after == "proj0":
                        add_dump("uT", uT[:], [128, 4, S + 32], BF16, d_uT)
                        add_dump("p_sb", p_sb[:], [128, NT, 512], BF16, d_p)
                        return
                    with ExitStack() as s2:
                        band = sb(s2, nc, "band", [128, 4, 5, 128], BF16)
                        pw = sb(s2, nc, "pw", [128, 4, 128], BF16)
                        pooled = sb(s2, nc, "pooled", [128, 2, 512], BF16)
                        d_band = k.dmadep()
                        d_pw = k.dmadep()
                        d_pooled = [Dep(), Dep()]
                        k.dma(pool, [(band[:], T["band"])], writes=[d_band], semdep=d_band)
                        k.dma(pool, [(pw[:], T["ev_pool_w"])], writes=[d_pw], semdep=d_pw)
                        npl = 0
                        for tg in range(8):
                            for g in range(4):
                                b = k.banks(1)
                                mms = []
                                rd = {}
                                for ti in range(4):
                                    t = tg * 4 + ti
                                    srcs = []
                                    if t > 0:
                                        srcs.append((t - 1, 3))
                                    srcs.append((t, 1 if t == 0 else (2 if t == NT - 1 else 0)))
                                    if t < NT - 1:
                                        srcs.append((t + 1, 4))
                                    for si, (tt, v) in enumerate(srcs):
                                        mms.append((ps[:, b, ti * 128:(ti + 1) * 128],
                                                    p_sb[:, tt, g * 128:(g + 1) * 128], band[:, g, v, :],
                                                    si == 0, si == len(srcs) - 1))
                                        rd[tt] = d_p[tt]
                                k.mm(mms, reads=[d_band] + list(rd.values()), writes=[psd[b]])
                                pl = npl % 2
                                npl += 1
                                k.op(act, lambda e: e.activation(out=pooled[:, pl, :], in_=ps[:, b, :], func=AF.Copy),
                                     reads=[psd[b]], writes=[d_pooled[pl]])
                                b2 = k.banks(1)
                                k.mm([(ps[:, b2, :], pw[:, g, :], pooled[:, pl, :], True, True)],
                                     reads=[d_pw, d_pooled[pl]], writes=[psd[b2]])
                                k.op(dve, lambda e: e.tensor_scalar(out=bigA[:, 4 + g, tg * 512:(tg + 1) * 512],
                                                                    in0=ps[:, b2, :], scalar1=evec[:, g, 3:4],
                                                                    scalar2=None, op0=ALU.mult),
                                     reads=[psd[b2], d_evec] + d_hT, writes=[d_yT])
                        k.barrier()
                if stop_after == "pool0":
                    add_dump("yT", bigA[:], [128, 8, S], BF16, [d_yT])
                    return
                with ExitStack() as s1:
                    cw = sb(s1, nc, "cw", [128, 4, 31], F32)
                    identf = sb(s1, nc, "identf", [128, 128], F32)
                    ones = sb(s1, nc, "ones", [128, 128], BF16)
                    diag = sb(s1, nc, "diag", [128, 4, 31, 128], BF16)
                    vf = sb(s1, nc, "vf", [128, 2, 4, 512], F32)
                    vb = sb(s1, nc, "vb", [128, 2, 4, 512], BF16)
                    vq = sb(s1, nc, "vq", [128, 2, 4, 512], BF16)
                    st_sb = sb(s1, nc, "st_sb", [128, 2, 3, 512], F32)
                    t1 = sb(s1, nc, "t1", [128, 3, 512], F32)
                    d_cw = k.dmadep()
                    d_idf = k.dmadep()
                    d_ones = k.dmadep()
                    d_diag = Dep()
                    d_vf = [[Dep() for _ in range(4)] for _ in range(2)]
                    d_vb = [[Dep() for _ in range(4)] for _ in range(2)]
                    d_vq = [[Dep() for _ in range(4)] for _ in range(2)]
                    d_st = [[Dep() for _ in range(3)] for _ in range(2)]
                    d_t1 = [Dep() for _ in range(3)]
                    k.dma(sp, [(cw[:], T["ev_conv_w"])], writes=[d_cw], semdep=d_cw)
                    k.dma(sp, [(identf[:], T["ident"])], writes=[d_idf], semdep=d_idf)
                    k.dma(pool, [(ones[:], T["ones512"])], writes=[d_ones], semdep=d_ones)
                    for c in range(4):
                        for tap in range(31):
                            k.op(dve, lambda e: e.tensor_scalar(out=diag[:, c, tap, :], in0=identf[:],
                                                                scalar1=cw[:, c, tap:tap + 1], scalar2=None,
                                                                op0=ALU.mult),
                                 reads=[d_cw, d_idf], writes=[d_diag])
                    nt1 = 0
                    for tg in range(8):
                        r_ = tg % 2
                        for c in range(4):
                            b = k.banks(1)
                            k.mm([(ps[:, b, :], diag[:, c, tap, :],
                                   uT[:, c, 1 + tg * 512 + tap:1 + tg * 512 + tap + 512], tap == 0, tap == 30)
                                  for tap in range(31)],
                                 reads=[d_diag, d_uT[c]], writes=[psd[b]])
                            k.op(act, lambda e: e.activation(out=vf[:, r_, c, :], in_=ps[:, b, :], func=AF.Identity,
                                                             bias=evec[:, c, 0:1]),
                                 reads=[psd[b], d_evec], writes=[d_vf[r_][c]])
                            k.op(act, lambda e: e.activation(out=vq[:, r_, c, :], in_=ps[:, b, :], func=AF.Square,
                                                             bias=evec[:, c, 0:1]),
                                 reads=[psd[b], d_evec], writes=[d_vq[r_][c]])
                            k.op(pool, lambda e: e.tensor_copy(out=vb[:, r_, c, :], in_=vf[:, r_, c, :]),
                                 reads=[d_vf[r_][c]], writes=[d_vb[r_][c]])
                        bm = k.banks(1)
                        k.mm([(ps[:, bm, :], ones[:], vb[:, r_, c, :], c == 0, c == 3) for c in range(4)],
                             reads=[d_ones] + d_vb[r_], writes=[psd[bm]])
                        bq = k.banks(1)
                        k.mm([(ps[:, bq, :], ones[:], vq[:, r_, c, :], c == 0, c == 3) for c in range(4)],
                             reads=[d_ones] + d_vq[r_], writes=[psd[bq]])
                        k.op(act, lambda e: e.activation(out=st_sb[:, r_, 0, :], in_=ps[:, bm, :], func=AF.Copy),
                             reads=[psd[bm]], writes=[d_st[r_][0]])
                        k.op(act, lambda e: e.activation(out=st_sb[:, r_, 1, :], in_=ps[:, bm, :], func=AF.Square),
                             reads=[psd[bm]], writes=[d_st[r_][1]])
                        k.op(dve, lambda e: e.tensor_tensor(out=st_sb[:, r_, 1, :], in0=ps[:, bq, :],
                                                            in1=st_sb[:, r_, 1, :], op=ALU.subtract),
                             reads=[psd[bq], d_st[r_][1]], writes=[d_st[r_][1]])
                        k.op(act, lambda e: e.activation(out=st_sb[:, r_, 1, :], in_=st_sb[:, r_, 1, :], func=AF.Sqrt,
                                                         bias=float(EPS)),
                             reads=[d_st[r_][1]], writes=[d_st[r_][1]])
                        k.op(dve, lambda e: e.reciprocal(out=st_sb[:, r_, 1, :], in_=st_sb[:, r_, 1, :]),
                             reads=[d_st[r_][1]], writes=[d_st[r_][1]])
                        for c in range(4):
                            ti_ = nt1 % 3
                            nt1 += 1
                            k.op(pool, lambda e: e.tensor_tensor(out=t1[:, ti_, :], in0=vf[:, r_, c, :],
                                                                 in1=st_sb[:, r_, 0, :], op=ALU.subtract),
                                 reads=[d_vf[r_][c], d_st[r_][0]], writes=[d_t1[ti_]])
                            k.op(dve, lambda e: e.tensor_tensor(out=t1[:, ti_, :], in0=t1[:, ti_, :],
                                                                in1=st_sb[:, r_, 1, :], op=ALU.mult),
                                 reads=[d_t1[ti_], d_st[r_][1]], writes=[d_t1[ti_]])
                            k.op(act, lambda e: e.activation(out=bigA[:, c, tg * 512:(tg + 1) * 512], in_=t1[:, ti_, :],
                                                             func=AF.Silu, scale=evec[:, c, 1:2], bias=evec[:, c, 2:3]),
                                 reads=[d_t1[ti_], d_evec, d_yT], writes=[d_yT])
                    k.barrier()
                if stop_after == "conv0":
                    add_dump("yT", bigA[:], [128, 8, S], BF16, [d_yT])
                    return
                with ExitStack() as s1:
                    wo = sb(s1, nc, "wo", [128, 8, D], BF16)
                    d_wo = k.dmadep()
                    k.dma(pool, [(wo[:], T["ev_w_out"].rearrange("(k p) n -> p k n", p=128))],
                          writes=[d_wo], semdep=d_wo)
                    out_phase(IO(s1), bigA, [d_yT], wo, d_wo, src, None)
                    k.barrier()
                k.barrier()

        def mixer1():
            with ExitStack() as ms:
                bigA = sb(ms, nc, "bigA1", [128, 8, S], BF16)
                cT = sb(ms, nc, "cT", [128, 4, S], BF16)
                d_hT = [Dep() for _ in range(NT)]
                d_cT = [Dep() for _ in range(4)]
                d_yTv = [Dep() for _ in range(NT)]
                d_yTf = [Dep() for _ in range(32)]
                src = lambda t: xres[t * 128:(t + 1) * 128, :]
                with ExitStack() as sn:
                    norm_phase(IO(sn), src, d_xres, list(range(NT)), 2, bigA, d_hT)
                    k.barrier()
                with ExitStack() as s1:
                    uT = sb(s1, nc, "uT1", [128, 4, S], BF16)
                    d_uT = [Dep() for _ in range(4)]
                    w_in = sb(s1, nc, "w_in1", [128, 8, 1536], BF16)
                    d_w = k.dmadep()
                    wv = T["od_w_in"].rearrange("(k p) n -> p k n", p=128)
                    k.dma(pool, [(w_in[:, :, 0:768], wv[:, :, 0:768]), (w_in[:, :, 768:1536], wv[:, :, 768:1536])],
                          writes=[d_w], semdep=d_w)
                    vln = sb(s1, nc, "vln", [128, 2, 512], F32)
                    sbb = sb(s1, nc, "sbb", [128, 512], F32)
                    swT = sb(s1, nc, "swT", [128, 4, 128], BF16)
                    d_vln = k.dmadep()
                    d_sbb = k.dmadep()
                    d_swT = k.dmadep()
                    k.dma(sp, [(vln[:], T["od_vln"])], writes=[d_vln], semdep=d_vln)
                    k.dma(sp, [(sbb[:], T["od_spatial_b"])], writes=[d_sbb], semdep=d_sbb)
                    k.dma(pool, [(swT[:], T["od_spatial_wT"])], writes=[d_swT], semdep=d_swT)

                    nev = [0]

                    def epi_c(tg, c, bl):
                        nev[0] += 1
                        if nev[0] % 2:
                            k.op(act, lambda e: e.activation(out=cT[:, c, tg * 512:(tg + 1) * 512], in_=ps[:, bl[0], :],
                                                             func=AF.Copy),
                                 reads=[psd[bl[0]]], writes=[d_cT[c]])
                        else:
                            k.op(dve, lambda e: e.tensor_copy(out=cT[:, c, tg * 512:(tg + 1) * 512], in_=ps[:, bl[0], :]),
                                 reads=[psd[bl[0]]], writes=[d_cT[c]])

                    proj_fm(w_in, d_w, [[c * 128] for c in range(4)], bigA, lambda tg: d_hT[tg * 4:(tg + 1) * 4], 8, epi_c)

                    def epi_u(tg, c, bl):
                        k.op(act, lambda e: e.activation(out=uT[:, c, tg * 512:(tg + 1) * 512], in_=ps[:, bl[0], :],
                                                         func=AF.Gelu),
                             reads=[psd[bl[0]]], writes=[d_uT[c]])

                    proj_fm(w_in, d_w, [[512 + c * 128] for c in range(4)], bigA, lambda tg: d_hT[tg * 4:(tg + 1) * 4],
                            8, epi_u)
                    if stop_after == "proj1":
                        add_dump("cT", cT[:], [128, 4, S], BF16, d_cT)
                        add_dump("uT", uT[:], [128, 4, S], BF16, d_uT)
                        return
                    NV = 4
                    vt = sb(s1, nc, "vt", [128, NV, 512], F32)
                    vn = sb(s1, nc, "vn", [128, 3, 512], BF16)
                    bst = sb(s1, nc, "bst", [128, NV, 4, 6], F32)
                    mv = sb(s1, nc, "mv", [128, NV, 4, 2], F32)
                    d_vt = [Dep() for _ in range(NV)]
                    d_vn = [Dep() for _ in range(3)]
                    d_bst = [Dep() for _ in range(NV)]
                    d_mv = [Dep() for _ in range(NV)]
                    vst = {}

                    def v0(t):
                        r_ = t % NV
                        b = k.banks(1)
                        k.mm([(ps[:, b, :], bigA[:, kk, t * 128:(t + 1) * 128], w_in[:, kk, 1024:1536],
                               kk == 0, kk == 7) for kk in range(8)],
                             reads=[d_w, d_hT[t]], writes=[psd[b]])
                        k.op(act, lambda e: e.activation(out=vt[:, r_, :], in_=ps[:, b, :], func=AF.Gelu),
                             reads=[psd[b]], writes=[d_vt[r_]])

                    def v1a(t):
                        r_ = t % NV
                        for h in range(4):
                            k.op(dve, lambda e: e.bn_stats(out=bst[:, r_, h, :], in_=vt[:, r_, h * 128:(h + 1) * 128]),
                                 reads=[d_vt[r_]], writes=[d_bst[r_]])
                        for h in range(4):
                            k.op(dve, lambda e: e.bn_aggr(out=mv[:, r_, h, :], in_=bst[:, r_, h, :]),
                                 reads=[d_bst[r_]], writes=[d_mv[r_]])
                        k.op(act, lambda e: e.activation(out=mv[:, r_, :, 1], in_=mv[:, r_, :, 1], func=AF.Sqrt,
                                                         bias=float(EPS)),
                             reads=[d_mv[r_]], writes=[d_mv[r_]])

                    def v1b(t):
                        r_ = t % NV
                        n_ = t % 3
                        k.op(dve, lambda e: e.reciprocal(out=mv[:, r_, :, 1], in_=mv[:, r_, :, 1]),
                             reads=[d_mv[r_]], writes=[d_mv[r_]])
                        for h in range(4):
                            k.op(dve, lambda e: e.tensor_scalar(out=vt[:, r_, h * 128:(h + 1) * 128],
                                                                in0=vt[:, r_, h * 128:(h + 1) * 128],
                                                                scalar1=mv[:, r_, h, 0:1], scalar2=mv[:, r_, h, 1:2],
                                                                op0=ALU.subtract, op1=ALU.mult),
                                 reads=[d_vt[r_], d_mv[r_]], writes=[d_vt[r_]])
                        k.op(pool, lambda e: e.tensor_tensor(out=vt[:, r_, :], in0=vt[:, r_, :], in1=vln[:, 0, :],
                                                             op=ALU.mult),
                             reads=[d_vt[r_], d_vln], writes=[d_vt[r_]])
                        k.op(pool, lambda e: e.tensor_tensor(out=vn[:, n_, :], in0=vt[:, r_, :], in1=vln[:, 1, :],
                                                             op=ALU.add),
                             reads=[d_vt[r_], d_vln], writes=[d_vn[n_]])

                    def v2(t):
                        n_ = t % 3
                        b2 = k.banks(1)
                        k.mm([(ps[:, b2, h * 128:(h + 1) * 128], vn[:, n_, h * 128:(h + 1) * 128], swT[:, h, :],
                               True, True) for h in range(4)],
                             reads=[d_vn[n_], d_swT], writes=[psd[b2]])
                        ts_ = rr("tmp", 3)
                        k.op(dve, lambda e: e.tensor_tensor(out=tmp[:, ts_, :], in0=ps[:, b2, :], in1=sbb[:],
                                                            op=ALU.add),
                             reads=[psd[b2], d_sbb], writes=[d_tmp[ts_]])
                        k.op(dve, lambda e: e.tensor_tensor(
                            out=bigA[:, 4:8, t * 128:(t + 1) * 128],
                            in0=tmp[:, ts_, :].rearrange("p (h q) -> p h q", h=4),
                            in1=uT[:, :, t * 128:(t + 1) * 128], op=ALU.mult),
                             reads=[d_tmp[ts_]] + d_uT, writes=[d_yTv[t]])

                    run_pipeline(NT, [(0, v0), (3, v2), (2, v1b), (1, v1a)])
                    k.barrier()
                if stop_after == "sgu1":
                    add_dump("yT", bigA[:], [128, 8, S], BF16, d_yTv)
                    return
                with ExitStack() as s1:
                    fab = sb(s1, nc, "fab", [128, 2, 256], BF16)
                    gcs = sb(s1, nc, "gcs", [128, 2, 32, 128], BF16)
                    cs128 = sb(s1, nc, "cs128", [128, 2, 128], BF16)
                    fw = sb(s1, nc, "fw", [128, 4, 128], BF16)
                    m12 = sb(s1, nc, "m12", [128, 4, 2, 128], BF16)
                    Q = sb(s1, nc, "Q", [128, 2, 32, 32, 4], BF16)
                    Q2 = sb(s1, nc, "Q2", [128, 2, 32, 128], BF16)
                    Y = sb(s1, nc, "Y", [128, 2, 32, 128], BF16)
                    d_fab = k.dmadep()
                    d_gcs = k.dmadep()
                    d_cs = k.dmadep()
                    d_fw = k.dmadep()
                    d_m12 = Dep()
                    d_Q2 = [[Dep() for _ in range(8)] for _ in range(2)]
                    k.dma(pool, [(fab[:], T["fab"])], writes=[d_fab], semdep=d_fab)
                    k.dma(pool, [(gcs[:, 0], T["gcs"][:, 0]), (gcs[:, 1], T["gcs"][:, 1])], writes=[d_gcs], semdep=d_gcs)
                    k.dma(pool, [(cs128[:], T["cs128"])], writes=[d_cs], semdep=d_cs)
                    k.dma(pool, [(fw[:], T["od_fourier_w"])], writes=[d_fw], semdep=d_fw)
                    sc = 1.0 / np.sqrt(4096.0 * 128.0)
                    for h in range(4):
                        b = k.banks(1)
                        k.mm([(ps[:, b, 0:128], cs128[:, 0, :], fw[:, h, :], True, True),
                              (ps[:, b, 128:256], cs128[:, 1, :], fw[:, h, :], True, True)],
                             reads=[d_cs, d_fw], writes=[psd[b]])
                        k.op(act, lambda e: e.activation(out=m12[:, h, 0, :], in_=ps[:, b, 0:128], func=AF.Copy,
                                                         scale=float(sc)),
                             reads=[psd[b]], writes=[d_m12])
                        k.op(act, lambda e: e.activation(out=m12[:, h, 1, :], in_=ps[:, b, 128:256], func=AF.Copy,
                                                         scale=float(-sc)),
                             reads=[psd[b]], writes=[d_m12])
                    def evac_on(which, out_ap, in_ap, reads, writes):
                        if which == 0:
                            k.op(act, lambda e: e.activation(out=out_ap, in_=in_ap, func=AF.Copy), reads=reads,
                                 writes=writes)
                        else:
                            k.op(dve, lambda e: e.tensor_copy(out=out_ap, in_=in_ap), reads=reads, writes=writes)

                    d_Q = [[Dep() for _ in range(8)] for _ in range(2)]
                    d_Y = [[Dep() for _ in range(8)] for _ in range(2)]
                    for h in range(4):
                        for bb in range(0, 32, 4):
                            for r in range(2):
                                b = k.banks(1)
                                k.mm([(ps[:, b, i * 128:(i + 1) * 128], cT[:, h, (bb + i) * 128:(bb + i + 1) * 128],
                                       m12[:, h, r, :], True, True) for i in range(4)],
                                     reads=[d_cT[h], d_m12], writes=[psd[b]])
                                evac_on(r, Q[:, r, :, bb:bb + 4, :].rearrange("p c b j -> p b c j"),
                                        ps[:, b, :].rearrange("p (i c j) -> p i c j", i=4, j=4),
                                        [psd[b]], [d_Q[r][bb // 4]])
                        for ri in range(2):
                            for cg0 in range(0, 32, 4):
                                b = k.banks(1)
                                k.mm([(ps[:, b, i * 128:(i + 1) * 128],
                                       Q[:, ri, cg0 + i, :, :].rearrange("p b j -> p (b j)"), ident[:], True, True)
                                      for i in range(4)],
                                     reads=d_Q[ri] + [d_ident], writes=[psd[b]])
                                evac_on(ri, Q2[:, ri, cg0:cg0 + 4, :].rearrange("p c a -> p (c a)"), ps[:, b, :],
                                        [psd[b]], [d_Q2[ri][cg0 // 4]])
                        for cg0 in range(0, 32, 4):
                            for r in range(2):
                                b = k.banks(1)
                                mms = []
                                for i in range(4):
                                    mms.append((ps[:, b, i * 128:(i + 1) * 128], Q2[:, 0, cg0 + i, :],
                                                fab[:, 0, r * 128:(r + 1) * 128], True, False))
                                    mms.append((ps[:, b, i * 128:(i + 1) * 128], Q2[:, 1, cg0 + i, :],
                                                fab[:, 1, r * 128:(r + 1) * 128], False, True))
                                k.mm(mms, reads=[d_Q2[0][cg0 // 4], d_Q2[1][cg0 // 4], d_fab], writes=[psd[b]])
                                evac_on(r, Y[:, r, :, cg0 * 4:(cg0 + 4) * 4].rearrange("p b (i j) -> p i b j", i=4),
                                        ps[:, b, :].rearrange("p (i b j) -> p i b j", i=4, j=4),
                                        [psd[b]], [d_Y[r][cg0 // 4]])
                        yv = bigA[:, h, :].rearrange("p (a b) -> p b a", b=32)
                        for b0 in range(0, 32, 4):
                            b = k.banks(1)
                            mms = []
                            for i in range(4):
                                mms.append((ps[:, b, i * 128:(i + 1) * 128], Y[:, 0, b0 + i, :], gcs[:, 0, b0 + i, :], True, False))
                                mms.append((ps[:, b, i * 128:(i + 1) * 128], Y[:, 1, b0 + i, :], gcs[:, 1, b0 + i, :], False, True))
                            k.mm(mms, reads=d_Y[0] + d_Y[1] + [d_gcs], writes=[psd[b]])
                            evac_on(h % 2, yv[:, b0:b0 + 4, :], ps[:, b, :].rearrange("p (b a) -> p b a", b=4),
                                    [psd[b]], [d_yTf[h * 8 + b0 // 4]])
                    k.barrier()
                if stop_after == "four1":
                    add_dump("yT", bigA[:], [128, 8, S], BF16, d_yTv + d_yTf)
                    return
                with ExitStack() as s1:
                    wo = sb(s1, nc, "wo1", [128, 8, D], BF16)
                    d_wo = k.dmadep()
                    k.dma(pool, [(wo[:], T["od_w_out"].rearrange("(k p) n -> p k n", p=128))],
                          writes=[d_wo], semdep=d_wo)
                    out_phase(IO(s1), bigA, d_yTv + d_yTf, wo, d_wo, src, d_xres)
                    k.barrier()
                k.barrier()

        stages = ["mixer0", "ffn0", "mixer1", "ffn1"]
        m0_stops = ("norm0", "proj0", "pool0", "conv0")
        m1_stops = ("proj1", "sgu1", "four1")
        done = False
        mixer0()
        if stop_after in m0_stops:
            done = True
        if not done and stop_after == "mixer0":
            done = True
        if not done:
            ffn_phase(0, last=False)
            if stop_after == "ffn0":
                done = True
        if not done:
            mixer1()
            if stop_after in m1_stops or stop_after == "mixer1":
                done = True
        if not done:
            ffn_phase(1, last=True)
        if done and stop_after in ("mixer0", "ffn0", "mixer1"):
            k.barrier()
            dd = k.dmadep()
            k.dma(sp, [(out[:, :], xres[:, :])], semdep=dd)
        k.barrier()
    return nc


def _rep(v, n=128):
    return np.ascontiguousarray(np.broadcast_to(np.asarray(v, np.float32).reshape(1, -1), (n, v.size)))


def prep_inputs(inputs):
    f = lambda a: np.ascontiguousarray(np.asarray(a, dtype=np.float32))
    g = {}
    mg, fg, fin = f(inputs["mix_norm_g"]), f(inputs["ffn_norm_g"]), f(inputs["final_norm_g"])
    g["norm_g"] = np.stack([_rep(mg[0]), _rep(fg[0]), _rep(mg[1]), _rep(fg[1]), _rep(fin)], axis=0)
    g["ev_w_in"] = f(inputs["ev_w_in"])[0]
    g["ev_w_out"] = f(inputs["ev_w_out"])[0]
    g["od_w_in"] = f(inputs["od_w_in"])[0]
    g["od_w_out"] = f(inputs["od_w_out"])[0]
    g["ffn_w_gate"] = f(inputs["ffn_w_gate"])
    g["ffn_w_up"] = f(inputs["ffn_w_up"])
    g["ffn_w_down"] = f(inputs["ffn_w_down"])
    cw = f(inputs["ev_conv_w"])[0]
    g["ev_conv_w"] = np.ascontiguousarray(cw.reshape(31, 4, 128).transpose(2, 1, 0))
    vecs = np.stack([f(inputs["ev_conv_b"])[0], f(inputs["ev_ln_g"])[0], f(inputs["ev_ln_b"])[0],
                     f(inputs["ev_pool_scale"])[0].reshape(512)], axis=0)
    g["ev_vec"] = np.ascontiguousarray(vecs.reshape(4, 4, 128).transpose(2, 1, 0))
    g["ev_pool_w"] = np.ascontiguousarray(f(inputs["ev_pool_w"])[0].transpose(1, 0, 2))
    g["od_fourier_w"] = np.ascontiguousarray(f(inputs["od_fourier_w"])[0].transpose(1, 0, 2))
    g["od_vln"] = np.stack([_rep(f(inputs["od_v_ln_g"])[0].reshape(512)),
                            _rep(f(inputs["od_v_ln_b"])[0].reshape(512))], axis=1)
    g["od_spatial_wT"] = np.ascontiguousarray(f(inputs["od_spatial_w"])[0].transpose(2, 0, 1))
    g["od_spatial_b"] = _rep(f(inputs["od_spatial_b"])[0].reshape(512))
    g.update(make_consts())
    return g


_NC_CACHE = {}


def kernel(**inputs):
    x = np.asarray(inputs["x"], dtype=np.float32)
    shared = prep_inputs(inputs)
    if "nc" not in _NC_CACHE:
        _NC_CACHE["nc"] = build()
    nc = _NC_CACHE["nc"]
    in_maps = []
    for b in range(8):
        m = dict(shared)
        m["x"] = np.ascontiguousarray(x[b])
        in_maps.append(m)
    res = run_bass_kernel_spmd(nc, in_maps, core_ids=list(range(8)))
    return np.stack([np.asarray(r["out"], dtype=np.float32) for r in res.results], axis=0)
```

```python
import numpy as np
from contextlib import ExitStack
import concourse.bass as bass
import concourse.mybir as mybir
from concourse.bass_utils import run_bass_kernel_spmd

F32 = mybir.dt.float32
BF16 = mybir.dt.bfloat16
AF = mybir.ActivationFunctionType
ALU = mybir.AluOpType

S = 4096
D = 1024
DFF = 2816
NT = S // 128
EPS = 1e-6
NJ = DFF // 128
ST = 1024
NST = S // ST


class Eng:
    def __init__(self, es, nc, eng, name):
        self.e = eng
        self.name = name
        self.sem = es.enter_context(nc.semaphore("es_" + name))
        self.cnt = 0
        self.seen = {}

    def wait(self, ev):
        if ev is None:
            return
        sem, val = ev
        k = id(sem)
        if self.seen.get(k, 0) >= val:
            return
        self.e.wait_ge(sem, val)
        self.seen[k] = val

    def sig(self, ins):
        self.cnt += 1
        ins.then_inc(self.sem, 1)
        return (self.sem, self.cnt)


class Dep:
    def __init__(self, sem=None):
        self.w = None
        self.w2 = []
        self.r = {}
        self.sem = sem
        self.semval = 0


class K:
    def __init__(self, es, nc):
        self.nc = nc
        self.es = es
        self.pe = Eng(es, nc, nc.tensor, "pe")
        self.act = Eng(es, nc, nc.scalar, "act")
        self.dve = Eng(es, nc, nc.vector, "dve")
        self.pool = Eng(es, nc, nc.gpsimd, "pool")
        self.sp = Eng(es, nc, nc.sync, "sp")
        self.engs = [self.pe, self.act, self.dve, self.pool, self.sp]
        self.nsem = 0
        self.dma_deps = []
        self.ps = es.enter_context(nc.psum_tensor("ps", [128, 8, 512], F32))
        self.psd = [Dep() for _ in range(8)]
        self.bank_ptr = 0
        self.pool_ptr = {}

    def dmadep(self):
        self.nsem += 1
        d = Dep(self.es.enter_context(self.nc.semaphore("ds%d" % self.nsem)))
        self.dma_deps.append(d)
        return d

    def _pre(self, eng, reads, writes, addw=()):
        for d in reads:
            eng.wait(d.w)
            for ev in d.w2:
                eng.wait(ev)
        for d in writes:
            eng.wait(d.w)
            for ev in d.w2:
                eng.wait(ev)
            for ev in d.r.values():
                eng.wait(ev)
        for d in addw:
            for ev in d.r.values():
                eng.wait(ev)

    def _post(self, ev, reads, writes, addw=()):
        for d in reads:
            d.r[id(ev[0])] = ev
        for d in writes:
            d.w = ev
            d.w2 = []
            d.r = {}
        for d in addw:
            d.w2.append(ev)

    def op(self, eng, fn, reads=(), writes=(), addw=()):
        self._pre(eng, reads, writes, addw)
        ins = fn(eng.e)
        ev = eng.sig(ins)
        self._post(ev, reads, writes, addw)
        return ev

    def mm(self, mms, reads=(), writes=()):
        self._pre(self.pe, reads, writes)
        ins = None
        for (o, l, r, st, sp) in mms:
            ins = self.nc.tensor.matmul(o, lhsT=l, rhs=r, start=st, stop=sp)
        ev = self.pe.sig(ins)
        self._post(ev, reads, writes)
        return ev

    def dma(self, q, pairs, reads=(), writes=(), semdep=None):
        self._pre(q, reads, writes)
        for (o, i) in pairs:
            ins = q.e.dma_start(out=o, in_=i)
            semdep.semval += 16
            ins.then_inc(semdep.sem, 16)
        ev = (semdep.sem, semdep.semval)
        self._post(ev, reads, writes)
        return ev

    def banks(self, n=1, pool=None):
        if pool is not None:
            key = tuple(pool)
            i = self.pool_ptr.get(key, 0)
            self.pool_ptr[key] = i + 1
            return pool[i % len(pool)]
        if n == 2 and self.bank_ptr % 2:
            self.bank_ptr += 1
        b = self.bank_ptr % 8
        self.bank_ptr += n
        return b

    def barrier(self):
        evs = []
        for e in self.engs:
            if e.cnt:
                evs.append((e.sem, e.cnt))
        for d in self.dma_deps:
            if d.semval:
                evs.append((d.sem, d.semval))
        for e in self.engs:
            for ev in evs:
                e.wait(ev)


_SB_UID = [0]


def sb(es, nc, name, shape, dt):
    _SB_UID[0] += 1
    return es.enter_context(nc.sbuf_tensor("sb%d_%s" % (_SB_UID[0], name), shape, dt))


def make_consts():
    c = {}
    c["ident"] = np.eye(128, dtype=np.float32)
    c["ones512"] = np.full((128, 128), 1.0 / 512.0, dtype=np.float32)
    wins = (2, 4, 8, 16)
    band = np.zeros((4, 5, 128, 128), dtype=np.float64)
    for g, w in enumerate(wins):
        half = w // 2
        for v, (tile, dt) in enumerate([(5, 0), (0, 0), (NT - 1, 0), (5, -1), (5, 1)]):
            for tl in range(128):
                t = tile * 128 + tl
                lo = min(max(t - half, 0), S - 1)
                hi = min(max(t + half - 1, 0), S - 1)
                cnt = hi - lo + 1
                for j in range(lo, hi + 1):
                    jt, jl = divmod(j, 128)
                    if jt == tile + dt:
                        band[g, v, jl, tl] += 1.0 / cnt
                if dt == 0:
                    band[g, v, tl, tl] -= 1.0
    c["band"] = np.ascontiguousarray(band.transpose(2, 0, 1, 3)).astype(np.float32)
    b = np.arange(32)
    ang = 2 * np.pi * np.outer(b, b) / 32.0
    Cr = np.cos(ang)
    Ci = -np.sin(ang)
    I4 = np.eye(4)
    FA = np.concatenate([np.kron(Cr, I4), np.kron(Ci, I4)], axis=1)
    FB = np.concatenate([np.kron(-Ci, I4), np.kron(Cr, I4)], axis=1)
    c["fab"] = np.stack([FA, FB], axis=1).astype(np.float32)
    a = np.arange(128)
    sp = (32 * a[None, :] + b[:, None])
    th = 2 * np.pi * (a[:, None, None] * sp[None, :, :]) / 4096.0
    c["gcs"] = np.stack([np.cos(th), np.sin(th)], axis=1).astype(np.float32)
    ch = np.arange(128)
    a2 = 2 * np.pi * np.outer(ch, ch) / 128.0
    c["cs128"] = np.stack([np.cos(a2), np.sin(a2)], axis=1).astype(np.float32)
    return c


CONST_SHAPES = {
    "ident": [128, 128], "ones512": [128, 128], "band": [128, 4, 5, 128], "fab": [128, 2, 256],
    "gcs": [128, 2, 32, 128], "cs128": [128, 2, 128],
}

IN_SHAPES = {
    "x": [S, D],
    "norm_g": [5, 128, D],
    "ev_w_in": [D, 1536], "ev_w_out": [D, D], "od_w_in": [D, 1536], "od_w_out": [D, D],
    "ffn_w_gate": [2, D, DFF], "ffn_w_up": [2, D, DFF], "ffn_w_down": [2, DFF, D],
    "ev_conv_w": [128, 4, 31],
    "ev_vec": [128, 4, 4],
    "ev_pool_w": [128, 4, 128],
    "od_fourier_w": [128, 4, 128],
    "od_vln": [128, 2, 512],
    "od_spatial_wT": [128, 4, 128],
    "od_spatial_b": [128, 512],
}
IN_SHAPES.update(CONST_SHAPES)


def build(stop_after=None, dumps=()):
    nc = bass.Bass("TRN2", target_bir_lowering=False)
    T = {}
    for name, shp in IN_SHAPES.items():
        T[name] = nc.dram_tensor(name, shp, F32, kind="ExternalInput").ap()
    out = nc.dram_tensor("out", [S, D], F32, kind="ExternalOutput").ap()
    xres = nc.dram_tensor("xres", [S, D], F32, kind="Internal").ap()
    dump_t = {}

    with ExitStack() as es:
        k = K(es, nc)
        pe, act, dve, pool, sp = k.pe, k.act, k.dve, k.pool, k.sp
        ps = k.ps
        psd = k.psd

        ident = sb(es, nc, "ident", [128, 128], BF16)
        ssb = sb(es, nc, "ssb", [128, 8], F32)
        tmp = sb(es, nc, "tmp", [128, 3, 512], F32)
        d_ident = k.dmadep()
        d_ss = [Dep() for _ in range(8)]
        d_tmp = [Dep() for _ in range(3)]
        d_xres = [Dep() for _ in range(NT)]
        cnt = {"xin": 0, "xout": 0, "hb": 0, "ss": 0, "tmp": 0}
        uid = [0]

        def rr(name, n):
            v = cnt[name] % n
            cnt[name] += 1
            return v

        k.dma(pool, [(ident[:], T["ident"])], writes=[d_ident], semdep=d_ident)

        io_sems = {"gbc": [k.dmadep(), k.dmadep()], "xin": [k.dmadep() for _ in range(6)],
                   "xout": [k.dmadep() for _ in range(2)]}

        class IO:
            def __init__(self, scope, nxin=6):
                self.nxin = nxin
                uid[0] += 1
                u = str(uid[0])
                self.gbc = sb(scope, nc, "gbc" + u, [128, 2, D], F32)
                self.xin = sb(scope, nc, "xin" + u, [128, nxin, D], F32)
                self.xout = sb(scope, nc, "xout" + u, [128, 2, D], F32)
                self.hb = sb(scope, nc, "hb" + u, [128, 3, D], BF16)
                self.junk = sb(scope, nc, "junk" + u, [128, D], BF16)
                self.d_gbc = io_sems["gbc"]
                self.d_xin = io_sems["xin"]
                self.d_xout = io_sems["xout"]
                self.d_hb = [Dep(), Dep(), Dep()]
                self.d_junk = Dep()

        def add_dump(name, ap_sb, shape, dt, deps):
            if name not in dumps:
                return
            t = nc.dram_tensor("dbg_" + name, shape, dt, kind="ExternalOutput").ap()
            dd = k.dmadep()
            k.dma(sp, [(t, ap_sb)], reads=deps, semdep=dd)
            dump_t[name] = dd

        def run_pipeline(n, stages):
            mx = max(sk for sk, _ in stages)
            for i in range(n + mx):
                for sk, fn in stages:
                    t = i - sk
                    if 0 <= t < n:
                        fn(t)

        def norm_stages(io, src_fn, src_deps, tiles, gidx, hT, hT_deps, col0=0, bpool=None):
            gbc, xin, hb, junk = io.gbc, io.xin, io.hb, io.junk
            d_gbc, d_xin, d_hb, d_junk = io.d_gbc, io.d_xin, io.d_hb, io.d_junk
            k.dma(sp, [(gbc[:, 0, :], T["norm_g"][gidx])], writes=[d_gbc[0]], semdep=d_gbc[0])
            stt = {}

            def s_load(i):
                t = tiles[i]
                xs = rr("xin", io.nxin)
                stt[i] = {"xs": xs}
                k.dma(sp, [(xin[:, xs, :], src_fn(t))], reads=[src_deps[t]] if src_deps else [],
                      writes=[d_xin[xs]], semdep=d_xin[xs])

            def s_stat(i):
                xs = stt[i]["xs"]
                s_ = rr("ss", 8)
                k.op(act, lambda e: e.activation(out=junk[:], in_=xin[:, xs, :], func=AF.Square,
                                                 accum_out=ssb[:, s_:s_ + 1]),
                     reads=[d_xin[xs]], writes=[d_junk, d_ss[s_]])
                k.op(act, lambda e: e.activation(out=ssb[:, s_:s_ + 1], in_=ssb[:, s_:s_ + 1], func=AF.Sqrt,
                                                 bias=float(D * EPS)),
                     reads=[d_ss[s_]], writes=[d_ss[s_]])
                k.op(dve, lambda e: e.reciprocal(out=ssb[:, s_:s_ + 1], in_=ssb[:, s_:s_ + 1]),
                     reads=[d_ss[s_]], writes=[d_ss[s_]])
                h_ = rr("hb", 3)
                stt[i]["h"] = h_
                k.op(dve, lambda e: e.scalar_tensor_tensor(out=hb[:, h_, :], in0=xin[:, xs, :],
                                                           scalar=ssb[:, s_:s_ + 1], in1=gbc[:, 0, :],
                                                           op0=ALU.mult, op1=ALU.mult),
                     reads=[d_xin[xs], d_ss[s_], d_gbc[0]], writes=[d_hb[h_]])

            def s_tr(i):
                h_ = stt[i]["h"]
                b = k.banks(2, pool=bpool)
                psv = ps[:, b:b + 2, :].rearrange("p b (c n) -> p (b c) n", n=128)
                k.mm([(psv[:, kk, :], hb[:, h_, kk * 128:(kk + 1) * 128], ident[:], True, True)
                      for kk in range(8)],
                     reads=[d_hb[h_], d_ident], writes=[psd[b], psd[b + 1]])
                c0 = col0 + i * 128
                k.op(act, lambda e: e.activation(out=hT[:, 0:4, c0:c0 + 128], in_=psv[:, 0:4, :], func=AF.Copy,
                                                 scale=float(np.sqrt(D))),
                     reads=[psd[b]], writes=[hT_deps[i]])
                k.op(dve, lambda e: e.tensor_scalar(out=hT[:, 4:8, c0:c0 + 128], in0=psv[:, 4:8, :],
                                                    scalar1=float(np.sqrt(D)), scalar2=None, op0=ALU.mult),
                     reads=[psd[b + 1]], addw=[hT_deps[i]])

            return [(0, s_load), (4, s_tr), (2, s_stat)]

        def norm_phase(io, src_fn, src_deps, tiles, gidx, hT, hT_deps, col0=0):
            run_pipeline(len(tiles), norm_stages(io, src_fn, src_deps, tiles, gidx, hT, hT_deps, col0))

        def proj_fm(w_sb, w_dep, col_chunks, hT, hT_deps_all, ntg, epilogue):
            for tg in range(ntg):
                for ci, cols in enumerate(col_chunks):
                    bl = []
                    for c0 in cols:
                        b = k.banks(1)
                        k.mm([(ps[:, b, :], w_sb[:, kk, c0:c0 + 128], hT[:, kk, tg * 512:(tg + 1) * 512],
                               kk == 0, kk == 7) for kk in range(8)],
                             reads=[w_dep] + hT_deps_all(tg), writes=[psd[b]])
                        bl.append(b)
                    epilogue(tg, ci, bl)

        def out_phase(io, yT, yT_dep, wo, wo_dep, src_fn, src_deps):
            xin, xout = io.xin, io.xout
            d_xin, d_xout = io.d_xin, io.d_xout
            stt = {}

            def s_load(t):
                xs = rr("xin", io.nxin)
                stt[t] = {"xs": xs}
                k.dma(sp, [(xin[:, xs, :], src_fn(t))], reads=[src_deps[t]] if src_deps else [],
                      writes=[d_xin[xs]], semdep=d_xin[xs])

            def s_mm(t):
                b = k.banks(2)
                stt[t]["b"] = b
                mms = []
                for nh in range(2):
                    for kk in range(8):
                        mms.append((ps[:, b + nh, :], yT[:, kk, t * 128:(t + 1) * 128],
                                    wo[:, kk, nh * 512:(nh + 1) * 512], kk == 0, kk == 7))
                k.mm(mms, reads=list(yT_dep) + [wo_dep], writes=[psd[b], psd[b + 1]])

            def s_add(t):
                xs, b = stt[t]["xs"], stt[t]["b"]
                xo = rr("xout", 2)
                k.op(dve, lambda e: e.tensor_tensor(out=xout[:, xo, :], in0=xin[:, xs, :],
                                                    in1=ps[:, b:b + 2, :].rearrange("p b n -> p (b n)"),
                                                    op=ALU.add),
                     reads=[d_xin[xs], psd[b], psd[b + 1]], writes=[d_xout[xo]])
                k.dma(pool, [(xres[t * 128:(t + 1) * 128, :], xout[:, xo, :])], reads=[d_xout[xo]],
                      writes=[d_xres[t]], semdep=d_xout[xo])

            run_pipeline(NT, [(0, s_load), (1, s_mm), (2, s_add)])

        def ffn_phase(l, last):
            with ExitStack() as fs:
                io = IO(fs, nxin=6)
                gbc, xin, xout, junk = io.gbc, io.xin, io.xout, io.junk
                d_gbc, d_xin, d_xout, d_junk = io.d_gbc, io.d_xin, io.d_xout, io.d_junk
                if last:
                    k.dma(sp, [(gbc[:, 1, :], T["norm_g"][4])], writes=[d_gbc[1]], semdep=d_gbc[1])
                    k.op(dve, lambda e: e.tensor_scalar(out=gbc[:, 1, :], in0=gbc[:, 1, :], scalar1=float(np.sqrt(D)),
                                                        scalar2=None, op0=ALU.mult),
                         reads=[d_gbc[1]], writes=[d_gbc[1]])
                h2T = sb(fs, nc, "h2T%d" % l, [128, 2, 8, ST], BF16)
                gT = sb(fs, nc, "gT%d" % l, [128, NJ, ST], BF16)
                wd = sb(fs, nc, "wd%d" % l, [128, NJ, D], BF16)
                wgu = sb(fs, nc, "wgu%d" % l, [128, 3, 2, 8, 256], BF16)
                d_h2T = [[Dep() for _ in range(8)] for _ in range(2)]
                d_gT = [Dep() for _ in range(NJ)]
                d_wd = k.dmadep()
                d_wgu = [k.dmadep() for _ in range(3)]
                wdv = T["ffn_w_down"][l].rearrange("(j p) n -> p j n", p=128)
                wgv = T["ffn_w_gate"][l].rearrange("(k p) n -> p k n", p=128)
                wuv = T["ffn_w_up"][l].rearrange("(k p) n -> p k n", p=128)
                nslot = [0]
                xsrc = lambda t: xres[t * 128:(t + 1) * 128, :]

                def nstages(st, bpool=None):
                    tiles = list(range(st * 8, st * 8 + 8))
                    return norm_stages(io, xsrc, d_xres, tiles, 1 + 2 * l, h2T[:, st % 2], d_h2T[st % 2], bpool=bpool)

                def gu(st):
                    hT_ = h2T[:, st % 2]
                    dh = d_h2T[st % 2]
                    for c in range(NJ // 2):
                        sl = nslot[0] % 3
                        nslot[0] += 1
                        k.dma(pool, [(wgu[:, sl, 0, :, :], wgv[:, :, c * 256:(c + 1) * 256]),
                                     (wgu[:, sl, 1, :, :], wuv[:, :, c * 256:(c + 1) * 256])],
                              writes=[d_wgu[sl]], semdep=d_wgu[sl])
                        if st == 0 and c == 2:
                            k.dma(pool, [(wd[:, 0:11, :], wdv[:, 0:11, :]), (wd[:, 11:22, :], wdv[:, 11:22, :])],
                                  writes=[d_wd], semdep=d_wd)
                        for jj in range(2):
                            j = c * 2 + jj
                            for tg in range(ST // 512):
                                bg = k.banks(1)
                                k.mm([(ps[:, bg, :], wgu[:, sl, 0, kk, jj * 128:(jj + 1) * 128],
                                       hT_[:, kk, tg * 512:(tg + 1) * 512], kk == 0, kk == 7) for kk in range(8)],
                                     reads=[d_wgu[sl]] + dh[tg * 4:(tg + 1) * 4], writes=[psd[bg]])
                                bu = k.banks(1)
                                k.mm([(ps[:, bu, :], wgu[:, sl, 1, kk, jj * 128:(jj + 1) * 128],
                                       hT_[:, kk, tg * 512:(tg + 1) * 512], kk == 0, kk == 7) for kk in range(8)],
                                     reads=[d_wgu[sl]] + dh[tg * 4:(tg + 1) * 4], writes=[psd[bu]])
                                ts_ = rr("tmp", 3)
                                k.op(act, lambda e: e.activation(out=tmp[:, ts_, :], in_=ps[:, bg, :], func=AF.Silu),
                                     reads=[psd[bg]], writes=[d_tmp[ts_]])
                                k.op(dve, lambda e: e.tensor_tensor(out=gT[:, j, tg * 512:(tg + 1) * 512],
                                                                    in0=tmp[:, ts_, :], in1=ps[:, bu, :], op=ALU.mult),
                                     reads=[d_tmp[ts_], psd[bu]], writes=[d_gT[j]])

                def down_stages(st):
                    stt = {}

                    def s_load(ti):
                        t = st * 8 + ti
                        xs = rr("xin", io.nxin)
                        stt[ti] = {"xs": xs}
                        k.dma(sp, [(xin[:, xs, :], xres[t * 128:(t + 1) * 128, :])], reads=[d_xres[t]],
                              writes=[d_xin[xs]], semdep=d_xin[xs])

                    def s_mm(ti):
                        b = k.banks(2, pool=[0, 2, 4])
                        stt[ti]["b"] = b
                        mms = []
                        for nh in range(2):
                            for j in range(NJ):
                                mms.append((ps[:, b + nh, :], gT[:, j, ti * 128:(ti + 1) * 128],
                                            wd[:, j, nh * 512:(nh + 1) * 512], j == 0, j == NJ - 1))
                        k.mm(mms, reads=[d_wd] + d_gT, writes=[psd[b], psd[b + 1]])

                    def s_add(ti):
                        t = st * 8 + ti
                        xs, b = stt[ti]["xs"], stt[ti]["b"]
                        xo = rr("xout", 2)
                        k.op(dve, lambda e: e.tensor_tensor(out=xout[:, xo, :], in0=xin[:, xs, :],
                                                            in1=ps[:, b:b + 2, :].rearrange("p b n -> p (b n)"),
                                                            op=ALU.add),
                             reads=[d_xin[xs], psd[b], psd[b + 1]], writes=[d_xout[xo]])
                        if not last:
                            k.dma(sp, [(xres[t * 128:(t + 1) * 128, :], xout[:, xo, :])], reads=[d_xout[xo]],
                                  writes=[d_xres[t]], semdep=d_xout[xo])
                        else:
                            s_ = rr("ss", 8)
                            k.op(act, lambda e: e.activation(out=junk[:], in_=xout[:, xo, :], func=AF.Square,
                                                             accum_out=ssb[:, s_:s_ + 1]),
                                 reads=[d_xout[xo]], writes=[d_junk, d_ss[s_]])
                            k.op(act, lambda e: e.activation(out=ssb[:, s_:s_ + 1], in_=ssb[:, s_:s_ + 1],
                                                             func=AF.Sqrt, bias=float(D * EPS)),
                                 reads=[d_ss[s_]], writes=[d_ss[s_]])
                            k.op(dve, lambda e: e.reciprocal(out=ssb[:, s_:s_ + 1], in_=ssb[:, s_:s_ + 1]),
                                 reads=[d_ss[s_]], writes=[d_ss[s_]])
                            k.op(dve, lambda e: e.scalar_tensor_tensor(out=xout[:, xo, :], in0=xout[:, xo, :],
                                                                       scalar=ssb[:, s_:s_ + 1], in1=gbc[:, 1, :],
                                                                       op0=ALU.mult, op1=ALU.mult),
                                 reads=[d_xout[xo], d_ss[s_], d_gbc[1]], writes=[d_xout[xo]])
                            k.dma(sp, [(out[t * 128:(t + 1) * 128, :], xout[:, xo, :])], reads=[d_xout[xo]],
                                  semdep=d_xout[xo])

                    return [(0, s_load), (1, s_mm), (2, s_add)]

                run_pipeline(8, nstages(0))
                for st in range(NST):
                    gu(st)
                    stages = down_stages(st)
                    if st + 1 < NST:
                        ns = nstages(st + 1, bpool=[6])
                        stages = [stages[0], ns[0], stages[2], stages[1], ns[1], ns[2]]
                    run_pipeline(8, stages)
                k.barrier()

        def mixer0():
            with ExitStack() as ms:
                bigA = sb(ms, nc, "bigA", [128, 8, S], BF16)
                uT = sb(ms, nc, "uT", [128, 4, S + 32], BF16)
                d_hT = [Dep() for _ in range(NT)]
                d_uT = [Dep() for _ in range(4)]
                d_yT = Dep()
                evec = sb(ms, nc, "evec", [128, 4, 4], F32)
                d_evec = k.dmadep()
                k.dma(sp, [(evec[:], T["ev_vec"])], writes=[d_evec], semdep=d_evec)
                for c in range(4):
                    k.op(pool, lambda e: e.memset(uT[:, c, 0:16], 0.0), writes=[d_uT[c]])
                    k.op(pool, lambda e: e.memset(uT[:, c, S + 16:S + 32], 0.0), writes=[d_uT[c]])
                src = lambda t: T["x"][t * 128:(t + 1) * 128, :]
                with ExitStack() as sn:
                    norm_phase(IO(sn), src, None, list(range(NT)), 0, bigA, d_hT)
                    k.barrier()
                if stop_after == "norm0":
                    add_dump("hT", bigA[:], [128, 8, S], BF16, d_hT)
                    return
                with ExitStack() as s1:
                    p_sb = sb(s1, nc, "p_sb", [128, NT, 512], BF16)
                    d_p = [Dep() for _ in range(NT)]
                    with ExitStack() as s2:
                        w_in = sb(s2, nc, "w_in", [128, 8, 1536], BF16)
                        d_w = k.dmadep()
                        wv = T["ev_w_in"].rearrange("(k p) n -> p k n", p=128)
                        k.dma(pool, [(w_in[:, :, 0:768], wv[:, :, 0:768]), (w_in[:, :, 768:1536], wv[:, :, 768:1536])],
                              writes=[d_w], semdep=d_w)

                        def epi_a(tg, c, bl):
                            ts_ = rr("tmp", 3)
                            k.op(act, lambda e: e.activation(out=tmp[:, ts_, :], in_=ps[:, bl[1], :], func=AF.Sigmoid),
                                 reads=[psd[bl[1]]], writes=[d_tmp[ts_]])
                            k.op(dve, lambda e: e.tensor_tensor(out=uT[:, c, 16 + tg * 512:16 + (tg + 1) * 512],
                                                                in0=tmp[:, ts_, :], in1=ps[:, bl[0], :], op=ALU.mult),
                                 reads=[d_tmp[ts_], psd[bl[0]]], writes=[d_uT[c]])

                        proj_fm(w_in, d_w, [[c * 128, 512 + c * 128] for c in range(4)], bigA,
                                lambda tg: d_hT[tg * 4:(tg + 1) * 4], 8, epi_a)
                        for t in range(NT):
                            b = k.banks(1)
                            k.mm([(ps[:, b, :], bigA[:, kk, t * 128:(t + 1) * 128], w_in[:, kk, 1024:1536],
                                   kk == 0, kk == 7) for kk in range(8)],
                                 reads=[d_w, d_hT[t]], writes=[psd[b]])
                            eng = act if t % 2 == 0 else dve
                            if eng is act:
                                k.op(act, lambda e: e.activation(out=p_sb[:, t, :], in_=ps[:, b, :], func=AF.Copy),
                                     reads=[psd[b]], writes=[d_p[t]])
                            else:
                                k.op(dve, lambda e: e.tensor_copy(out=p_sb[:, t, :], in_=ps[:, b, :]),
                                     reads=[psd[b]], writes=[d_p[t]])
                        k.barrier()
                    if stop_after == "proj0":
                        add_dump("uT", uT[:], [128, 4, S + 32], BF16, d_uT)
                        add_dump("p_sb", p_sb[:], [128, NT, 512], BF16, d_p)
                        return
                    with ExitStack() as s2:
                        band = sb(s2, nc, "band", [128, 4, 5, 128], BF16)
                        pw = sb(s2, nc, "pw", [128, 4, 128], BF16)
                        pooled = sb(s2, nc, "pooled", [128, 3, 512], BF16)
                        d_band = k.dmadep()
                        d_pw = k.dmadep()
                        d_pooled = [Dep(), Dep(), Dep()]
                        k.dma(pool, [(band[:], T["band"])], writes=[d_band], semdep=d_band)
                        k.dma(pool, [(pw[:], T["ev_pool_w"])], writes=[d_pw], semdep=d_pw)
                        pooled_n = 3
                        pst = {}

                        def pl_a(n):
                            tg, g = divmod(n, 4)
                            b = k.banks(1)
                            mms = []
                            rd = {}
                            for ti in range(4):
                                t = tg * 4 + ti
                                srcs = []
                                if t > 0:
                                    srcs.append((t - 1, 3))
                                srcs.append((t, 1 if t == 0 else (2 if t == NT - 1 else 0)))
                                if t < NT - 1:
                                    srcs.append((t + 1, 4))
                                for si, (tt, v) in enumerate(srcs):
                                    mms.append((ps[:, b, ti * 128:(ti + 1) * 128],
                                                p_sb[:, tt, g * 128:(g + 1) * 128], band[:, g, v, :],
                                                si == 0, si == len(srcs) - 1))
                                    rd[tt] = d_p[tt]
                            k.mm(mms, reads=[d_band] + list(rd.values()), writes=[psd[b]])
                            pl = n % pooled_n
                            pst[n] = pl
                            k.op(act, lambda e: e.activation(out=pooled[:, pl, :], in_=ps[:, b, :], func=AF.Copy),
                                 reads=[psd[b]], writes=[d_pooled[pl]])

                        def pl_b(n):
                            tg, g = divmod(n, 4)
                            pl = pst[n]
                            b2 = k.banks(1)
                            k.mm([(ps[:, b2, :], pw[:, g, :], pooled[:, pl, :], True, True)],
                                 reads=[d_pw, d_pooled[pl]], writes=[psd[b2]])
                            k.op(dve, lambda e: e.tensor_scalar(out=bigA[:, 4 + g, tg * 512:(tg + 1) * 512],
                                                                in0=ps[:, b2, :], scalar1=evec[:, g, 3:4],
                                                                scalar2=None, op0=ALU.mult),
                                 reads=[psd[b2], d_evec], writes=[d_yT])

                        run_pipeline(32, [(0, pl_a), (2, pl_b)])
                        k.barrier()
                if stop_after == "pool0":
                    add_dump("yT", bigA[:], [128, 8, S], BF16, [d_yT])
                    return
                with ExitStack() as s1:
                    cw = sb(s1, nc, "cw", [128, 4, 31], F32)
                    identf = sb(s1, nc, "identf", [128, 128], F32)
                    ones = sb(s1, nc, "ones", [128, 128], BF16)
                    diag = sb(s1, nc, "diag", [128, 4, 31, 128], BF16)
                    vf = sb(s1, nc, "vf", [128, 2, 4, 512], F32)
                    vb = sb(s1, nc, "vb", [128, 2, 4, 512], BF16)
                    vq = sb(s1, nc, "vq", [128, 2, 4, 512], BF16)
                    st_sb = sb(s1, nc, "st_sb", [128, 2, 3, 512], F32)
                    t1 = sb(s1, nc, "t1", [128, 3, 512], F32)
                    d_cw = k.dmadep()
                    d_idf = k.dmadep()
                    d_ones = k.dmadep()
                    d_diag = Dep()
                    d_vf = [[Dep() for _ in range(4)] for _ in range(2)]
                    d_vb = [[Dep() for _ in range(4)] for _ in range(2)]
                    d_vq = [[Dep() for _ in range(4)] for _ in range(2)]
                    d_st = [[Dep() for _ in range(3)] for _ in range(2)]
                    d_t1 = [Dep() for _ in range(3)]
                    k.dma(sp, [(cw[:], T["ev_conv_w"])], writes=[d_cw], semdep=d_cw)
                    k.dma(sp, [(identf[:], T["ident"])], writes=[d_idf], semdep=d_idf)
                    k.dma(pool, [(ones[:], T["ones512"])], writes=[d_ones], semdep=d_ones)
                    for c in range(4):
                        for tap in range(31):
                            k.op(dve, lambda e: e.tensor_scalar(out=diag[:, c, tap, :], in0=identf[:],
                                                                scalar1=cw[:, c, tap:tap + 1], scalar2=None,
                                                                op0=ALU.mult),
                                 reads=[d_cw, d_idf], writes=[d_diag])
                    nt1 = 0
                    for tg in range(8):
                        r_ = tg % 2
                        for c in range(4):
                            b = k.banks(1)
                            k.mm([(ps[:, b, :], diag[:, c, tap, :],
                                   uT[:, c, 1 + tg * 512 + tap:1 + tg * 512 + tap + 512], tap == 0, tap == 30)
                                  for tap in range(31)],
                                 reads=[d_diag, d_uT[c]], writes=[psd[b]])
                            k.op(act, lambda e: e.activation(out=vf[:, r_, c, :], in_=ps[:, b, :], func=AF.Identity,
                                                             bias=evec[:, c, 0:1]),
                                 reads=[psd[b], d_evec], writes=[d_vf[r_][c]])
                            k.op(act, lambda e: e.activation(out=vq[:, r_, c, :], in_=ps[:, b, :], func=AF.Square,
                                                             bias=evec[:, c, 0:1]),
                                 reads=[psd[b], d_evec], writes=[d_vq[r_][c]])
                            k.op(pool, lambda e: e.tensor_copy(out=vb[:, r_, c, :], in_=vf[:, r_, c, :]),
                                 reads=[d_vf[r_][c]], writes=[d_vb[r_][c]])
                        bm = k.banks(1)
                        k.mm([(ps[:, bm, :], ones[:], vb[:, r_, c, :], c == 0, c == 3) for c in range(4)],
                             reads=[d_ones] + d_vb[r_], writes=[psd[bm]])
                        bq = k.banks(1)
                        k.mm([(ps[:, bq, :], ones[:], vq[:, r_, c, :], c == 0, c == 3) for c in range(4)],
                             reads=[d_ones] + d_vq[r_], writes=[psd[bq]])
                        k.op(act, lambda e: e.activation(out=st_sb[:, r_, 0, :], in_=ps[:, bm, :], func=AF.Copy),
                             reads=[psd[bm]], writes=[d_st[r_][0]])
                        k.op(act, lambda e: e.activation(out=st_sb[:, r_, 1, :], in_=ps[:, bm, :], func=AF.Square),
                             reads=[psd[bm]], writes=[d_st[r_][1]])
                        k.op(dve, lambda e: e.tensor_tensor(out=st_sb[:, r_, 1, :], in0=ps[:, bq, :],
                                                            in1=st_sb[:, r_, 1, :], op=ALU.subtract),
                             reads=[psd[bq], d_st[r_][1]], writes=[d_st[r_][1]])
                        k.op(act, lambda e: e.activation(out=st_sb[:, r_, 1, :], in_=st_sb[:, r_, 1, :], func=AF.Sqrt,
                                                         bias=float(EPS)),
                             reads=[d_st[r_][1]], writes=[d_st[r_][1]])
                        k.op(dve, lambda e: e.reciprocal(out=st_sb[:, r_, 1, :], in_=st_sb[:, r_, 1, :]),
                             reads=[d_st[r_][1]], writes=[d_st[r_][1]])
                        for c in range(4):
                            ti_ = nt1 % 3
                            nt1 += 1
                            k.op(pool, lambda e: e.tensor_tensor(out=t1[:, ti_, :], in0=vf[:, r_, c, :],
                                                                 in1=st_sb[:, r_, 0, :], op=ALU.subtract),
                                 reads=[d_vf[r_][c], d_st[r_][0]], writes=[d_t1[ti_]])
                            k.op(dve, lambda e: e.tensor_tensor(out=t1[:, ti_, :], in0=t1[:, ti_, :],
                                                                in1=st_sb[:, r_, 1, :], op=ALU.mult),
                                 reads=[d_t1[ti_], d_st[r_][1]], writes=[d_t1[ti_]])
                            k.op(act, lambda e: e.activation(out=bigA[:, c, tg * 512:(tg + 1) * 512], in_=t1[:, ti_, :],
                                                             func=AF.Silu, scale=evec[:, c, 1:2], bias=evec[:, c, 2:3]),
                                 reads=[d_t1[ti_], d_evec, d_yT], writes=[d_yT])
                    k.barrier()
                if stop_after == "conv0":
                    add_dump("yT", bigA[:], [128, 8, S], BF16, [d_yT])
                    return
                with ExitStack() as s1:
                    wo = sb(s1, nc, "wo", [128, 8, D], BF16)
                    d_wo = k.dmadep()
                    k.dma(pool, [(wo[:], T["ev_w_out"].rearrange("(k p) n -> p k n", p=128))],
                          writes=[d_wo], semdep=d_wo)
                    out_phase(IO(s1), bigA, [d_yT], wo, d_wo, src, None)
                    k.barrier()
                k.barrier()

        def mixer1():
            with ExitStack() as ms:
                bigA = sb(ms, nc, "bigA1", [128, 8, S], BF16)
                cT = sb(ms, nc, "cT", [128, 4, S], BF16)
                d_hT = [Dep() for _ in range(NT)]
                d_cT = [Dep() for _ in range(4)]
                d_yTv = [Dep() for _ in range(NT)]
                d_yTf = [Dep() for _ in range(32)]
                src = lambda t: xres[t * 128:(t + 1) * 128, :]
                with ExitStack() as sn:
                    norm_phase(IO(sn), src, d_xres, list(range(NT)), 2, bigA, d_hT)
                    k.barrier()
                with ExitStack() as s1:
                    uT = sb(s1, nc, "uT1", [128, 4, S], BF16)
                    d_uT = [Dep() for _ in range(4)]
                    w_in = sb(s1, nc, "w_in1", [128, 8, 1536], BF16)
                    d_w = k.dmadep()
                    wv = T["od_w_in"].rearrange("(k p) n -> p k n", p=128)
                    k.dma(pool, [(w_in[:, :, 0:768], wv[:, :, 0:768]), (w_in[:, :, 768:1536], wv[:, :, 768:1536])],
                          writes=[d_w], semdep=d_w)
                    vln = sb(s1, nc, "vln", [128, 2, 512], F32)
                    sbb = sb(s1, nc, "sbb", [128, 512], F32)
                    swT = sb(s1, nc, "swT", [128, 4, 128], BF16)
                    d_vln = k.dmadep()
                    d_sbb = k.dmadep()
                    d_swT = k.dmadep()
                    k.dma(sp, [(vln[:], T["od_vln"])], writes=[d_vln], semdep=d_vln)
                    k.dma(sp, [(sbb[:], T["od_spatial_b"])], writes=[d_sbb], semdep=d_sbb)
                    k.dma(pool, [(swT[:], T["od_spatial_wT"])], writes=[d_swT], semdep=d_swT)

                    nev = [0]

                    def epi_c(tg, c, bl):
                        nev[0] += 1
                        if nev[0] % 2:
                            k.op(act, lambda e: e.activation(out=cT[:, c, tg * 512:(tg + 1) * 512], in_=ps[:, bl[0], :],
                                                             func=AF.Copy),
                                 reads=[psd[bl[0]]], writes=[d_cT[c]])
                        else:
                            k.op(dve, lambda e: e.tensor_copy(out=cT[:, c, tg * 512:(tg + 1) * 512], in_=ps[:, bl[0], :]),
                                 reads=[psd[bl[0]]], writes=[d_cT[c]])

                    proj_fm(w_in, d_w, [[c * 128] for c in range(4)], bigA, lambda tg: d_hT[tg * 4:(tg + 1) * 4], 8, epi_c)

                    def epi_u(tg, c, bl):
                        k.op(act, lambda e: e.activation(out=uT[:, c, tg * 512:(tg + 1) * 512], in_=ps[:, bl[0], :],
                                                         func=AF.Gelu),
                             reads=[psd[bl[0]]], writes=[d_uT[c]])

                    proj_fm(w_in, d_w, [[512 + c * 128] for c in range(4)], bigA, lambda tg: d_hT[tg * 4:(tg + 1) * 4],
                            8, epi_u)
                    if stop_after == "proj1":
                        add_dump("cT", cT[:], [128, 4, S], BF16, d_cT)
                        add_dump("uT", uT[:], [128, 4, S], BF16, d_uT)
                        return
                    NV = 8
                    vt = sb(s1, nc, "vt", [128, NV, 512], F32)
                    vn = sb(s1, nc, "vn", [128, 3, 512], BF16)
                    bst = sb(s1, nc, "bst", [128, NV, 4, 6], F32)
                    mv = sb(s1, nc, "mv", [128, NV, 4, 2], F32)
                    nmr = sb(s1, nc, "nmr", [128, NV, 4], F32)
                    sbrow = sb(s1, nc, "sbrow", [1, 512], BF16)
                    ones_row = sb(s1, nc, "ones_row", [1, 128], BF16)
                    d_sbrow = k.dmadep()
                    d_onesr = Dep()
                    k.dma(pool, [(sbrow[:], T["od_spatial_b"][0:1, :])], writes=[d_sbrow], semdep=d_sbrow)
                    k.op(dve, lambda e: e.memset(ones_row[:], 1.0), writes=[d_onesr])
                    d_vt = [Dep() for _ in range(NV)]
                    d_vn = [Dep() for _ in range(3)]
                    d_bst = [Dep() for _ in range(NV)]
                    d_mv = [Dep() for _ in range(NV)]
                    vst = {}

                    def v0(t):
                        r_ = t % NV
                        b = k.banks(1)
                        k.mm([(ps[:, b, :], bigA[:, kk, t * 128:(t + 1) * 128], w_in[:, kk, 1024:1536],
                               kk == 0, kk == 7) for kk in range(8)],
                             reads=[d_w, d_hT[t]], writes=[psd[b]])
                        k.op(act, lambda e: e.activation(out=vt[:, r_, :], in_=ps[:, b, :], func=AF.Gelu),
                             reads=[psd[b]], writes=[d_vt[r_]])

                    def v1a(t):
                        r_ = t % NV
                        for h in range(4):
                            k.op(dve, lambda e: e.bn_stats(out=bst[:, r_, h, :], in_=vt[:, r_, h * 128:(h + 1) * 128]),
                                 reads=[d_vt[r_]], writes=[d_bst[r_]])
                        for h in range(4):
                            k.op(dve, lambda e: e.bn_aggr(out=mv[:, r_, h, :], in_=bst[:, r_, h, :]),
                                 reads=[d_bst[r_]], writes=[d_mv[r_]])
                        if t % 4 == 3:
                            s0 = (t - 3) % NV
                            gd = d_mv[s0:s0 + 4]
                            k.op(act, lambda e: e.activation(out=mv[:, s0:s0 + 4, :, 1], in_=mv[:, s0:s0 + 4, :, 1],
                                                             func=AF.Sqrt, bias=float(EPS)),
                                 reads=gd, writes=gd)
                            k.op(dve, lambda e: e.reciprocal(out=mv[:, s0:s0 + 4, :, 1], in_=mv[:, s0:s0 + 4, :, 1]),
                                 reads=gd, writes=gd)
                            k.op(dve, lambda e: e.scalar_tensor_tensor(out=nmr[:, s0:s0 + 4, :], in0=mv[:, s0:s0 + 4, :, 0],
                                                                       scalar=-1.0, in1=mv[:, s0:s0 + 4, :, 1],
                                                                       op0=ALU.mult, op1=ALU.mult),
                                 reads=gd, writes=gd)

                    def v1b(t):
                        r_ = t % NV
                        n_ = t % 3
                        for h in range(4):
                            k.op(act, lambda e: e.activation(out=vt[:, r_, h * 128:(h + 1) * 128],
                                                             in_=vt[:, r_, h * 128:(h + 1) * 128], func=AF.Identity,
                                                             scale=mv[:, r_, h, 1:2], bias=nmr[:, r_, h:h + 1]),
                                 reads=[d_vt[r_], d_mv[r_]], writes=[d_vt[r_]])
                        k.op(pool, lambda e: e.tensor_tensor(out=vt[:, r_, :], in0=vt[:, r_, :], in1=vln[:, 0, :],
                                                             op=ALU.mult),
                             reads=[d_vt[r_], d_vln], writes=[d_vt[r_]])
                        k.op(pool, lambda e: e.tensor_tensor(out=vn[:, n_, :], in0=vt[:, r_, :], in1=vln[:, 1, :],
                                                             op=ALU.add),
                             reads=[d_vt[r_], d_vln], writes=[d_vn[n_]])

                    def v2(t):
                        n_ = t % 3
                        b2 = k.banks(1)
                        mms = []
                        for h in range(4):
                            mms.append((ps[:, b2, h * 128:(h + 1) * 128], vn[:, n_, h * 128:(h + 1) * 128], swT[:, h, :],
                                        True, False))
                            mms.append((ps[:, b2, h * 128:(h + 1) * 128], ones_row[:], sbrow[:, h * 128:(h + 1) * 128],
                                        False, True))
                        k.mm(mms, reads=[d_vn[n_], d_swT, d_sbrow, d_onesr], writes=[psd[b2]])
                        k.op(dve, lambda e: e.tensor_tensor(
                            out=bigA[:, 4:8, t * 128:(t + 1) * 128],
                            in0=ps[:, b2, :].rearrange("p (h q) -> p h q", h=4),
                            in1=uT[:, :, t * 128:(t + 1) * 128], op=ALU.mult),
                             reads=[psd[b2]] + d_uT, writes=[d_yTv[t]])

                    run_pipeline(NT, [(0, v0), (6, v2), (5, v1b), (1, v1a)])
                    k.barrier()
                if stop_after == "sgu1":
                    add_dump("yT", bigA[:], [128, 8, S], BF16, d_yTv)
                    return
                with ExitStack() as s1:
                    fab = sb(s1, nc, "fab", [128, 2, 256], BF16)
                    gcs = sb(s1, nc, "gcs", [128, 2, 32, 128], BF16)
                    cs128 = sb(s1, nc, "cs128", [128, 2, 128], BF16)
                    fw = sb(s1, nc, "fw", [128, 4, 128], BF16)
                    m12 = sb(s1, nc, "m12", [128, 4, 2, 128], BF16)
                    Q = sb(s1, nc, "Q", [128, 2, 32, 32, 4], BF16)
                    Q2 = sb(s1, nc, "Q2", [128, 2, 32, 128], BF16)
                    Y = sb(s1, nc, "Y", [128, 2, 32, 128], BF16)
                    d_fab = k.dmadep()
                    d_gcs = k.dmadep()
                    d_cs = k.dmadep()
                    d_fw = k.dmadep()
                    d_m12 = Dep()
                    d_Q2 = [[Dep() for _ in range(8)] for _ in range(2)]
                    k.dma(pool, [(fab[:], T["fab"])], writes=[d_fab], semdep=d_fab)
                    k.dma(pool, [(gcs[:, 0], T["gcs"][:, 0]), (gcs[:, 1], T["gcs"][:, 1])], writes=[d_gcs], semdep=d_gcs)
                    k.dma(pool, [(cs128[:], T["cs128"])], writes=[d_cs], semdep=d_cs)
                    k.dma(pool, [(fw[:], T["od_fourier_w"])], writes=[d_fw], semdep=d_fw)
                    sc = 1.0 / np.sqrt(4096.0 * 128.0)
                    for h in range(4):
                        b = k.banks(1)
                        k.mm([(ps[:, b, 0:128], cs128[:, 0, :], fw[:, h, :], True, True),
                              (ps[:, b, 128:256], cs128[:, 1, :], fw[:, h, :], True, True)],
                             reads=[d_cs, d_fw], writes=[psd[b]])
                        k.op(act, lambda e: e.activation(out=m12[:, h, 0, :], in_=ps[:, b, 0:128], func=AF.Copy,
                                                         scale=float(sc)),
                             reads=[psd[b]], writes=[d_m12])
                        k.op(act, lambda e: e.activation(out=m12[:, h, 1, :], in_=ps[:, b, 128:256], func=AF.Copy,
                                                         scale=float(-sc)),
                             reads=[psd[b]], writes=[d_m12])
                    def evac_on(which, out_ap, in_ap, reads, writes):
                        if which == 0:
                            k.op(act, lambda e: e.activation(out=out_ap, in_=in_ap, func=AF.Copy), reads=reads,
                                 writes=writes)
                        else:
                            k.op(dve, lambda e: e.tensor_copy(out=out_ap, in_=in_ap), reads=reads, writes=writes)

                    d_Q = [[Dep() for _ in range(8)] for _ in range(2)]
                    d_Y = [[Dep() for _ in range(8)] for _ in range(2)]
                    for h in range(4):
                        for bb in range(0, 32, 4):
                            for r in range(2):
                                b = k.banks(1)
                                k.mm([(ps[:, b, i * 128:(i + 1) * 128], cT[:, h, (bb + i) * 128:(bb + i + 1) * 128],
                                       m12[:, h, r, :], True, True) for i in range(4)],
                                     reads=[d_cT[h], d_m12], writes=[psd[b]])
                                evac_on(r, Q[:, r, :, bb:bb + 4, :].rearrange("p c b j -> p b c j"),
                                        ps[:, b, :].rearrange("p (i c j) -> p i c j", i=4, j=4),
                                        [psd[b]], [d_Q[r][bb // 4]])
                        for ri in range(2):
                            for cg0 in range(0, 32, 4):
                                b = k.banks(1)
                                k.mm([(ps[:, b, i * 128:(i + 1) * 128],
                                       Q[:, ri, cg0 + i, :, :].rearrange("p b j -> p (b j)"), ident[:], True, True)
                                      for i in range(4)],
                                     reads=d_Q[ri] + [d_ident], writes=[psd[b]])
                                evac_on(ri, Q2[:, ri, cg0:cg0 + 4, :].rearrange("p c a -> p (c a)"), ps[:, b, :],
                                        [psd[b]], [d_Q2[ri][cg0 // 4]])
                        for cg0 in range(0, 32, 4):
                            for r in range(2):
                                b = k.banks(1)
                                mms = []
                                for i in range(4):
                                    mms.append((ps[:, b, i * 128:(i + 1) * 128], Q2[:, 0, cg0 + i, :],
                                                fab[:, 0, r * 128:(r + 1) * 128], True, False))
                                    mms.append((ps[:, b, i * 128:(i + 1) * 128], Q2[:, 1, cg0 + i, :],
                                                fab[:, 1, r * 128:(r + 1) * 128], False, True))
                                k.mm(mms, reads=[d_Q2[0][cg0 // 4], d_Q2[1][cg0 // 4], d_fab], writes=[psd[b]])
                                evac_on(r, Y[:, r, :, cg0 * 4:(cg0 + 4) * 4].rearrange("p b (i j) -> p i b j", i=4),
                                        ps[:, b, :].rearrange("p (i b j) -> p i b j", i=4, j=4),
                                        [psd[b]], [d_Y[r][cg0 // 4]])
                        yv = bigA[:, h, :].rearrange("p (a b) -> p b a", b=32)
                        for b0 in range(0, 32, 4):
                            b = k.banks(1)
                            mms = []
                            for i in range(4):
                                mms.append((ps[:, b, i * 128:(i + 1) * 128], Y[:, 0, b0 + i, :], gcs[:, 0, b0 + i, :], True, False))
                                mms.append((ps[:, b, i * 128:(i + 1) * 128], Y[:, 1, b0 + i, :], gcs[:, 1, b0 + i, :], False, True))
                            k.mm(mms, reads=d_Y[0] + d_Y[1] + [d_gcs], writes=[psd[b]])
                            evac_on(h % 2, yv[:, b0:b0 + 4, :], ps[:, b, :].rearrange("p (b a) -> p b a", b=4),
                                    [psd[b]], [d_yTf[h * 8 + b0 // 4]])
                    k.barrier()
                if stop_after == "four1":
                    add_dump("yT", bigA[:], [128, 8, S], BF16, d_yTv + d_yTf)
                    return
                with ExitStack() as s1:
                    wo = sb(s1, nc, "wo1", [128, 8, D], BF16)
                    d_wo = k.dmadep()
                    k.dma(pool, [(wo[:], T["od_w_out"].rearrange("(k p) n -> p k n", p=128))],
                          writes=[d_wo], semdep=d_wo)
                    out_phase(IO(s1), bigA, d_yTv + d_yTf, wo, d_wo, src, d_xres)
                    k.barrier()
                k.barrier()

        stages = ["mixer0", "ffn0", "mixer1", "ffn1"]
        m0_stops = ("norm0", "proj0", "pool0", "conv0")
        m1_stops = ("proj1", "sgu1", "four1")
        done = False
        mixer0()
        if stop_after in m0_stops:
            done = True
        if not done and stop_after == "mixer0":
            done = True
        if not done:
            ffn_phase(0, last=False)
            if stop_after == "ffn0":
                done = True
        if not done:
            mixer1()
            if stop_after in m1_stops or stop_after == "mixer1":
                done = True
        if not done:
            ffn_phase(1, last=True)
        if done and stop_after in ("mixer0", "ffn0", "mixer1"):
            k.barrier()
            dd = k.dmadep()
            k.dma(sp, [(out[:, :], xres[:, :])], semdep=dd)
        k.barrier()
    return nc


def _rep(v, n=128):
    return np.ascontiguousarray(np.broadcast_to(np.asarray(v, np.float32).reshape(1, -1), (n, v.size)))


def prep_inputs(inputs):
    f = lambda a: np.ascontiguousarray(np.asarray(a, dtype=np.float32))
    g = {}
    mg, fg, fin = f(inputs["mix_norm_g"]), f(inputs["ffn_norm_g"]), f(inputs["final_norm_g"])
    g["norm_g"] = np.stack([_rep(mg[0]), _rep(fg[0]), _rep(mg[1]), _rep(fg[1]), _rep(fin)], axis=0)
    g["ev_w_in"] = f(inputs["ev_w_in"])[0]
    g["ev_w_out"] = f(inputs["ev_w_out"])[0]
    g["od_w_in"] = f(inputs["od_w_in"])[0]
    g["od_w_out"] = f(inputs["od_w_out"])[0]
    g["ffn_w_gate"] = f(inputs["ffn_w_gate"])
    g["ffn_w_up"] = f(inputs["ffn_w_up"])
    g["ffn_w_down"] = f(inputs["ffn_w_down"])
    cw = f(inputs["ev_conv_w"])[0]
    g["ev_conv_w"] = np.ascontiguousarray(cw.reshape(31, 4, 128).transpose(2, 1, 0))
    vecs = np.stack([f(inputs["ev_conv_b"])[0], f(inputs["ev_ln_g"])[0], f(inputs["ev_ln_b"])[0],
                     f(inputs["ev_pool_scale"])[0].reshape(512)], axis=0)
    g["ev_vec"] = np.ascontiguousarray(vecs.reshape(4, 4, 128).transpose(2, 1, 0))
    g["ev_pool_w"] = np.ascontiguousarray(f(inputs["ev_pool_w"])[0].transpose(1, 0, 2))
    g["od_fourier_w"] = np.ascontiguousarray(f(inputs["od_fourier_w"])[0].transpose(1, 0, 2))
    g["od_vln"] = np.stack([_rep(f(inputs["od_v_ln_g"])[0].reshape(512)),
                            _rep(f(inputs["od_v_ln_b"])[0].reshape(512))], axis=1)
    g["od_spatial_wT"] = np.ascontiguousarray(f(inputs["od_spatial_w"])[0].transpose(2, 0, 1))
    g["od_spatial_b"] = _rep(f(inputs["od_spatial_b"])[0].reshape(512))
    g.update(make_consts())
    return g


_NC_CACHE = {}


def kernel(**inputs):
    x = np.asarray(inputs["x"], dtype=np.float32)
    shared = prep_inputs(inputs)
    if "nc" not in _NC_CACHE:
        _NC_CACHE["nc"] = build()
    nc = _NC_CACHE["nc"]
    in_maps = []
    for b in range(8):
        m = dict(shared)
        m["x"] = np.ascontiguousarray(x[b])
        in_maps.append(m)
    res = run_bass_kernel_spmd(nc, in_maps, core_ids=list(range(8)))
    return np.stack([np.asarray(r["out"], dtype=np.float32) for r in res.results], axis=0)
```

```python
import numpy as np
from contextlib import ExitStack
import concourse.bass as bass
import concourse.mybir as mybir
from concourse.bass_utils import run_bass_kernel_spmd

F32 = mybir.dt.float32
BF16 = mybir.dt.bfloat16
AF = mybir.ActivationFunctionType
ALU = mybir.AluOpType

S = 4096
D = 1024
DFF = 2816
NT = S // 128
EPS = 1e-6
NJ = DFF // 128
ST = 1024
NST = S // ST


class Eng:
    def __init__(self, es, nc, eng, name):
        self.e = eng
        self.name = name
        self.sem = es.enter_context(nc.semaphore("es_" + name))
        self.cnt = 0
        self.seen = {}

    def wait(self, ev):
        if ev is None:
            return
        sem, val = ev
        k = id(sem)
        if self.seen.get(k, 0) >= val:
            return
        self.e.wait_ge(sem, val)
        self.seen[k] = val

    def sig(self, ins):
        self.cnt += 1
        ins.then_inc(self.sem, 1)
        return (self.sem, self.cnt)


class Dep:
    def __init__(self, sem=None):
        self.w = None
        self.w2 = []
        self.r = {}
        self.sem = sem
        self.semval = 0


class K:
    def __init__(self, es, nc):
        self.nc = nc
        self.es = es
        self.pe = Eng(es, nc, nc.tensor, "pe")
        self.act = Eng(es, nc, nc.scalar, "act")
        self.dve = Eng(es, nc, nc.vector, "dve")
        self.pool = Eng(es, nc, nc.gpsimd, "pool")
        self.sp = Eng(es, nc, nc.sync, "sp")
        self.engs = [self.pe, self.act, self.dve, self.pool, self.sp]
        self.nsem = 0
        self.dma_deps = []
        self.ps = es.enter_context(nc.psum_tensor("ps", [128, 8, 512], F32))
        self.psd = [Dep() for _ in range(8)]
        self.bank_ptr = 0
        self.pool_ptr = {}

    def dmadep(self):
        self.nsem += 1
        d = Dep(self.es.enter_context(self.nc.semaphore("ds%d" % self.nsem)))
        self.dma_deps.append(d)
        return d

    def _pre(self, eng, reads, writes, addw=()):
        for d in reads:
            eng.wait(d.w)
            for ev in d.w2:
                eng.wait(ev)
        for d in writes:
            eng.wait(d.w)
            for ev in d.w2:
                eng.wait(ev)
            for ev in d.r.values():
                eng.wait(ev)
        for d in addw:
            for ev in d.r.values():
                eng.wait(ev)

    def _post(self, ev, reads, writes, addw=()):
        for d in reads:
            d.r[id(ev[0])] = ev
        for d in writes:
            d.w = ev
            d.w2 = []
            d.r = {}
        for d in addw:
            d.w2.append(ev)

    def op(self, eng, fn, reads=(), writes=(), addw=()):
        self._pre(eng, reads, writes, addw)
        ins = fn(eng.e)
        ev = eng.sig(ins)
        self._post(ev, reads, writes, addw)
        return ev

    def mm(self, mms, reads=(), writes=()):
        self._pre(self.pe, reads, writes)
        ins = None
        for (o, l, r, st, sp) in mms:
            ins = self.nc.tensor.matmul(o, lhsT=l, rhs=r, start=st, stop=sp)
        ev = self.pe.sig(ins)
        self._post(ev, reads, writes)
        return ev

    def dma(self, q, pairs, reads=(), writes=(), semdep=None):
        self._pre(q, reads, writes)
        for (o, i) in pairs:
            ins = q.e.dma_start(out=o, in_=i)
            semdep.semval += 16
            ins.then_inc(semdep.sem, 16)
        ev = (semdep.sem, semdep.semval)
        self._post(ev, reads, writes)
        return ev

    def banks(self, n=1, pool=None):
        if pool is not None:
            key = tuple(pool)
            i = self.pool_ptr.get(key, 0)
            self.pool_ptr[key] = i + 1
            return pool[i % len(pool)]
        if n == 2 and self.bank_ptr % 2:
            self.bank_ptr += 1
        b = self.bank_ptr % 8
        self.bank_ptr += n
        return b

    def barrier(self, skip=()):
        evs = []
        for e in self.engs:
            if e.cnt:
                evs.append((e.sem, e.cnt))
        for d in self.dma_deps:
            if d.semval and not any(d is x for x in skip):
                evs.append((d.sem, d.semval))
        for e in self.engs:
            for ev in evs:
                e.wait(ev)


_SB_UID = [0]


def sb(es, nc, name, shape, dt):
    _SB_UID[0] += 1
    return es.enter_context(nc.sbuf_tensor("sb%d_%s" % (_SB_UID[0], name), shape, dt))


def make_consts():
    c = {}
    c["ident"] = np.eye(128, dtype=np.float32)
    c["ones512"] = np.full((128, 128), 1.0 / 512.0, dtype=np.float32)
    wins = (2, 4, 8, 16)
    band = np.zeros((4, 5, 128, 128), dtype=np.float64)
    for g, w in enumerate(wins):
        half = w // 2
        for v, (tile, dt) in enumerate([(5, 0), (0, 0), (NT - 1, 0), (5, -1), (5, 1)]):
            for tl in range(128):
                t = tile * 128 + tl
                lo = min(max(t - half, 0), S - 1)
                hi = min(max(t + half - 1, 0), S - 1)
                cnt = hi - lo + 1
                for j in range(lo, hi + 1):
                    jt, jl = divmod(j, 128)
                    if jt == tile + dt:
                        band[g, v, jl, tl] += 1.0 / cnt
                if dt == 0:
                    band[g, v, tl, tl] -= 1.0
    c["band"] = np.ascontiguousarray(band.transpose(2, 0, 1, 3)).astype(np.float32)
    b = np.arange(32)
    ang = 2 * np.pi * np.outer(b, b) / 32.0
    Cr = np.cos(ang)
    Ci = -np.sin(ang)
    I4 = np.eye(4)
    FA = np.concatenate([np.kron(Cr, I4), np.kron(Ci, I4)], axis=1)
    FB = np.concatenate([np.kron(-Ci, I4), np.kron(Cr, I4)], axis=1)
    c["fab"] = np.stack([FA, FB], axis=1).astype(np.float32)
    a = np.arange(128)
    sp = (32 * a[None, :] + b[:, None])
    th = 2 * np.pi * (a[:, None, None] * sp[None, :, :]) / 4096.0
    c["gcs"] = np.stack([np.cos(th), np.sin(th)], axis=1).astype(np.float32)
    ch = np.arange(128)
    a2 = 2 * np.pi * np.outer(ch, ch) / 128.0
    c["cs128"] = np.stack([np.cos(a2), np.sin(a2)], axis=1).astype(np.float32)
    return c


CONST_SHAPES = {
    "ident": [128, 128], "ones512": [128, 128], "band": [128, 4, 5, 128], "fab": [128, 2, 256],
    "gcs": [128, 2, 32, 128], "cs128": [128, 2, 128],
}

IN_SHAPES = {
    "x": [S, D],
    "norm_g": [5, 128, D],
    "ev_w_in": [D, 1536], "ev_w_out": [D, D], "od_w_in": [D, 1536], "od_w_out": [D, D],
    "ffn_w_gate": [2, D, DFF], "ffn_w_up": [2, D, DFF], "ffn_w_down": [2, DFF, D],
    "ev_conv_w": [128, 4, 31],
    "ev_vec": [128, 4, 4],
    "ev_pool_w": [128, 4, 128],
    "od_fourier_w": [128, 4, 128],
    "od_vln": [128, 2, 512],
    "od_spatial_wT": [128, 4, 128],
    "od_spatial_b": [128, 512],
}
IN_SHAPES.update(CONST_SHAPES)


def build(stop_after=None, dumps=()):
    nc = bass.Bass("TRN2", target_bir_lowering=False)
    T = {}
    for name, shp in IN_SHAPES.items():
        T[name] = nc.dram_tensor(name, shp, F32, kind="ExternalInput").ap()
    out = nc.dram_tensor("out", [S, D], F32, kind="ExternalOutput").ap()
    xres = nc.dram_tensor("xres", [S, D], F32, kind="Internal").ap()
    dump_t = {}

    with ExitStack() as es:
        k = K(es, nc)
        pe, act, dve, pool, sp = k.pe, k.act, k.dve, k.pool, k.sp
        ps = k.ps
        psd = k.psd

        ident = sb(es, nc, "ident", [128, 128], BF16)
        ssb = sb(es, nc, "ssb", [128, 8], F32)
        tmp = sb(es, nc, "tmp", [128, 3, 512], F32)
        d_ident = k.dmadep()
        d_ss = [Dep() for _ in range(8)]
        d_tmp = [Dep() for _ in range(3)]
        d_xres = [Dep() for _ in range(NT)]
        cnt = {"xin": 0, "xout": 0, "hb": 0, "ss": 0, "tmp": 0}
        uid = [0]

        def rr(name, n):
            v = cnt[name] % n
            cnt[name] += 1
            return v

        k.dma(pool, [(ident[:], T["ident"])], writes=[d_ident], semdep=d_ident)

        io_sems = {"gbc": [k.dmadep(), k.dmadep()], "xin": [k.dmadep() for _ in range(6)],
                   "xout": [k.dmadep() for _ in range(2)]}

        class IO:
            def __init__(self, scope, nxin=6):
                self.nxin = nxin
                uid[0] += 1
                u = str(uid[0])
                self.gbc = sb(scope, nc, "gbc" + u, [128, 2, D], F32)
                self.xin = sb(scope, nc, "xin" + u, [128, nxin, D], F32)
                self.xout = sb(scope, nc, "xout" + u, [128, 2, D], F32)
                self.hb = sb(scope, nc, "hb" + u, [128, 3, D], BF16)
                self.junk = sb(scope, nc, "junk" + u, [128, D], BF16)
                self.d_gbc = io_sems["gbc"]
                self.d_xin = io_sems["xin"]
                self.d_xout = io_sems["xout"]
                self.d_hb = [Dep(), Dep(), Dep()]
                self.d_junk = Dep()

        def add_dump(name, ap_sb, shape, dt, deps):
            if name not in dumps:
                return
            t = nc.dram_tensor("dbg_" + name, shape, dt, kind="ExternalOutput").ap()
            dd = k.dmadep()
            k.dma(sp, [(t, ap_sb)], reads=deps, semdep=dd)
            dump_t[name] = dd

        def run_pipeline(n, stages):
            mx = max(sk for sk, _ in stages)
            for i in range(n + mx):
                for sk, fn in stages:
                    t = i - sk
                    if 0 <= t < n:
                        fn(t)

        def norm_stages(io, src_fn, src_deps, tiles, gidx, hT, hT_deps, col0=0, bpool=None):
            gbc, xin, hb, junk = io.gbc, io.xin, io.hb, io.junk
            d_gbc, d_xin, d_hb, d_junk = io.d_gbc, io.d_xin, io.d_hb, io.d_junk
            k.dma(sp, [(gbc[:, 0, :], T["norm_g"][gidx])], writes=[d_gbc[0]], semdep=d_gbc[0])
            stt = {}

            def s_load(i):
                t = tiles[i]
                xs = rr("xin", io.nxin)
                stt[i] = {"xs": xs}
                k.dma(sp, [(xin[:, xs, :], src_fn(t))], reads=[src_deps[t]] if src_deps else [],
                      writes=[d_xin[xs]], semdep=d_xin[xs])

            def s_stat(i):
                xs = stt[i]["xs"]
                s_ = rr("ss", 8)
                k.op(act, lambda e: e.activation(out=junk[:], in_=xin[:, xs, :], func=AF.Square,
                                                 accum_out=ssb[:, s_:s_ + 1]),
                     reads=[d_xin[xs]], writes=[d_junk, d_ss[s_]])
                k.op(act, lambda e: e.activation(out=ssb[:, s_:s_ + 1], in_=ssb[:, s_:s_ + 1], func=AF.Sqrt,
                                                 bias=float(D * EPS)),
                     reads=[d_ss[s_]], writes=[d_ss[s_]])
                k.op(dve, lambda e: e.reciprocal(out=ssb[:, s_:s_ + 1], in_=ssb[:, s_:s_ + 1]),
                     reads=[d_ss[s_]], writes=[d_ss[s_]])
                h_ = rr("hb", 3)
                stt[i]["h"] = h_
                k.op(dve, lambda e: e.scalar_tensor_tensor(out=hb[:, h_, :], in0=xin[:, xs, :],
                                                           scalar=ssb[:, s_:s_ + 1], in1=gbc[:, 0, :],
                                                           op0=ALU.mult, op1=ALU.mult),
                     reads=[d_xin[xs], d_ss[s_], d_gbc[0]], writes=[d_hb[h_]])

            def s_tr(i):
                h_ = stt[i]["h"]
                b = k.banks(2, pool=bpool)
                psv = ps[:, b:b + 2, :].rearrange("p b (c n) -> p (b c) n", n=128)
                k.mm([(psv[:, kk, :], hb[:, h_, kk * 128:(kk + 1) * 128], ident[:], True, True)
                      for kk in range(8)],
                     reads=[d_hb[h_], d_ident], writes=[psd[b], psd[b + 1]])
                c0 = col0 + i * 128
                k.op(act, lambda e: e.activation(out=hT[:, 0:4, c0:c0 + 128], in_=psv[:, 0:4, :], func=AF.Copy,
                                                 scale=float(np.sqrt(D))),
                     reads=[psd[b]], writes=[hT_deps[i]])
                k.op(dve, lambda e: e.tensor_scalar(out=hT[:, 4:8, c0:c0 + 128], in0=psv[:, 4:8, :],
                                                    scalar1=float(np.sqrt(D)), scalar2=None, op0=ALU.mult),
                     reads=[psd[b + 1]], addw=[hT_deps[i]])

            return [(0, s_load), (4, s_tr), (2, s_stat)]

        def norm_phase(io, src_fn, src_deps, tiles, gidx, hT, hT_deps, col0=0):
            run_pipeline(len(tiles), norm_stages(io, src_fn, src_deps, tiles, gidx, hT, hT_deps, col0))

        def proj_fm(w_sb, w_dep, col_chunks, hT, hT_deps_all, ntg, epilogue):
            for tg in range(ntg):
                for ci, cols in enumerate(col_chunks):
                    bl = []
                    for c0 in cols:
                        b = k.banks(1)
                        k.mm([(ps[:, b, :], w_sb[:, kk, c0:c0 + 128], hT[:, kk, tg * 512:(tg + 1) * 512],
                               kk == 0, kk == 7) for kk in range(8)],
                             reads=[w_dep] + hT_deps_all(tg), writes=[psd[b]])
                        bl.append(b)
                    epilogue(tg, ci, bl)

        def out_phase(io, yT, yT_dep, wo, wo_dep, src_fn, src_deps):
            xin, xout = io.xin, io.xout
            d_xin, d_xout = io.d_xin, io.d_xout
            stt = {}

            def s_load(t):
                xs = rr("xin", io.nxin)
                stt[t] = {"xs": xs}
                k.dma(sp, [(xin[:, xs, :], src_fn(t))], reads=[src_deps[t]] if src_deps else [],
                      writes=[d_xin[xs]], semdep=d_xin[xs])

            def s_mm(t):
                b = k.banks(2)
                stt[t]["b"] = b
                mms = []
                for nh in range(2):
                    for kk in range(8):
                        mms.append((ps[:, b + nh, :], yT[:, kk, t * 128:(t + 1) * 128],
                                    wo[:, kk, nh * 512:(nh + 1) * 512], kk == 0, kk == 7))
                k.mm(mms, reads=list(yT_dep) + [wo_dep], writes=[psd[b], psd[b + 1]])

            def s_add(t):
                xs, b = stt[t]["xs"], stt[t]["b"]
                xo = rr("xout", 2)
                k.op(dve, lambda e: e.tensor_tensor(out=xout[:, xo, :], in0=xin[:, xs, :],
                                                    in1=ps[:, b:b + 2, :].rearrange("p b n -> p (b n)"),
                                                    op=ALU.add),
                     reads=[d_xin[xs], psd[b], psd[b + 1]], writes=[d_xout[xo]])
                k.dma(pool, [(xres[t * 128:(t + 1) * 128, :], xout[:, xo, :])], reads=[d_xout[xo]],
                      writes=[d_xres[t]], semdep=d_xout[xo])

            run_pipeline(NT, [(0, s_load), (1, s_mm), (2, s_add)])

        def ffn_phase(l, last):
            with ExitStack() as fs:
                io = IO(fs, nxin=6)
                gbc, xin, xout, junk = io.gbc, io.xin, io.xout, io.junk
                d_gbc, d_xin, d_xout, d_junk = io.d_gbc, io.d_xin, io.d_xout, io.d_junk
                if last:
                    k.dma(sp, [(gbc[:, 1, :], T["norm_g"][4])], writes=[d_gbc[1]], semdep=d_gbc[1])
                    k.op(dve, lambda e: e.tensor_scalar(out=gbc[:, 1, :], in0=gbc[:, 1, :], scalar1=float(np.sqrt(D)),
                                                        scalar2=None, op0=ALU.mult),
                         reads=[d_gbc[1]], writes=[d_gbc[1]])
                h2T = sb(fs, nc, "h2T%d" % l, [128, 2, 8, ST], BF16)
                gT = sb(fs, nc, "gT%d" % l, [128, NJ, ST], BF16)
                wd = sb(fs, nc, "wd%d" % l, [128, NJ, D], BF16)
                wgu = sb(fs, nc, "wgu%d" % l, [128, 3, 2, 8, 256], BF16)
                d_h2T = [[Dep() for _ in range(8)] for _ in range(2)]
                d_gT = [Dep() for _ in range(NJ)]
                d_wd = k.dmadep()
                d_wgu = [k.dmadep() for _ in range(3)]
                wdv = T["ffn_w_down"][l].rearrange("(j p) n -> p j n", p=128)
                wgv = T["ffn_w_gate"][l].rearrange("(k p) n -> p k n", p=128)
                wuv = T["ffn_w_up"][l].rearrange("(k p) n -> p k n", p=128)
                nslot = [0]
                xsrc = lambda t: xres[t * 128:(t + 1) * 128, :]

                def nstages(st, bpool=None):
                    tiles = list(range(st * 8, st * 8 + 8))
                    return norm_stages(io, xsrc, d_xres, tiles, 1 + 2 * l, h2T[:, st % 2], d_h2T[st % 2], bpool=bpool)

                def gu(st):
                    hT_ = h2T[:, st % 2]
                    dh = d_h2T[st % 2]
                    for c in range(NJ // 2):
                        sl = nslot[0] % 3
                        nslot[0] += 1
                        k.dma(pool, [(wgu[:, sl, 0, :, :], wgv[:, :, c * 256:(c + 1) * 256]),
                                     (wgu[:, sl, 1, :, :], wuv[:, :, c * 256:(c + 1) * 256])],
                              writes=[d_wgu[sl]], semdep=d_wgu[sl])
                        if st == 0 and c == 2:
                            k.dma(pool, [(wd[:, 0:11, :], wdv[:, 0:11, :]), (wd[:, 11:22, :], wdv[:, 11:22, :])],
                                  writes=[d_wd], semdep=d_wd)
                        for jj in range(2):
                            j = c * 2 + jj
                            for tg in range(ST // 512):
                                bg = k.banks(1)
                                k.mm([(ps[:, bg, :], wgu[:, sl, 0, kk, jj * 128:(jj + 1) * 128],
                                       hT_[:, kk, tg * 512:(tg + 1) * 512], kk == 0, kk == 7) for kk in range(8)],
                                     reads=[d_wgu[sl]] + dh[tg * 4:(tg + 1) * 4], writes=[psd[bg]])
                                bu = k.banks(1)
                                k.mm([(ps[:, bu, :], wgu[:, sl, 1, kk, jj * 128:(jj + 1) * 128],
                                       hT_[:, kk, tg * 512:(tg + 1) * 512], kk == 0, kk == 7) for kk in range(8)],
                                     reads=[d_wgu[sl]] + dh[tg * 4:(tg + 1) * 4], writes=[psd[bu]])
                                ts_ = rr("tmp", 3)
                                k.op(act, lambda e: e.activation(out=tmp[:, ts_, :], in_=ps[:, bg, :], func=AF.Silu),
                                     reads=[psd[bg]], writes=[d_tmp[ts_]])
                                k.op(dve, lambda e: e.tensor_tensor(out=gT[:, j, tg * 512:(tg + 1) * 512],
                                                                    in0=tmp[:, ts_, :], in1=ps[:, bu, :], op=ALU.mult),
                                     reads=[d_tmp[ts_], psd[bu]], writes=[d_gT[j]])

                def down_stages(st):
                    stt = {}

                    def s_load(ti):
                        t = st * 8 + ti
                        xs = rr("xin", io.nxin)
                        stt[ti] = {"xs": xs}
                        k.dma(sp, [(xin[:, xs, :], xres[t * 128:(t + 1) * 128, :])], reads=[d_xres[t]],
                              writes=[d_xin[xs]], semdep=d_xin[xs])

                    def s_mm(ti):
                        b = k.banks(2, pool=[0, 2, 4])
                        stt[ti]["b"] = b
                        mms = []
                        for nh in range(2):
                            for j in range(NJ):
                                mms.append((ps[:, b + nh, :], gT[:, j, ti * 128:(ti + 1) * 128],
                                            wd[:, j, nh * 512:(nh + 1) * 512], j == 0, j == NJ - 1))
                        k.mm(mms, reads=[d_wd] + d_gT, writes=[psd[b], psd[b + 1]])

                    def s_add(ti):
                        t = st * 8 + ti
                        xs, b = stt[ti]["xs"], stt[ti]["b"]
                        xo = rr("xout", 2)
                        k.op(dve, lambda e: e.tensor_tensor(out=xout[:, xo, :], in0=xin[:, xs, :],
                                                            in1=ps[:, b:b + 2, :].rearrange("p b n -> p (b n)"),
                                                            op=ALU.add),
                             reads=[d_xin[xs], psd[b], psd[b + 1]], writes=[d_xout[xo]])
                        if not last:
                            k.dma(sp, [(xres[t * 128:(t + 1) * 128, :], xout[:, xo, :])], reads=[d_xout[xo]],
                                  writes=[d_xres[t]], semdep=d_xout[xo])
                        else:
                            s_ = rr("ss", 8)
                            k.op(act, lambda e: e.activation(out=junk[:], in_=xout[:, xo, :], func=AF.Square,
                                                             accum_out=ssb[:, s_:s_ + 1]),
                                 reads=[d_xout[xo]], writes=[d_junk, d_ss[s_]])
                            k.op(act, lambda e: e.activation(out=ssb[:, s_:s_ + 1], in_=ssb[:, s_:s_ + 1],
                                                             func=AF.Sqrt, bias=float(D * EPS)),
                                 reads=[d_ss[s_]], writes=[d_ss[s_]])
                            k.op(dve, lambda e: e.reciprocal(out=ssb[:, s_:s_ + 1], in_=ssb[:, s_:s_ + 1]),
                                 reads=[d_ss[s_]], writes=[d_ss[s_]])
                            k.op(dve, lambda e: e.scalar_tensor_tensor(out=xout[:, xo, :], in0=xout[:, xo, :],
                                                                       scalar=ssb[:, s_:s_ + 1], in1=gbc[:, 1, :],
                                                                       op0=ALU.mult, op1=ALU.mult),
                                 reads=[d_xout[xo], d_ss[s_], d_gbc[1]], writes=[d_xout[xo]])
                            k.dma(sp, [(out[t * 128:(t + 1) * 128, :], xout[:, xo, :])], reads=[d_xout[xo]],
                                  semdep=d_xout[xo])

                    return [(0, s_load), (1, s_mm), (2, s_add)]

                run_pipeline(8, nstages(0))
                for st in range(NST):
                    gu(st)
                    stages = down_stages(st)
                    if st + 1 < NST:
                        ns = nstages(st + 1, bpool=[6])
                        stages = [stages[0], ns[0], stages[2], stages[1], ns[1], ns[2]]
                    run_pipeline(8, stages)
                k.barrier()

        def mixer0():
            with ExitStack() as ms:
                bigA = sb(ms, nc, "bigA", [128, 8, S], BF16)
                uT = sb(ms, nc, "uT", [128, 4, S + 32], BF16)
                d_hT = [Dep() for _ in range(NT)]
                d_uT = [Dep() for _ in range(4)]
                d_yT = Dep()
                evec = sb(ms, nc, "evec", [128, 4, 4], F32)
                d_evec = k.dmadep()
                k.dma(sp, [(evec[:], T["ev_vec"])], writes=[d_evec], semdep=d_evec)
                w_in = sb(ms, nc, "w_in", [128, 8, 1536], BF16)
                d_w = k.dmadep()
                wv = T["ev_w_in"].rearrange("(k p) n -> p k n", p=128)
                k.dma(pool, [(w_in[:, :, 0:768], wv[:, :, 0:768]), (w_in[:, :, 768:1536], wv[:, :, 768:1536])],
                      writes=[d_w], semdep=d_w)
                cw = sb(ms, nc, "cw", [128, 4, 31], F32)
                identf = sb(ms, nc, "identf", [128, 128], F32)
                diag = sb(ms, nc, "diag", [128, 4, 31, 128], BF16)
                d_cw = k.dmadep()
                d_idf = k.dmadep()
                d_diag = Dep()
                k.dma(sp, [(cw[:], T["ev_conv_w"])], writes=[d_cw], semdep=d_cw)
                k.dma(sp, [(identf[:], T["ident"])], writes=[d_idf], semdep=d_idf)
                for c in range(4):
                    k.op(pool, lambda e: e.memset(uT[:, c, 0:16], 0.0), writes=[d_uT[c]])
                    k.op(pool, lambda e: e.memset(uT[:, c, S + 16:S + 32], 0.0), writes=[d_uT[c]])
                src = lambda t: T["x"][t * 128:(t + 1) * 128, :]
                with ExitStack() as sn:
                    norm_phase(IO(sn, nxin=4), src, None, list(range(NT)), 0, bigA, d_hT)
                    k.barrier()
                if stop_after == "norm0":
                    add_dump("hT", bigA[:], [128, 8, S], BF16, d_hT)
                    return
                with ExitStack() as s1:
                    p_sb = sb(s1, nc, "p_sb", [128, NT, 512], BF16)
                    d_p = [Dep() for _ in range(NT)]
                    with ExitStack() as s2:

                        def epi_a(tg, c, bl):
                            ts_ = rr("tmp", 3)
                            k.op(act, lambda e: e.activation(out=tmp[:, ts_, :], in_=ps[:, bl[1], :], func=AF.Sigmoid),
                                 reads=[psd[bl[1]]], writes=[d_tmp[ts_]])
                            k.op(dve, lambda e: e.tensor_tensor(out=uT[:, c, 16 + tg * 512:16 + (tg + 1) * 512],
                                                                in0=tmp[:, ts_, :], in1=ps[:, bl[0], :], op=ALU.mult),
                                 reads=[d_tmp[ts_], psd[bl[0]]], writes=[d_uT[c]])

                        proj_fm(w_in, d_w, [[c * 128, 512 + c * 128] for c in range(4)], bigA,
                                lambda tg: d_hT[tg * 4:(tg + 1) * 4], 8, epi_a)
                        for c in range(4):
                            for tap in range(31):
                                k.op(dve, lambda e: e.tensor_scalar(out=diag[:, c, tap, :], in0=identf[:],
                                                                    scalar1=cw[:, c, tap:tap + 1], scalar2=None,
                                                                    op0=ALU.mult),
                                     reads=[d_cw, d_idf], writes=[d_diag])
                        for t in range(NT):
                            b = k.banks(1)
                            k.mm([(ps[:, b, :], bigA[:, kk, t * 128:(t + 1) * 128], w_in[:, kk, 1024:1536],
                                   kk == 0, kk == 7) for kk in range(8)],
                                 reads=[d_w, d_hT[t]], writes=[psd[b]])
                            eng = act if t % 2 == 0 else dve
                            if eng is act:
                                k.op(act, lambda e: e.activation(out=p_sb[:, t, :], in_=ps[:, b, :], func=AF.Copy),
                                     reads=[psd[b]], writes=[d_p[t]])
                            else:
                                k.op(dve, lambda e: e.tensor_copy(out=p_sb[:, t, :], in_=ps[:, b, :]),
                                     reads=[psd[b]], writes=[d_p[t]])
                        wo = w_in[:, :, 0:D]
                        d_wo = d_w
                        k.dma(pool, [(wo, T["ev_w_out"].rearrange("(k p) n -> p k n", p=128))],
                              writes=[d_w], semdep=d_w)
                        k.barrier(skip=[d_w])
                    if stop_after == "proj0":
                        add_dump("uT", uT[:], [128, 4, S + 32], BF16, d_uT)
                        add_dump("p_sb", p_sb[:], [128, NT, 512], BF16, d_p)
                        return
                    with ExitStack() as s2:
                        band = sb(s2, nc, "band", [128, 4, 5, 128], BF16)
                        pw = sb(s2, nc, "pw", [128, 4, 128], BF16)
                        pooled = sb(s2, nc, "pooled", [128, 3, 512], BF16)
                        d_band = k.dmadep()
                        d_pw = k.dmadep()
                        d_pooled = [Dep(), Dep(), Dep()]
                        k.dma(pool, [(band[:], T["band"])], writes=[d_band], semdep=d_band)
                        k.dma(pool, [(pw[:], T["ev_pool_w"])], writes=[d_pw], semdep=d_pw)
                        pooled_n = 3
                        pst = {}

                        def pl_a(n):
                            tg, g = divmod(n, 4)
                            b = k.banks(1)
                            mms = []
                            rd = {}
                            for ti in range(4):
                                t = tg * 4 + ti
                                srcs = []
                                if t > 0:
                                    srcs.append((t - 1, 3))
                                srcs.append((t, 1 if t == 0 else (2 if t == NT - 1 else 0)))
                                if t < NT - 1:
                                    srcs.append((t + 1, 4))
                                for si, (tt, v) in enumerate(srcs):
                                    mms.append((ps[:, b, ti * 128:(ti + 1) * 128],
                                                p_sb[:, tt, g * 128:(g + 1) * 128], band[:, g, v, :],
                                                si == 0, si == len(srcs) - 1))
                                    rd[tt] = d_p[tt]
                            k.mm(mms, reads=[d_band] + list(rd.values()), writes=[psd[b]])
                            pl = n % pooled_n
                            pst[n] = pl
                            k.op(act, lambda e: e.activation(out=pooled[:, pl, :], in_=ps[:, b, :], func=AF.Copy),
                                 reads=[psd[b]], writes=[d_pooled[pl]])

                        def pl_b(n):
                            tg, g = divmod(n, 4)
                            pl = pst[n]
                            b2 = k.banks(1)
                            k.mm([(ps[:, b2, :], pw[:, g, :], pooled[:, pl, :], True, True)],
                                 reads=[d_pw, d_pooled[pl]], writes=[psd[b2]])
                            k.op(dve, lambda e: e.tensor_scalar(out=bigA[:, 4 + g, tg * 512:(tg + 1) * 512],
                                                                in0=ps[:, b2, :], scalar1=evec[:, g, 3:4],
                                                                scalar2=None, op0=ALU.mult),
                                 reads=[psd[b2], d_evec], writes=[d_yT])

                        run_pipeline(32, [(0, pl_a), (2, pl_b)])
                        k.barrier()
                if stop_after == "pool0":
                    add_dump("yT", bigA[:], [128, 8, S], BF16, [d_yT])
                    return
                with ExitStack() as s1:
                    ones = sb(s1, nc, "ones", [128, 128], BF16)
                    vf = sb(s1, nc, "vf", [128, 2, 4, 512], F32)
                    vb = sb(s1, nc, "vb", [128, 2, 4, 512], BF16)
                    vq = sb(s1, nc, "vq", [128, 2, 4, 512], BF16)
                    st_sb = sb(s1, nc, "st_sb", [128, 2, 2, 512], F32)
                    t1 = sb(s1, nc, "t1", [128, 3, 512], F32)
                    d_ones = k.dmadep()
                    d_vf = [[Dep() for _ in range(4)] for _ in range(2)]
                    d_vb = [[Dep() for _ in range(4)] for _ in range(2)]
                    d_vq = [[Dep() for _ in range(4)] for _ in range(2)]
                    d_st = [[Dep() for _ in range(3)] for _ in range(2)]
                    d_t1 = [Dep() for _ in range(3)]
                    k.dma(pool, [(ones[:], T["ones512"])], writes=[d_ones], semdep=d_ones)
                    nt1 = 0
                    for tg in range(8):
                        r_ = tg % 2
                        for c in range(4):
                            b = k.banks(1)
                            k.mm([(ps[:, b, :], diag[:, c, tap, :],
                                   uT[:, c, 1 + tg * 512 + tap:1 + tg * 512 + tap + 512], tap == 0, tap == 30)
                                  for tap in range(31)],
                                 reads=[d_diag, d_uT[c]], writes=[psd[b]])
                            k.op(act, lambda e: e.activation(out=vf[:, r_, c, :], in_=ps[:, b, :], func=AF.Identity,
                                                             bias=evec[:, c, 0:1]),
                                 reads=[psd[b], d_evec], writes=[d_vf[r_][c]])
                            k.op(act, lambda e: e.activation(out=vq[:, r_, c, :], in_=ps[:, b, :], func=AF.Square,
                                                             bias=evec[:, c, 0:1]),
                                 reads=[psd[b], d_evec], writes=[d_vq[r_][c]])
                            k.op(pool, lambda e: e.tensor_copy(out=vb[:, r_, c, :], in_=vf[:, r_, c, :]),
                                 reads=[d_vf[r_][c]], writes=[d_vb[r_][c]])
                        bm = k.banks(1)
                        k.mm([(ps[:, bm, :], ones[:], vb[:, r_, c, :], c == 0, c == 3) for c in range(4)],
                             reads=[d_ones] + d_vb[r_], writes=[psd[bm]])
                        bq = k.banks(1)
                        k.mm([(ps[:, bq, :], ones[:], vq[:, r_, c, :], c == 0, c == 3) for c in range(4)],
                             reads=[d_ones] + d_vq[r_], writes=[psd[bq]])
                        k.op(act, lambda e: e.activation(out=st_sb[:, r_, 0, :], in_=ps[:, bm, :], func=AF.Copy),
                             reads=[psd[bm]], writes=[d_st[r_][0]])
                        k.op(act, lambda e: e.activation(out=st_sb[:, r_, 1, :], in_=ps[:, bm, :], func=AF.Square),
                             reads=[psd[bm]], writes=[d_st[r_][1]])
                        k.op(dve, lambda e: e.tensor_tensor(out=st_sb[:, r_, 1, :], in0=ps[:, bq, :],
                                                            in1=st_sb[:, r_, 1, :], op=ALU.subtract),
                             reads=[psd[bq], d_st[r_][1]], writes=[d_st[r_][1]])
                        k.op(act, lambda e: e.activation(out=st_sb[:, r_, 1, :], in_=st_sb[:, r_, 1, :], func=AF.Sqrt,
                                                         bias=float(EPS)),
                             reads=[d_st[r_][1]], writes=[d_st[r_][1]])
                        k.op(dve, lambda e: e.reciprocal(out=st_sb[:, r_, 1, :], in_=st_sb[:, r_, 1, :]),
                             reads=[d_st[r_][1]], writes=[d_st[r_][1]])
                        for c in range(4):
                            ti_ = nt1 % 3
                            nt1 += 1
                            k.op(pool, lambda e: e.tensor_tensor(out=t1[:, ti_, :], in0=vf[:, r_, c, :],
                                                                 in1=st_sb[:, r_, 0, :], op=ALU.subtract),
                                 reads=[d_vf[r_][c], d_st[r_][0]], writes=[d_t1[ti_]])
                            k.op(dve, lambda e: e.tensor_tensor(out=t1[:, ti_, :], in0=t1[:, ti_, :],
                                                                in1=st_sb[:, r_, 1, :], op=ALU.mult),
                                 reads=[d_t1[ti_], d_st[r_][1]], writes=[d_t1[ti_]])
                            k.op(act, lambda e: e.activation(out=bigA[:, c, tg * 512:(tg + 1) * 512], in_=t1[:, ti_, :],
                                                             func=AF.Silu, scale=evec[:, c, 1:2], bias=evec[:, c, 2:3]),
                                 reads=[d_t1[ti_], d_evec, d_yT], writes=[d_yT])
                    k.barrier()
                if stop_after == "conv0":
                    add_dump("yT", bigA[:], [128, 8, S], BF16, [d_yT])
                    return
                with ExitStack() as s1:
                    out_phase(IO(s1, nxin=4), bigA, [d_yT], wo, d_wo, src, None)
                    k.barrier()
                k.barrier()

        def mixer1():
            with ExitStack() as ms:
                bigA = sb(ms, nc, "bigA1", [128, 8, S], BF16)
                cT = sb(ms, nc, "cT", [128, 4, S], BF16)
                d_hT = [Dep() for _ in range(NT)]
                d_cT = [Dep() for _ in range(4)]
                d_yTv = [Dep() for _ in range(NT)]
                d_yTf = [Dep() for _ in range(32)]
                src = lambda t: xres[t * 128:(t + 1) * 128, :]
                w_in = sb(ms, nc, "w_in1", [128, 8, 1536], BF16)
                d_w = k.dmadep()
                wv = T["od_w_in"].rearrange("(k p) n -> p k n", p=128)
                k.dma(pool, [(w_in[:, :, 0:768], wv[:, :, 0:768]), (w_in[:, :, 768:1536], wv[:, :, 768:1536])],
                      writes=[d_w], semdep=d_w)
                with ExitStack() as sn:
                    norm_phase(IO(sn), src, d_xres, list(range(NT)), 2, bigA, d_hT)
                    k.barrier()
                with ExitStack() as s1:
                    uT = sb(s1, nc, "uT1", [128, 4, S], BF16)
                    d_uT = [Dep() for _ in range(4)]
                    vln = sb(s1, nc, "vln", [128, 2, 512], F32)
                    sbb = sb(s1, nc, "sbb", [128, 512], F32)
                    swT = sb(s1, nc, "swT", [128, 4, 128], BF16)
                    d_vln = k.dmadep()
                    d_sbb = k.dmadep()
                    d_swT = k.dmadep()
                    k.dma(sp, [(vln[:], T["od_vln"])], writes=[d_vln], semdep=d_vln)
                    k.dma(sp, [(sbb[:], T["od_spatial_b"])], writes=[d_sbb], semdep=d_sbb)
                    k.dma(pool, [(swT[:], T["od_spatial_wT"])], writes=[d_swT], semdep=d_swT)

                    nev = [0]

                    def epi_c(tg, c, bl):
                        nev[0] += 1
                        if nev[0] % 2:
                            k.op(act, lambda e: e.activation(out=cT[:, c, tg * 512:(tg + 1) * 512], in_=ps[:, bl[0], :],
                                                             func=AF.Copy),
                                 reads=[psd[bl[0]]], writes=[d_cT[c]])
                        else:
                            k.op(dve, lambda e: e.tensor_copy(out=cT[:, c, tg * 512:(tg + 1) * 512], in_=ps[:, bl[0], :]),
                                 reads=[psd[bl[0]]], writes=[d_cT[c]])

                    proj_fm(w_in, d_w, [[c * 128] for c in range(4)], bigA, lambda tg: d_hT[tg * 4:(tg + 1) * 4], 8, epi_c)

                    def epi_u(tg, c, bl):
                        k.op(act, lambda e: e.activation(out=uT[:, c, tg * 512:(tg + 1) * 512], in_=ps[:, bl[0], :],
                                                         func=AF.Gelu),
                             reads=[psd[bl[0]]], writes=[d_uT[c]])

                    proj_fm(w_in, d_w, [[512 + c * 128] for c in range(4)], bigA, lambda tg: d_hT[tg * 4:(tg + 1) * 4],
                            8, epi_u)
                    if stop_after == "proj1":
                        add_dump("cT", cT[:], [128, 4, S], BF16, d_cT)
                        add_dump("uT", uT[:], [128, 4, S], BF16, d_uT)
                        return
                    NV = 8
                    vt = sb(s1, nc, "vt", [128, NV, 512], F32)
                    vn = sb(s1, nc, "vn", [128, 3, 512], BF16)
                    bst = sb(s1, nc, "bst", [128, NV, 4, 6], F32)
                    mv = sb(s1, nc, "mv", [128, NV, 4, 2], F32)
                    nmr = sb(s1, nc, "nmr", [128, NV, 4], F32)
                    sbrow = sb(s1, nc, "sbrow", [1, 512], BF16)
                    ones_row = sb(s1, nc, "ones_row", [1, 128], BF16)
                    d_sbrow = k.dmadep()
                    d_onesr = Dep()
                    k.dma(pool, [(sbrow[:], T["od_spatial_b"][0:1, :])], writes=[d_sbrow], semdep=d_sbrow)
                    k.op(dve, lambda e: e.memset(ones_row[:], 1.0), writes=[d_onesr])
                    d_vt = [Dep() for _ in range(NV)]
                    d_vn = [Dep() for _ in range(3)]
                    d_bst = [Dep() for _ in range(NV)]
                    d_mv = [Dep() for _ in range(NV)]
                    vst = {}

                    def v0(t):
                        r_ = t % NV
                        b = k.banks(1)
                        k.mm([(ps[:, b, :], bigA[:, kk, t * 128:(t + 1) * 128], w_in[:, kk, 1024:1536],
                               kk == 0, kk == 7) for kk in range(8)],
                             reads=[d_w, d_hT[t]], writes=[psd[b]])
                        k.op(act, lambda e: e.activation(out=vt[:, r_, :], in_=ps[:, b, :], func=AF.Gelu),
                             reads=[psd[b]], writes=[d_vt[r_]])

                    def v1a(t):
                        r_ = t % NV
                        for h in range(4):
                            k.op(dve, lambda e: e.bn_stats(out=bst[:, r_, h, :], in_=vt[:, r_, h * 128:(h + 1) * 128]),
                                 reads=[d_vt[r_]], writes=[d_bst[r_]])
                        for h in range(4):
                            k.op(dve, lambda e: e.bn_aggr(out=mv[:, r_, h, :], in_=bst[:, r_, h, :]),
                                 reads=[d_bst[r_]], writes=[d_mv[r_]])
                        if t % 4 == 3:
                            s0 = (t - 3) % NV
                            gd = d_mv[s0:s0 + 4]
                            k.op(act, lambda e: e.activation(out=mv[:, s0:s0 + 4, :, 1], in_=mv[:, s0:s0 + 4, :, 1],
                                                             func=AF.Sqrt, bias=float(EPS)),
                                 reads=gd, writes=gd)
                            k.op(dve, lambda e: e.reciprocal(out=mv[:, s0:s0 + 4, :, 1], in_=mv[:, s0:s0 + 4, :, 1]),
                                 reads=gd, writes=gd)
                            k.op(dve, lambda e: e.scalar_tensor_tensor(out=nmr[:, s0:s0 + 4, :], in0=mv[:, s0:s0 + 4, :, 0],
                                                                       scalar=-1.0, in1=mv[:, s0:s0 + 4, :, 1],
                                                                       op0=ALU.mult, op1=ALU.mult),
                                 reads=gd, writes=gd)

                    def v1b(t):
                        r_ = t % NV
                        n_ = t % 3
                        for h in range(4):
                            k.op(act, lambda e: e.activation(out=vt[:, r_, h * 128:(h + 1) * 128],
                                                             in_=vt[:, r_, h * 128:(h + 1) * 128], func=AF.Identity,
                                                             scale=mv[:, r_, h, 1:2], bias=nmr[:, r_, h:h + 1]),
                                 reads=[d_vt[r_], d_mv[r_]], writes=[d_vt[r_]])
                        k.op(pool, lambda e: e.tensor_tensor(out=vt[:, r_, :], in0=vt[:, r_, :], in1=vln[:, 0, :],
                                                             op=ALU.mult),
                             reads=[d_vt[r_], d_vln], writes=[d_vt[r_]])
                        k.op(pool, lambda e: e.tensor_tensor(out=vn[:, n_, :], in0=vt[:, r_, :], in1=vln[:, 1, :],
                                                             op=ALU.add),
                             reads=[d_vt[r_], d_vln], writes=[d_vn[n_]])

                    def v2(t):
                        n_ = t % 3
                        b2 = k.banks(1)
                        mms = []
                        for h in range(4):
                            mms.append((ps[:, b2, h * 128:(h + 1) * 128], vn[:, n_, h * 128:(h + 1) * 128], swT[:, h, :],
                                        True, False))
                            mms.append((ps[:, b2, h * 128:(h + 1) * 128], ones_row[:], sbrow[:, h * 128:(h + 1) * 128],
                                        False, True))
                        k.mm(mms, reads=[d_vn[n_], d_swT, d_sbrow, d_onesr], writes=[psd[b2]])
                        k.op(dve, lambda e: e.tensor_tensor(
                            out=bigA[:, 4:8, t * 128:(t + 1) * 128],
                            in0=ps[:, b2, :].rearrange("p (h q) -> p h q", h=4),
                            in1=uT[:, :, t * 128:(t + 1) * 128], op=ALU.mult),
                             reads=[psd[b2]] + d_uT, writes=[d_yTv[t]])

                    run_pipeline(NT, [(0, v0), (6, v2), (5, v1b), (1, v1a)])
                    wo = w_in[:, :, 0:D]
                    d_wo = d_w
                    k.dma(pool, [(wo, T["od_w_out"].rearrange("(k p) n -> p k n", p=128))],
                          writes=[d_w], semdep=d_w)
                    k.barrier(skip=[d_w])
                if stop_after == "sgu1":
                    add_dump("yT", bigA[:], [128, 8, S], BF16, d_yTv)
                    return
                with ExitStack() as s1:
                    fab = sb(s1, nc, "fab", [128, 2, 256], BF16)
                    gcs = sb(s1, nc, "gcs", [128, 2, 32, 128], BF16)
                    cs128 = sb(s1, nc, "cs128", [128, 2, 128], BF16)
                    fw = sb(s1, nc, "fw", [128, 4, 128], BF16)
                    m12 = sb(s1, nc, "m12", [128, 4, 2, 128], BF16)
                    Q = sb(s1, nc, "Q", [128, 2, 32, 32, 4], BF16)
                    Q2 = sb(s1, nc, "Q2", [128, 2, 32, 128], BF16)
                    Y = sb(s1, nc, "Y", [128, 2, 32, 128], BF16)
                    d_fab = k.dmadep()
                    d_gcs = k.dmadep()
                    d_cs = k.dmadep()
                    d_fw = k.dmadep()
                    d_m12 = Dep()
                    d_Q2 = [[Dep() for _ in range(8)] for _ in range(2)]
                    k.dma(pool, [(fab[:], T["fab"])], writes=[d_fab], semdep=d_fab)
                    k.dma(pool, [(gcs[:, 0], T["gcs"][:, 0]), (gcs[:, 1], T["gcs"][:, 1])], writes=[d_gcs], semdep=d_gcs)
                    k.dma(pool, [(cs128[:], T["cs128"])], writes=[d_cs], semdep=d_cs)
                    k.dma(pool, [(fw[:], T["od_fourier_w"])], writes=[d_fw], semdep=d_fw)
                    sc = 1.0 / np.sqrt(4096.0 * 128.0)
                    for h in range(4):
                        b = k.banks(1)
                        k.mm([(ps[:, b, 0:128], cs128[:, 0, :], fw[:, h, :], True, True),
                              (ps[:, b, 128:256], cs128[:, 1, :], fw[:, h, :], True, True)],
                             reads=[d_cs, d_fw], writes=[psd[b]])
                        k.op(act, lambda e: e.activation(out=m12[:, h, 0, :], in_=ps[:, b, 0:128], func=AF.Copy,
                                                         scale=float(sc)),
                             reads=[psd[b]], writes=[d_m12])
                        k.op(act, lambda e: e.activation(out=m12[:, h, 1, :], in_=ps[:, b, 128:256], func=AF.Copy,
                                                         scale=float(-sc)),
                             reads=[psd[b]], writes=[d_m12])
                    def evac_on(which, out_ap, in_ap, reads, writes):
                        if which == 0:
                            k.op(act, lambda e: e.activation(out=out_ap, in_=in_ap, func=AF.Copy), reads=reads,
                                 writes=writes)
                        else:
                            k.op(dve, lambda e: e.tensor_copy(out=out_ap, in_=in_ap), reads=reads, writes=writes)

                    d_Q = [[Dep() for _ in range(8)] for _ in range(2)]
                    d_Y = [[Dep() for _ in range(8)] for _ in range(2)]
                    for h in range(4):
                        for bb in range(0, 32, 4):
                            for r in range(2):
                                b = k.banks(1)
                                k.mm([(ps[:, b, i * 128:(i + 1) * 128], cT[:, h, (bb + i) * 128:(bb + i + 1) * 128],
                                       m12[:, h, r, :], True, True) for i in range(4)],
                                     reads=[d_cT[h], d_m12], writes=[psd[b]])
                                evac_on(r, Q[:, r, :, bb:bb + 4, :].rearrange("p c b j -> p b c j"),
                                        ps[:, b, :].rearrange("p (i c j) -> p i c j", i=4, j=4),
                                        [psd[b]], [d_Q[r][bb // 4]])
                        for ri in range(2):
                            for cg0 in range(0, 32, 4):
                                b = k.banks(1)
                                k.mm([(ps[:, b, i * 128:(i + 1) * 128],
                                       Q[:, ri, cg0 + i, :, :].rearrange("p b j -> p (b j)"), ident[:], True, True)
                                      for i in range(4)],
                                     reads=d_Q[ri] + [d_ident], writes=[psd[b]])
                                evac_on(ri, Q2[:, ri, cg0:cg0 + 4, :].rearrange("p c a -> p (c a)"), ps[:, b, :],
                                        [psd[b]], [d_Q2[ri][cg0 // 4]])
                        for cg0 in range(0, 32, 4):
                            for r in range(2):
                                b = k.banks(1)
                                mms = []
                                for i in range(4):
                                    mms.append((ps[:, b, i * 128:(i + 1) * 128], Q2[:, 0, cg0 + i, :],
                                                fab[:, 0, r * 128:(r + 1) * 128], True, False))
                                    mms.append((ps[:, b, i * 128:(i + 1) * 128], Q2[:, 1, cg0 + i, :],
                                                fab[:, 1, r * 128:(r + 1) * 128], False, True))
                                k.mm(mms, reads=[d_Q2[0][cg0 // 4], d_Q2[1][cg0 // 4], d_fab], writes=[psd[b]])
                                evac_on(r, Y[:, r, :, cg0 * 4:(cg0 + 4) * 4].rearrange("p b (i j) -> p i b j", i=4),
                                        ps[:, b, :].rearrange("p (i b j) -> p i b j", i=4, j=4),
                                        [psd[b]], [d_Y[r][cg0 // 4]])
                        yv = bigA[:, h, :].rearrange("p (a b) -> p b a", b=32)
                        for b0 in range(0, 32, 4):
                            b = k.banks(1)
                            mms = []
                            for i in range(4):
                                mms.append((ps[:, b, i * 128:(i + 1) * 128], Y[:, 0, b0 + i, :], gcs[:, 0, b0 + i, :], True, False))
                                mms.append((ps[:, b, i * 128:(i + 1) * 128], Y[:, 1, b0 + i, :], gcs[:, 1, b0 + i, :], False, True))
                            k.mm(mms, reads=d_Y[0] + d_Y[1] + [d_gcs], writes=[psd[b]])
                            evac_on(h % 2, yv[:, b0:b0 + 4, :], ps[:, b, :].rearrange("p (b a) -> p b a", b=4),
                                    [psd[b]], [d_yTf[h * 8 + b0 // 4]])
                    k.barrier()
                if stop_after == "four1":
                    add_dump("yT", bigA[:], [128, 8, S], BF16, d_yTv + d_yTf)
                    return
                with ExitStack() as s1:
                    out_phase(IO(s1), bigA, d_yTv + d_yTf, wo, d_wo, src, d_xres)
                    k.barrier()
                k.barrier()

        stages = ["mixer0", "ffn0", "mixer1", "ffn1"]
        m0_stops = ("norm0", "proj0", "pool0", "conv0")
        m1_stops = ("proj1", "sgu1", "four1")
        done = False
        mixer0()
        if stop_after in m0_stops:
            done = True
        if not done and stop_after == "mixer0":
            done = True
        if not done:
            ffn_phase(0, last=False)
            if stop_after == "ffn0":
                done = True
        if not done:
            mixer1()
            if stop_after in m1_stops or stop_after == "mixer1":
                done = True
        if not done:
            ffn_phase(1, last=True)
        if done and stop_after in ("mixer0", "ffn0", "mixer1"):
            k.barrier()
            dd = k.dmadep()
            k.dma(sp, [(out[:, :], xres[:, :])], semdep=dd)
        k.barrier()
    return nc


def _rep(v, n=128):
    return np.ascontiguousarray(np.broadcast_to(np.asarray(v, np.float32).reshape(1, -1), (n, v.size)))


def prep_inputs(inputs):
    f = lambda a: np.ascontiguousarray(np.asarray(a, dtype=np.float32))
    g = {}
    mg, fg, fin = f(inputs["mix_norm_g"]), f(inputs["ffn_norm_g"]), f(inputs["final_norm_g"])
    g["norm_g"] = np.stack([_rep(mg[0]), _rep(fg[0]), _rep(mg[1]), _rep(fg[1]), _rep(fin)], axis=0)
    g["ev_w_in"] = f(inputs["ev_w_in"])[0]
    g["ev_w_out"] = f(inputs["ev_w_out"])[0]
    g["od_w_in"] = f(inputs["od_w_in"])[0]
    g["od_w_out"] = f(inputs["od_w_out"])[0]
    g["ffn_w_gate"] = f(inputs["ffn_w_gate"])
    g["ffn_w_up"] = f(inputs["ffn_w_up"])
    g["ffn_w_down"] = f(inputs["ffn_w_down"])
    cw = f(inputs["ev_conv_w"])[0]
    g["ev_conv_w"] = np.ascontiguousarray(cw.reshape(31, 4, 128).transpose(2, 1, 0))
    vecs = np.stack([f(inputs["ev_conv_b"])[0], f(inputs["ev_ln_g"])[0], f(inputs["ev_ln_b"])[0],
                     f(inputs["ev_pool_scale"])[0].reshape(512)], axis=0)
    g["ev_vec"] = np.ascontiguousarray(vecs.reshape(4, 4, 128).transpose(2, 1, 0))
    g["ev_pool_w"] = np.ascontiguousarray(f(inputs["ev_pool_w"])[0].transpose(1, 0, 2))
    g["od_fourier_w"] = np.ascontiguousarray(f(inputs["od_fourier_w"])[0].transpose(1, 0, 2))
    g["od_vln"] = np.stack([_rep(f(inputs["od_v_ln_g"])[0].reshape(512)),
                            _rep(f(inputs["od_v_ln_b"])[0].reshape(512))], axis=1)
    g["od_spatial_wT"] = np.ascontiguousarray(f(inputs["od_spatial_w"])[0].transpose(2, 0, 1))
    g["od_spatial_b"] = _rep(f(inputs["od_spatial_b"])[0].reshape(512))
    g.update(make_consts())
    return g


_NC_CACHE = {}


def kernel(**inputs):
    x = np.asarray(inputs["x"], dtype=np.float32)
    shared = prep_inputs(inputs)
    if "nc" not in _NC_CACHE:
        _NC_CACHE["nc"] = build()
    nc = _NC_CACHE["nc"]
    in_maps = []
    for b in range(8):
        m = dict(shared)
        m["x"] = np.ascontiguousarray(x[b])
        in_maps.append(m)
    res = run_bass_kernel_spmd(nc, in_maps, core_ids=list(range(8)))
    return np.stack([np.asarray(r["out"], dtype=np.float32) for r in res.results], axis=0)
```
